# Optimizing a Trainium2 kernel written in Bass

```python
import math
import jax, jax.numpy as jnp
from jax import lax
import numpy as np

D_MODEL = 1024
BATCH = 4
SEQ = 4096
DEPTH = 1

HEAD_DIM_A = 64
N_HEADS_A = 8
WIDTH_A = N_HEADS_A * HEAD_DIM_A
DILATED_BRANCHES = ((128, 1), (512, 4), (2048, 16))

N_HEADS_B = 8
QK_NOPE_DIM = 64
QK_ROPE_DIM = 32
V_HEAD_DIM = 64
QK_HEAD_DIM_B = QK_NOPE_DIM + QK_ROPE_DIM
Q_LORA_RANK = 768
KV_LORA_RANK = 256
WIDTH_B = N_HEADS_B * V_HEAD_DIM
ROPE_THETA = 10000.0

MIX_WIDTH = WIDTH_A + WIDTH_B
D_FF = 4 * D_MODEL

REL_BUCKETS = 32
REL_MAX_DIST = 2048
Q_BLOCK = 128
EPS = 1e-6

IN_SPLITS = (WIDTH_A, WIDTH_A, WIDTH_A, Q_LORA_RANK, KV_LORA_RANK, QK_ROPE_DIM)
IN_WIDTH = sum(IN_SPLITS)

kernel_name = "hybrid_dilated_mla_sqrelu_layer"


def rms_norm(x, g):
    xf = x.astype(jnp.float32)
    y = xf * lax.rsqrt(jnp.mean(xf * xf, axis=-1, keepdims=True) + EPS)
    return (y * g.astype(jnp.float32)).astype(x.dtype)


def t5_causal_bucket(dist):
    dist = np.asarray(dist, dtype=np.int64)
    max_exact = REL_BUCKETS // 2
    safe = np.maximum(dist, 1).astype(np.float32)
    large = max_exact + (np.log(safe / max_exact) / math.log(REL_MAX_DIST / max_exact)
                         * (REL_BUCKETS - max_exact)).astype(np.int64)
    large = np.minimum(large, REL_BUCKETS - 1)
    return np.where(dist < max_exact, dist, large).astype(np.int32)


def apply_rope(x, positions):
    r = x.shape[-1]
    inv_freq = 1.0 / (ROPE_THETA ** (jnp.arange(0, r, 2, dtype=jnp.float32) / r))
    ang = positions.astype(jnp.float32)[..., None] * inv_freq
    cos = jnp.cos(ang)[:, :, None, :]
    sin = jnp.sin(ang)[:, :, None, :]
    xf = x.astype(jnp.float32)
    x1, x2 = xf[..., : r // 2], xf[..., r // 2:]
    out = jnp.concatenate([x1 * cos - x2 * sin, x2 * cos + x1 * sin], axis=-1)
    return out.astype(x.dtype)


def to_query_blocks(t):
    b, s = t.shape[:2]
    nblk = s // Q_BLOCK
    return jnp.moveaxis(t.reshape((b, nblk, Q_BLOCK) + t.shape[2:]), 1, 0)


def from_query_blocks(t):
    t = jnp.moveaxis(t, 0, 1)
    return t.reshape((t.shape[0], t.shape[1] * t.shape[2]) + t.shape[3:])


def dilated_attention(q, k, v, rel_bias):
    b, s, h, dh = q.shape
    scale = 1.0 / math.sqrt(dh)
    branches = []
    for window, dil in DILATED_BRANCHES:
        offs_np = dil * np.arange(window // dil + 1, dtype=np.int32)
        bias = rel_bias[jnp.asarray(t5_causal_bucket(offs_np))].T.astype(jnp.float32)
        branches.append((jnp.asarray(offs_np), bias))
    nblk = s // Q_BLOCK

    def block_fn(args):
        i, qi = args
        tq = i * Q_BLOCK + jnp.arange(Q_BLOCK, dtype=jnp.int32)
        ms, ss, os_ = [], [], []
        for offs, bias in branches:
            idx = tq[:, None] - offs[None, :]
            valid = idx >= 0
            idxc = jnp.maximum(idx, 0)
            kg = k[:, idxc]
            vg = v[:, idxc]
            logits = jnp.einsum('bqhd,bqkhd->bhqk', qi, kg).astype(jnp.float32) * scale
            logits = logits + bias[None, :, None, :]
            logits = jnp.where(valid[None, None], logits, -jnp.inf)
            m = jnp.max(logits, axis=-1, keepdims=True)
            p = jnp.exp(logits - m)
            den = jnp.sum(p, axis=-1, keepdims=True)
            o = jnp.einsum('bhqk,bqkhd->bhqd', p, vg.astype(jnp.float32)) / den
            ms.append(m); ss.append(den); os_.append(o)
        m_all = jnp.stack(ms)
        w = jnp.stack(ss) * jnp.exp(m_all - jnp.max(m_all, axis=0, keepdims=True))
        out = jnp.sum(w * jnp.stack(os_), axis=0) / jnp.sum(w, axis=0)
        return jnp.transpose(out, (0, 2, 1, 3)).astype(q.dtype)

    out = lax.map(block_fn, (jnp.arange(nblk, dtype=jnp.int32), to_query_blocks(q)))
    return from_query_blocks(out)


def causal_block_attention(q, k, v):
    b, s, h, dqk = q.shape
    scale = 1.0 / math.sqrt(dqk)
    kpos = jnp.arange(s, dtype=jnp.int32)
    nblk = s // Q_BLOCK

    def block_fn(args):
        i, qi = args
        tq = i * Q_BLOCK + jnp.arange(Q_BLOCK, dtype=jnp.int32)
        logits = jnp.einsum('bqhd,bkhd->bhqk', qi, k).astype(jnp.float32) * scale
        logits = jnp.where((kpos[None, :] <= tq[:, None])[None, None], logits, -jnp.inf)
        p = jax.nn.softmax(logits, axis=-1)
        out = jnp.einsum('bhqk,bkhd->bqhd', p, v.astype(jnp.float32))
        return out.astype(q.dtype)

    out = lax.map(block_fn, (jnp.arange(nblk, dtype=jnp.int32), to_query_blocks(q)))
    return from_query_blocks(out)


def setup_inputs(seed: int = 0) -> dict:
    key = jax.random.key(seed)
    ks = jax.random.split(key, 20)
    f32 = jnp.float32

    def nrm(k, shape, fan_in):
        return jax.random.normal(k, shape, f32) * (fan_in ** -0.5)

    def gain(k, n):
        return 1.0 + 0.05 * jax.random.normal(k, (n,), f32)

    x = jax.random.normal(ks[0], (BATCH, SEQ, D_MODEL), f32)
    offset = jax.random.randint(ks[1], (BATCH, 1), 0, 1024, dtype=jnp.int32)
    positions = jnp.arange(SEQ, dtype=jnp.int32)[None, :] + offset
    return {
        "x": x,
        "positions": positions,
        "norm_mix_g": gain(ks[2], D_MODEL),
        "w_in": nrm(ks[3], (D_MODEL, IN_WIDTH), D_MODEL),
        "qnorm_a_g": gain(ks[4], HEAD_DIM_A),
        "knorm_a_g": gain(ks[5], HEAD_DIM_A),
        "rel_bias": 0.5 * jax.random.normal(ks[6], (REL_BUCKETS, N_HEADS_A), f32),
        "cq_norm_g": gain(ks[7], Q_LORA_RANK),
        "ckv_norm_g": gain(ks[8], KV_LORA_RANK),
        "w_uq": nrm(ks[9], (Q_LORA_RANK, N_HEADS_B * QK_HEAD_DIM_B), Q_LORA_RANK),
        "w_ukv": nrm(ks[10], (KV_LORA_RANK, N_HEADS_B * (QK_NOPE_DIM + V_HEAD_DIM)), KV_LORA_RANK),
        "qnorm_b_g": gain(ks[11], QK_HEAD_DIM_B),
        "knorm_b_g": gain(ks[12], QK_HEAD_DIM_B),
        "w_o": nrm(ks[13], (MIX_WIDTH, D_MODEL), MIX_WIDTH),
        "norm_ffn_g": gain(ks[14], D_MODEL),
        "w_ff1": nrm(ks[15], (D_MODEL, D_FF), D_MODEL),
        "w_ff2": nrm(ks[16], (D_FF, D_MODEL), D_FF),
    }


def reference(x, positions, norm_mix_g, w_in, qnorm_a_g, knorm_a_g, rel_bias,
              cq_norm_g, ckv_norm_g, w_uq, w_ukv, qnorm_b_g, knorm_b_g, w_o,
              norm_ffn_g, w_ff1, w_ff2):
    b, s, _ = x.shape
    h = x
    for _layer in range(DEPTH):
        xn = rms_norm(h, norm_mix_g)
        proj = jnp.einsum('bsd,de->bse', xn, w_in)
        cuts = list(np.cumsum(IN_SPLITS)[:-1])
        qa, ka, va, c_q, c_kv, k_rope = jnp.split(proj, cuts, axis=-1)

        qa = rms_norm(qa.reshape(b, s, N_HEADS_A, HEAD_DIM_A), qnorm_a_g)
        ka = rms_norm(ka.reshape(b, s, N_HEADS_A, HEAD_DIM_A), knorm_a_g)
        va = va.reshape(b, s, N_HEADS_A, HEAD_DIM_A)
        out_a = dilated_attention(qa, ka, va, rel_bias).reshape(b, s, WIDTH_A)

        qb = jnp.einsum('bsr,re->bse', rms_norm(c_q, cq_norm_g), w_uq)
        qb = qb.reshape(b, s, N_HEADS_B, QK_HEAD_DIM_B)
        kv = jnp.einsum('bsr,re->bse', rms_norm(c_kv, ckv_norm_g), w_ukv)
        kv = kv.reshape(b, s, N_HEADS_B, QK_NOPE_DIM + V_HEAD_DIM)
        k_nope, vb = kv[..., :QK_NOPE_DIM], kv[..., QK_NOPE_DIM:]
        k_rope_h = jnp.broadcast_to(k_rope[:, :, None, :], (b, s, N_HEADS_B, QK_ROPE_DIM))
        kb = jnp.concatenate([k_nope, k_rope_h], axis=-1)
        qb = rms_norm(qb, qnorm_b_g)
        kb = rms_norm(kb, knorm_b_g)
        qb = jnp.concatenate([qb[..., :QK_NOPE_DIM],
                              apply_rope(qb[..., QK_NOPE_DIM:], positions)], axis=-1)
        kb = jnp.concatenate([kb[..., :QK_NOPE_DIM],
                              apply_rope(kb[..., QK_NOPE_DIM:], positions)], axis=-1)
        out_b = causal_block_attention(qb, kb, vb).reshape(b, s, WIDTH_B)

        mixed = jnp.concatenate([out_a, out_b], axis=-1)
        h = h + jnp.einsum('bse,ed->bsd', mixed, w_o)

        hn = rms_norm(h, norm_ffn_g)
        hid = jnp.square(jax.nn.relu(jnp.einsum('bsd,df->bsf', hn, w_ff1)))
        h = h + jnp.einsum('bsf,fd->bsd', hid, w_ff2)
    return h
```

```python
import os
import math
import numpy as np
import concourse.bass as bass
import concourse.mybir as mybir
from concourse.bass_utils import run_bass_kernel_spmd

F32 = mybir.dt.float32
BF16 = mybir.dt.bfloat16
I32 = mybir.dt.int32
AF = mybir.ActivationFunctionType
ALU = mybir.AluOpType

S = 4096
D = 1024
NOWN = 2048
EPS = 1e-6
LU = 2432
NTT = 2304
NTTP = 2432
ENG = ("pe", "act", "dve", "pool", "sp")
N_DSEM = 76
N_DSEM_SW = 32
ARENA_BYTES = 212800

PI = math.pi
C1 = 6.28125
C2 = 2.0 * math.pi - 6.28125


class Buf:
    __slots__ = ("name", "w", "r", "excl")

    def __init__(self, name, excl=False):
        self.name = name
        self.w = None
        self.r = {}
        self.excl = excl


class Rec:
    def __init__(self, esem, dsem):
        self.esem = esem
        self.dsem = dsem
        self.ops = {e: [] for e in ENG}
        self.cnt = {e: 0 for e in ENG}
        self.seen = {e: {} for e in ENG}
        self.dval = [0] * len(dsem)
        self.drr = {}

    def _sem(self, key):
        return self.esem[key[1]] if key[0] == "e" else self.dsem[key[1]]

    def _wait(self, eng, ev):
        if ev is None:
            return
        key, val = ev
        if key == ("e", "pe") and eng == "pe":
            return
        if self.seen[eng].get(key, 0) >= val:
            return
        self.seen[eng][key] = val
        sem = self._sem(key)
        self.ops[eng].append(lambda e, sem=sem, val=val: e.wait_ge(sem, val))

    @staticmethod
    def _split(reads, writes):
        ex = [b for b in reads if b.excl]
        if ex:
            reads = [b for b in reads if not b.excl]
            writes = list(writes) + [b for b in ex if b not in writes]
        return reads, writes

    @staticmethod
    def _deps(reads, writes):
        deps = []
        for b in reads:
            if b.w is not None:
                deps.append(b.w)
        for b in writes:
            if b.w is not None:
                deps.append(b.w)
            deps.extend(b.r.values())
        return deps

    @staticmethod
    def _commit(ev, reads, writes):
        for b in reads:
            b.r[ev[0]] = ev
        for b in writes:
            b.w = ev
            b.r = {}

    def op(self, eng, fn, reads=(), writes=()):
        reads, writes = self._split(reads, writes)
        for ev in self._deps(reads, writes):
            self._wait(eng, ev)
        self.cnt[eng] += 1
        ev = (("e", eng), self.cnt[eng])
        sem = self.esem[eng]
        self.ops[eng].append(lambda e, fn=fn, sem=sem: fn(e).then_inc(sem, 1))
        self._commit(ev, reads, writes)
        return ev

    def mm(self, fns, reads=(), writes=()):
        reads, writes = self._split(reads, writes)
        for ev in self._deps(reads, writes):
            self._wait("pe", ev)
        self.cnt["pe"] += 1
        ev = (("e", "pe"), self.cnt["pe"])
        sem = self.esem["pe"]
        for f in fns[:-1]:
            self.ops["pe"].append(lambda e, f=f: f(e))
        self.ops["pe"].append(lambda e, f=fns[-1], sem=sem: f(e).then_inc(sem, 1))
        self._commit(ev, reads, writes)
        return ev

    def dma(self, eng, out, in_, reads=(), writes=()):
        lo, hi = (0, N_DSEM_SW) if eng == "pool" else (N_DSEM_SW, len(self.dsem))
        i = self.drr.get(eng, lo)
        self.drr[eng] = lo + (i + 1 - lo) % (hi - lo)
        if self.dval[i] > 0:
            self._wait(eng, (("d", i), self.dval[i]))
        for ev in self._deps(reads, writes):
            self._wait(eng, ev)
        self.dval[i] += 16
        ev = (("d", i), self.dval[i])
        sem = self.dsem[i]
        self.ops[eng].append(
            lambda e, out=out, in_=in_, sem=sem: e.dma_start(out=out, in_=in_).then_inc(sem, 16))
        self._commit(ev, reads, writes)
        return ev

    def barrier(self):
        for e in ENG:
            for f in ENG:
                if f != e and self.cnt[f] > 0:
                    self._wait(e, (("e", f), self.cnt[f]))
            for i, v in enumerate(self.dval):
                if v > 0:
                    self._wait(e, (("d", i), v))

    def finish(self):
        for i, v in enumerate(self.dval):
            if v > 0:
                self._wait("sp", (("d", i), v))
        for f in ENG:
            if f != "sp" and self.cnt[f] > 0:
                self._wait("sp", (("e", f), self.cnt[f]))


class Arena:
    def __init__(self, ap, cap):
        self.ap = ap
        self.cap = cap
        self.top = 0
        self.peak = 0

    def alloc(self, nbytes):
        off = (self.top + 63) // 64 * 64
        self.top = off + nbytes
        self.peak = max(self.peak, self.top)
        assert self.top <= self.cap, f"arena overflow {self.top} > {self.cap}"
        return off

    def b16(self, n, parts=128):
        off = self.alloc(2 * n)
        return self.ap[0:parts, off // 2: off // 2 + n]

    def f32(self, n, parts=128, dt=F32):
        off = self.alloc(4 * n)
        return self.ap[0:parts, off // 2: off // 2 + 2 * n].bitcast(dt)


def r3(ap, b):
    return ap.rearrange("p (a b) -> p a b", b=b)


def build_program(dump=None):
    dump = dump or ()
    nc = bass.Bass("TRN2", target_bir_lowering=False)

    def din(name, shape, dt=F32):
        return nc.dram_tensor(name, list(shape), dt, kind="ExternalInput").ap()

    xf = din("xf", [S, D])
    xo = din("xo", [NOWN, D])
    posf = din("posf", [1, S], I32)
    poso = din("poso", [1, NOWN], I32)
    w_in = din("w_in", [D, 2592])
    w_uq = din("w_uq", [768, 768])
    w_ukv = din("w_ukv", [256, 1024])
    w_o = din("w_o", [D, D])
    w_ff1 = din("w_ff1", [D, 4096])
    w_ff2 = din("w_ff2", [4096, D])
    gcols_d = din("gcols", [128, 32])
    relb_d = din("rel_bias", [32, 8])
    ident_d = din("ident", [128, 128])
    onesblk_d = din("onesblk", [128, 128])
    oh_d = din("onehot", [32, LU])
    mm1_d = din("mask_m1", [128, 128])
    m0_d = din("mask_0", [128, 128])
    postm_d = din("pos_tm", [128, 48], I32)
    invf_d = din("invf_row", [128, 16])
    out_d = nc.dram_tensor("out", [NOWN, D], F32, kind="ExternalOutput").ap()
    h1s = nc.dram_tensor("h1s", [NOWN, D], F32, kind="Internal").ap()
    uscr = nc.dram_tensor("uscr", [8, 128, LU], BF16, kind="Internal").ap()
    xnTs = nc.dram_tensor("xnTs", [12, 128, 8 * 512], BF16, kind="Internal").ap()
    cqs = nc.dram_tensor("cqs", [4, 128, 6 * 512], BF16, kind="Internal").ap()
    ckvs = nc.dram_tensor("ckvs", [8, 128, 2 * 512], BF16, kind="Internal").ap()
    krgs = nc.dram_tensor("krgs", [8, 64, 512], F32, kind="Internal").ap()
    sqks = nc.dram_tensor("sqks", [8, 64, 512], BF16, kind="Internal").ap()
    trqs = nc.dram_tensor("trqs", [4, 2, 48, 512], F32, kind="Internal").ap()
    dump_out = {}

    import contextlib
    with contextlib.ExitStack() as es:
        arena_t = es.enter_context(nc.sbuf_tensor("arena", [128, ARENA_BYTES // 2], BF16))
        pairT = [es.enter_context(nc.psum_tensor(f"pp{i}", [128, 1024], F32)) for i in range(2)]
        psb = [es.enter_context(nc.psum_tensor(f"ps{i}", [128, 512], F32)) for i in range(4, 8)]
        esem = {e: es.enter_context(nc.semaphore(f"sem_{e}")) for e in ENG}
        dsem = [es.enter_context(nc.semaphore(f"dsem{i}")) for i in range(N_DSEM)]
        R = Rec(esem, dsem)
        A = Arena(arena_t, ARENA_BYTES)
        pairs = [p[:] for p in pairT]
        ps = [pairs[0][:, 0:512], pairs[0][:, 512:1024], pairs[1][:, 0:512], pairs[1][:, 512:1024]] + [p[:] for p in psb]
        SPB = [Buf("sp0", excl=True), Buf("sp1", excl=True)]
        PSB = [SPB[0], SPB[0], SPB[1], SPB[1]] + [Buf(f"ps{i}", excl=True) for i in range(4, 8)]

        def dump_ap(name, ap, reads, shape, dt=F32):
            if name not in dump:
                return
            t = nc.dram_tensor("dbg_" + name, list(shape), dt, kind="ExternalOutput").ap()
            dump_out[name] = t
            R.dma("sp", t, ap, reads=reads)

        ident = A.b16(128)
        ones_bf = A.b16(128)
        onesblk = A.b16(128)
        ones32 = A.f32(128)
        gcols = A.f32(32)
        mm1 = A.b16(128)
        m0 = A.b16(128)
        small = A.f32(16)
        invf_row = A.f32(16)
        CONST = Buf("const")
        G_MIX, G_FFN, G_CQ, G_CKV, G_QA, G_KA, G_QB, G_KB, INVF = 0, 8, 16, 22, 24, 25, 26, 27, 28

        R.dma("pool", ident, ident_d, writes=[CONST])
        R.dma("pool", onesblk, onesblk_d, writes=[CONST])
        R.dma("pool", mm1, mm1_d, writes=[CONST])
        R.dma("pool", m0, m0_d, writes=[CONST])
        R.dma("sp", gcols, gcols_d, writes=[CONST])
        R.dma("sp", invf_row, invf_d, writes=[CONST])
        R.op("dve", lambda e: e.memset(ones_bf, 1.0), writes=[CONST])
        R.op("dve", lambda e: e.memset(ones32, 0.0), writes=[CONST])
        R.op("dve", lambda e: e.memset(ones32[64:65, :], 1.0), writes=[CONST])

        XSD = [Buf(f"xnTs{i}") for i in range(12)]
        mixA = A.b16(4 * 2048)
        mixA3 = r3(mixA, 2048)
        MIXA = [[Buf(f"mixA{h}_{m}") for m in range(4)] for h in range(8)]
        work_base = A.top

        BO = 4
        BP = (5, 6)
        BT = 7
        BX = 7

        ATT_PER_ROUND = int(os.environ.get("MK_APR", "1"))

        LOCK7 = {"o": None}

        def acq7(name):
            while LOCK7["o"] not in (None, name):
                yield
            LOCK7["o"] = name

        def rel7():
            LOCK7["o"] = None

        def drain(g):
            for _ in g:
                pass

        def chain(a, b):
            if a is not None:
                yield from a
            yield from b

        def pipeline(steps, before_tail=None, side=None):
            from collections import deque
            queue = deque()
            posts = []

            def pop_head():
                _, pf = queue.popleft()
                for g in posts:
                    for _ in range(100000):
                        try:
                            next(g)
                        except StopIteration:
                            break
                    else:
                        raise RuntimeError("post generator cannot make progress")
                del posts[:]
                if pf is not None:
                    posts.append(pf())

            def advance(n):
                for _ in range(n):
                    while queue:
                        try:
                            next(queue[0][0])
                            break
                        except StopIteration:
                            pop_head()
                    if not queue:
                        break
                for g in list(posts):
                    try:
                        next(g)
                    except StopIteration:
                        posts.remove(g)

            drain(steps[0]["s1"]())
            side = list(side or [])
            for i, st in enumerate(steps):
                if st.get("att") is not None:
                    while len(queue) >= 2:
                        try:
                            next(queue[0][0])
                        except StopIteration:
                            pop_head()
                        for g in list(posts):
                            try:
                                next(g)
                            except StopIteration:
                                posts.remove(g)
                    for g in side:
                        drain(g)
                    side = []
                active = [st["s2"]()]
                if i + 1 < len(steps):
                    active.append(steps[i + 1]["s1"]())
                while active:
                    advance(ATT_PER_ROUND)
                    for g in list(active):
                        try:
                            next(g)
                        except StopIteration:
                            active.remove(g)
                    for g in list(side):
                        try:
                            next(g)
                        except StopIteration:
                            side.remove(g)
                if st.get("att") is not None and not os.environ.get("MK_NOATT"):
                    queue.append((st["att"](), st.get("post")))
            if before_tail is not None:
                before_tail()
            while queue or posts:
                advance(1)

        class XPipe:
            def __init__(self, own_bufs=True):
                if own_bufs:
                    self.xbuf = [A.f32(1024) for _ in range(2)]
                    self.XB = [Buf(f"xbuf{i}") for i in range(2)]
                self.xs = [A.b16(1024) for _ in range(2)]
                self.XS = [Buf(f"xs{i}") for i in range(2)]
                self.n = 0
                self.SM = [Buf(f"small{i}") for i in range(2)]

            def front(self, src_rows, sb=None):
                j = self.n % 2
                self.n += 1
                xsj, XSj, SMj = self.xs[j], self.XS[j], self.SM[j]
                if sb is None:
                    xb, XBj = self.xbuf[j], self.XB[j]
                    R.dma("sp", xb, src_rows, writes=[XBj])
                else:
                    xb, XBj = sb
                ss = small[:, 4 * j + 0: 4 * j + 1]
                lnv = small[:, 4 * j + 1: 4 * j + 2]
                rstd = small[:, 4 * j + 2: 4 * j + 3]
                R.op("act", lambda e: e.activation(out=xsj, in_=xb, func=AF.Square, accum_out=ss),
                     reads=[XBj], writes=[XSj, SMj])
                R.op("act", lambda e: e.activation(out=lnv, in_=ss, func=AF.Ln, scale=1.0 / 1024.0, bias=EPS),
                     reads=[SMj], writes=[SMj])
                R.op("act", lambda e: e.activation(out=rstd, in_=lnv, func=AF.Exp, scale=-0.5),
                     reads=[SMj], writes=[SMj])
                R.op("pool", lambda e: e.tensor_scalar(out=xsj, in0=xb, scalar1=rstd, scalar2=1.0, op0=ALU.mult,
                                                       op1=ALU.mult),
                     reads=[XBj, SMj], writes=[XSj])
                return xsj, XSj

            def back_pe(self, xsj, XSj):
                pst = ps[BT].bitcast(BF16)
                R.mm([(lambda e, k=k: e.transpose(pst[:, k * 128:(k + 1) * 128], xsj[:, k * 128:(k + 1) * 128], ident))
                      for k in range(8)], reads=[XSj, CONST], writes=[PSB[BT]])

            def back_evac(self, gofs, dst3, DST):
                pst = ps[BT].bitcast(BF16)
                g3 = gcols[:, gofs:gofs + 8].unsqueeze(2).to_broadcast([128, 8, 128])
                R.op("dve", lambda e: e.tensor_tensor(out=dst3, in0=r3(pst, 128), in1=g3, op=ALU.mult),
                     reads=[PSB[BT], CONST], writes=[DST])

            def chunk(self, srcs, gofs, xnT3, XNT):
                def fr(j):
                    if isinstance(srcs[j], tuple):
                        return self.front(None, sb=srcs[j])
                    return self.front(srcs[j])
                cur = fr(0)
                yield
                for j in range(4):
                    nxt = fr(j + 1) if j + 1 < 4 else None
                    yield
                    yield from acq7("s1")
                    self.back_pe(*cur)
                    yield
                    self.back_evac(gofs, xnT3[:, :, j * 128:(j + 1) * 128], XNT)
                    rel7()
                    yield
                    cur = nxt

        PROJ_CHUNK = int(os.environ.get("MK_PCH", "4"))

        def proj_fm(bank, m_lo, m_hi, W3, c0, ncol, xnT3, n, reads, chunk=None):
            nk = W3.shape[1]
            ch = chunk or PROJ_CHUNK
            o = ps[bank][m_lo:m_hi, 0:n]
            for k0 in range(0, nk, ch):
                R.mm([(lambda e, k=k: e.matmul(o, lhsT=W3[:, k, c0:c0 + ncol], rhs=xnT3[:, k, 0:n],
                                                start=(k == 0), stop=(k == nk - 1))) for k in range(k0, min(nk, k0 + ch))],
                     reads=reads, writes=[PSB[bank]])
                if k0 + ch < nk:
                    yield

        class NormBufs:
            def __init__(self):
                self.sq = [A.b16(512) for _ in range(2)]
                self.SQ = [Buf(f"sq{i}") for i in range(2)]
                self.lnv = A.f32(512)
                self.LNV = Buf("lnv")
                self.rstd = [A.f32(512) for _ in range(2)]
                self.RSTD = [Buf(f"rstd{i}") for i in range(2)]
                self.n = 0

            def next(self):
                j = self.n % 2
                self.n += 1
                return self.sq[j], self.SQ[j], self.rstd[j], self.RSTD[j]

        def ns_square(NB, slot, bank, sq_rows):
            sq, SQ, rstd, RS = slot
            lo, hi = sq_rows
            R.op("act", lambda e: e.activation(out=sq[lo:hi, :], in_=ps[bank][lo:hi, :], func=AF.Square),
                 reads=[PSB[bank]], writes=[SQ])

        def ns_stats_mm(NB, slot, st_rows, ones_lhsT, denom):
            sq, SQ, rstd, RS = slot
            slo, shi = st_rows
            st = ps[BX][slo:shi, :]
            R.mm([lambda e: e.matmul(st, lhsT=ones_lhsT, rhs=sq[slo:shi, :], start=True, stop=True)],
                 reads=[SQ, CONST], writes=[PSB[BX]])

        def ns_stats_act(NB, slot, st_rows, ones_lhsT, denom):
            sq, SQ, rstd, RS = slot
            slo, shi = st_rows
            st = ps[BX][slo:shi, :]
            R.op("act", lambda e: e.activation(out=NB.lnv[slo:shi, :], in_=st, func=AF.Ln, scale=1.0 / denom, bias=EPS),
                 reads=[PSB[BX]], writes=[NB.LNV])
            R.op("act", lambda e: e.activation(out=rstd[slo:shi, :], in_=NB.lnv[slo:shi, :], func=AF.Exp, scale=-0.5),
                 reads=[NB.LNV], writes=[RS])

        def normed_proj_gen(NB, n_items, proj_fn, sq_rows_fn, stats_args_fn, final_fn):
            slots = [None] * n_items
            banks = [BP[i % 2] for i in range(n_items)]

            def head(i):
                yield from proj_fn(i, banks[i])
                slots[i] = NB.next()
                ns_square(NB, slots[i], banks[i], sq_rows_fn(i))
            for i in range(min(2, n_items)):
                yield from head(i)
                yield
            for p0 in range(0, n_items, 2):
                pair = [i for i in (p0, p0 + 1) if i < n_items]
                for i in pair:
                    yield from acq7("s2")
                    ns_stats_mm(NB, slots[i], *stats_args_fn(i))
                    yield
                    ns_stats_act(NB, slots[i], *stats_args_fn(i))
                    rel7()
                    yield
                for i in pair:
                    final_fn(i, banks[i], slots[i][2], slots[i][3])
                    yield
                    if i + 2 < n_items:
                        yield from head(i + 2)
                        yield

        class AttnBufs:
            def __init__(self):
                self.pb = [A.b16(1024) for _ in range(3)]
                self.PB = [Buf(f"pb{i}") for i in range(3)]
                self.osb = [A.f32(512) for _ in range(2)]
                self.OSB = [Buf(f"osb{i}") for i in range(2)]
                for o_, OB_ in zip(self.osb, self.OSB):
                    R.op("pool", lambda e, o_=o_: e.memset(o_, 0.0), writes=[OB_])
                self.u = 0
                self.hn = 0

        def attention(AB, heads, scale):
            groups = []
            for h in heads:
                us = h["units"]
                k = 0
                while k < len(us):
                    if (not os.environ.get("MK_NOPAIR") and k + 1 < len(us) and us[k][4:6] == us[k + 1][4:6]
                            and (h.get("gmask") is None or us[k + 1][7] - us[k][7] == 128)):
                        groups.append((h, [k, k + 1]))
                        k += 2
                    else:
                        groups.append((h, [k]))
                        k += 1
            nG = len(groups)

            def issue_S(g):
                h, idxs = groups[g]
                pr = g % 2
                for slot, k in enumerate(idxs):
                    kT, KB, v, VB, i0, i1, masks, meta = h["units"][k]
                    o = pairs[pr][:, slot * 512 + i0 * 128:slot * 512 + i1 * 128]
                    q = h["qT"][:, i0 * 128:i1 * 128]
                    R.mm([lambda e, o=o, kT=kT, q=q: e.matmul(o, lhsT=kT, rhs=q, start=True, stop=True)],
                         reads=[KB, h["QB"]], writes=[SPB[pr]])

            def issue_PV(g):
                h, idxs = groups[g]
                i0, i1 = h["units"][idxs[0]][4:6]
                pb, PBj = pbs[g]
                nun = len(h["units"])
                for slot, k in enumerate(idxs):
                    v, VB = h["units"][k][2:4]
                    oo = ps[BO][0:65, i0 * 128:i1 * 128]
                    pv = pb[:, slot * 512 + i0 * 128:slot * 512 + i1 * 128]
                    R.mm([lambda e, oo=oo, v=v, pv=pv, k=k, nun=nun: e.matmul(oo, lhsT=v, rhs=pv, start=(k == 0),
                                                                              stop=(k == nun - 1))],
                         reads=[VB, PBj], writes=[PSB[BO]])
                return idxs[-1] == nun - 1

            def fin_a(h):
                oj = AB.hn % 2
                AB.hn += 1
                osb, OSBj = AB.osb[oj], AB.OSB[oj]
                R.op("act", lambda e, osb=osb: e.activation(out=osb[0:65, :], in_=ps[BO][0:65, :], func=AF.Copy),
                     reads=[PSB[BO]], writes=[OSBj])
                R.op("act", lambda e, osb=osb: e.activation(out=osb[64:65, :], in_=osb[64:65, :], func=AF.Ln),
                     reads=[OSBj], writes=[OSBj])
                R.op("act", lambda e, osb=osb: e.activation(out=osb[64:65, :], in_=osb[64:65, :], func=AF.Exp, scale=-1.0),
                     reads=[OSBj], writes=[OSBj])
                return (h, osb, OSBj)

            def fin_b(h, osb, OSBj):
                bc = ps[BX][0:64, :]
                R.mm([lambda e, osb=osb: e.matmul(ps[BX][:, :], lhsT=ones32, rhs=osb, start=True, stop=True)],
                     reads=[OSBj, CONST], writes=[PSB[BX]])
                dst = h["dst"]
                R.op("dve", lambda e, osb=osb, bc=bc, dst=dst: e.tensor_tensor(out=dst, in0=osb[0:64, :], in1=bc, op=ALU.mult),
                     reads=[OSBj, PSB[BX]], writes=[h["DST"]])

            pbs = {}
            pend_a = []
            pend_b = []
            issue_S(0)
            for g in range(nG + 1):
                if g + 1 < nG:
                    issue_S(g + 1)
                if pend_b:
                    while LOCK7["o"] not in (None, "att"):
                        yield
                    for item in pend_b:
                        fin_b(*item)
                    pend_b = []
                if pend_a:
                    pend_b = [fin_a(h_) for h_ in pend_a]
                    pend_a = []
                if g >= 1:
                    if issue_PV(g - 1):
                        pend_a.append(groups[g - 1][0])
                    del pbs[g - 1]
                if g < nG:
                    h, idxs = groups[g]
                    pr = g % 2
                    ns = len(idxs)
                    i0, i1 = h["units"][idxs[0]][4:6]
                    pj = AB.u % 3
                    AB.u += 1
                    pb, PBj = AB.pb[pj], AB.PB[pj]
                    pbs[g] = (pb, PBj)
                    src = r3(pairs[pr], 512)[:, 0:ns, i0 * 128:i1 * 128]
                    pv3 = r3(pb, 512)[:, 0:ns, i0 * 128:i1 * 128]
                    R.op("act", lambda e, src=src, pv3=pv3: e.activation(out=pv3, in_=src, func=AF.Exp, scale=scale),
                         reads=[SPB[pr]], writes=[PBj])
                    if h.get("gmask") is not None:
                        meng, mfn, mreads = h["gmask"]([h["units"][k][7] for k in idxs], i0, i1, pb)
                        R.op(meng, mfn, reads=list(mreads) + [PBj], writes=[PBj])
                    for slot, k in enumerate(idxs):
                        for (meng, mfn, mreads) in h["units"][k][6]:
                            pslot = pb[:, slot * 512:(slot + 1) * 512]
                            R.op(meng, (lambda e, mfn=mfn, pslot=pslot: mfn(e, pslot)), reads=list(mreads) + [PBj],
                                 writes=[PBj])
                yield
            for _ in range(2):
                if pend_b:
                    while LOCK7["o"] not in (None, "att"):
                        yield
                    for item in pend_b:
                        fin_b(*item)
                    pend_b = []
                if pend_a:
                    pend_b = [fin_a(h_) for h_ in pend_a]
                    pend_a = []
                yield

        def pass_A():
            A.top = work_base
            WA = A.b16(8 * 1536)
            WA3 = r3(WA, 1536)
            WAB = Buf("WA")
            kat_off = (A.top + 63) // 64 * 64
            KaT = A.b16(4 * S)
            KaT3 = r3(KaT, S)
            KAB = [Buf(f"KaT{c}") for c in range(8)]
            Va = A.b16(32 * 8 * 65)
            Va4 = Va.rearrange("p (k h d) -> p k h d", h=8, d=65)
            VAB = [Buf(f"Va{k}") for k in range(32)]
            TT = A.b16(8 * NTTP)
            TT3 = r3(TT, NTTP)
            TTB = [Buf(f"TT{h}") for h in range(8)]
            XP = XPipe()
            xnT = [A.b16(8 * 512) for _ in range(2)]
            xnT3 = [r3(t, 512) for t in xnT]
            XNT = [Buf(f"xnT{i}") for i in range(2)]
            QaT = [A.b16(8 * 512) for _ in range(2)]
            QaT3 = [r3(t, 512) for t in QaT]
            QAB = [[Buf(f"QaT{i}_{f}") for f in range(4)] for i in range(2)]

            NB = NormBufs()
            AB = AttnBufs()


            ALLMIX = [b for hb in MIXA for b in hb]
            mixa_off = work_base - 2 * 4 * 2048
            stg = Arena(arena_t, ARENA_BYTES)
            stg.top = mixa_off
            relb = stg.f32(8, parts=32)
            eb = stg.b16(8 * 128, parts=32)
            oh = stg.b16(LU, parts=32)
            ust = stg.b16(LU)
            assert stg.top <= work_base
            R.dma("sp", relb, relb_d, writes=ALLMIX)
            R.dma("pool", oh, oh_d, writes=ALLMIX)
            for g in (1, 2, 0):
                R.dma("pool", WA3[:, :, g * 512:(g + 1) * 512],
                      w_in[:, g * 512:(g + 1) * 512].rearrange("(k p) c -> p k c", p=128), writes=[WAB])
            R.op("pool", lambda e: e.memset(Va4[:, :, :, 64:65], 1.0), writes=VAB)
            for i_ in range(2):
                R.op("pool", lambda e, i_=i_: e.memset(QaT[i_], 0.0), writes=QAB[i_])
            R.op("act", lambda e: e.activation(out=relb, in_=relb, func=AF.Exp), reads=ALLMIX, writes=ALLMIX)
            R.op("dve", lambda e: e.tensor_copy(out=r3(eb, 128), in_=relb.unsqueeze(2).to_broadcast([32, 8, 128])),
                 reads=ALLMIX, writes=ALLMIX)
            USCR = [Buf(f"uscr{h}") for h in range(8)]

            def tt_gen():
                nb = 0
                for h in range(8):
                    for n in range(5):
                        w = min(512, LU - n * 512)
                        bank = nb % 4
                        nb += 1
                        R.mm([lambda e, h=h, n=n, w=w, bank=bank: e.matmul(ps[bank][:, 0:w], lhsT=eb[:, h * 128:(h + 1) * 128],
                                                                           rhs=oh[:, n * 512:n * 512 + w], start=True, stop=True)],
                             reads=ALLMIX, writes=[PSB[bank]])
                        yield
                        if n % 2 == 0:
                            R.op("dve", lambda e, n=n, w=w, bank=bank: e.tensor_copy(out=ust[:, n * 512:n * 512 + w],
                                                                                  in_=ps[bank][:, 0:w]),
                                 reads=[PSB[bank]], writes=ALLMIX)
                        else:
                            R.op("act", lambda e, n=n, w=w, bank=bank: e.activation(out=ust[:, n * 512:n * 512 + w],
                                                                                 in_=ps[bank][:, 0:w], func=AF.Copy),
                                 reads=[PSB[bank]], writes=ALLMIX)
                        yield
                    R.dma("sp", uscr[h], ust, reads=ALLMIX, writes=[USCR[h]])
                    src = bass.AP(uscr.tensor, h * 128 * LU + 127, [[LU - 1, 128], [1, NTT]])
                    R.dma("sp", TT3[:, h, 0:NTT], src, reads=[USCR[h]], writes=[TTB[h]])
                    yield

            def s1_kv(c, xb):
                yield from XP.chunk([xf[(c * 4 + j) * 128:(c * 4 + j + 1) * 128, :] for j in range(4)], G_MIX,
                                    xnT3[xb], XNT[xb])
                R.dma("sp", xnTs[c], xnT[xb], reads=[XNT[xb]], writes=[XSD[c]])

            def s1_q(M, xb):
                yield from XP.chunk([xo[(M * 4 + i) * 128:(M * 4 + i + 1) * 128, :] for i in range(4)], G_MIX,
                                    xnT3[xb], XNT[xb])
                R.dma("sp", xnTs[8 + M], xnT[xb], reads=[XNT[xb]], writes=[XSD[8 + M]])

            def s2_kv(c, xb):
                x3, XN = xnT3[xb], XNT[xb]

                def final(ft, bank, rstd, RS):
                    R.op("dve", lambda e: e.scalar_tensor_tensor(out=KaT3[:, ft, c * 512:(c + 1) * 512], in0=ps[bank][:, :],
                                                                 scalar=gcols[:, G_KA:G_KA + 1], in1=rstd,
                                                                 op0=ALU.mult, op1=ALU.mult),
                         reads=[PSB[bank], RS, CONST], writes=[KAB[c]])
                yield from normed_proj_gen(
                    NB, 4, lambda ft, bank: proj_fm(bank, 0, 128, WA3, 512 + ft * 128, 128, x3, 512, [WAB, XN]),
                    lambda ft: (0, 128), lambda ft: ((0, 128), onesblk, 64.0), final)
                for j in range(4):
                    bank = BP[j % 2]
                    kb = c * 4 + j
                    for k0 in range(0, 8, PROJ_CHUNK):
                        R.mm([(lambda e, k=k, j=j, bank=bank: e.matmul(ps[bank][:, :], lhsT=x3[:, k, j * 128:(j + 1) * 128],
                                                                      rhs=WA3[:, k, 1024:1536], start=(k == 0), stop=(k == 7)))
                              for k in range(k0, k0 + PROJ_CHUNK)], reads=[WAB, XN], writes=[PSB[bank]])
                        yield
                    R.op("dve", lambda e, kb=kb, bank=bank: e.tensor_copy(out=Va4[:, kb, :, 0:64],
                                                                       in_=ps[bank][:, :].rearrange("p (h d) -> p h d", d=64)),
                         reads=[PSB[bank]], writes=[VAB[kb]])
                    yield

            def s2_q(M, xb):
                x3, XN = xnT3[xb], XNT[xb]
                qb = M % 2

                def final(ft, bank, rstd, RS):
                    for hp in range(2):
                        rw = slice(hp * 64, hp * 64 + 64)
                        R.op("dve", lambda e, rw=rw, hp=hp: e.scalar_tensor_tensor(
                            out=QaT3[qb][rw, 2 * ft + hp, :], in0=ps[bank][rw, :], scalar=gcols[rw, G_QA:G_QA + 1],
                            in1=rstd[rw, :], op0=ALU.mult, op1=ALU.mult),
                            reads=[PSB[bank], RS, CONST], writes=[QAB[qb][ft]])
                yield from normed_proj_gen(
                    NB, 4, lambda ft, bank: proj_fm(bank, 0, 128, WA3, ft * 128, 128, x3, 512, [WAB, XN]),
                    lambda ft: (0, 128), lambda ft: ((0, 128), onesblk, 64.0), final)

            def att_A(M):
                qb = M % 2
                heads = []
                for h in range(8):
                    ft, bp = h // 2, (h % 2) * 64
                    units = []
                    kbs = list(range(8 * M + 7, max(0, 8 * M - 16) - 1, -1))
                    first = 8 * M + 1
                    kbs.remove(first)
                    kbs = [first] + kbs
                    for kb in kbs:
                        Dd = 8 * M - kb
                        i0 = max(0, -((1 + Dd) // 2))
                        i1 = min(4, (16 - Dd) // 2 + 1)
                        if i1 <= i0:
                            continue
                        col0 = (Dd + 2 * i0 + 1) * 128
                        ni = i1 - i0
                        units.append((KaT3[:, ft, kb * 128:(kb + 1) * 128], KAB[kb // 4],
                                      Va4[:, kb, h, :], VAB[kb], i0, i1, [], col0))

                    def gmask(metas, i0, i1, pb, h=h):
                        ns, ni = len(metas), i1 - i0
                        t0 = TT3[:, h, metas[0]:metas[0] + 128]
                        ttv = bass.AP(t0.tensor, t0.offset, [list(t0.ap[0]), [128, ns], [256, ni], [1, 128]])
                        p0 = pb[:, i0 * 128:i0 * 128 + 128]
                        pv4 = bass.AP(p0.tensor, p0.offset, [list(p0.ap[0]), [512, ns], [128, ni], [1, 128]])
                        return ("dve", (lambda e: e.tensor_tensor(out=pv4, in0=pv4, in1=ttv, op=ALU.mult)), [TTB[h]])
                    heads.append(dict(units=units, qT=QaT3[qb][:, h, :], QB=QAB[qb][ft], gmask=gmask,
                                      dst=mixA3[bp:bp + 64, ft, M * 512:(M + 1) * 512], DST=MIXA[h][M]))
                return attention(AB, heads, 0.125)

            steps = []
            n = 0
            for c in range(8):
                steps.append(dict(s1=(lambda c=c, n=n: s1_kv(c, n % 2)), s2=(lambda c=c, n=n: s2_kv(c, n % 2))))
                n += 1
                if c % 2 == 1:
                    M = c // 2
                    steps.append(dict(s1=(lambda M=M, n=n: s1_q(M, n % 2)), s2=(lambda M=M, n=n: s2_q(M, n % 2)),
                                      att=(lambda M=M: att_A(M))))
                    n += 1
            pipeline(steps, side=[tt_gen()])
            dump_ap("KaT", KaT, KAB, [128, 4 * S], BF16)
            dump_ap("Va", Va, VAB, [128, 32 * 8 * 65], BF16)
            dump_ap("TT", TT, TTB, [128, 8 * NTTP], BF16)
            dump_ap("mixA", mixA, [b for hb in MIXA for b in hb], [128, 4 * 2048], BF16)
            R.barrier()

        A.top = work_base
        mixB = A.b16(2 * 2048)
        mixB3 = r3(mixB, 2048)
        mixBh = [A.b16(2 * 512) for _ in range(2)]
        mixBh3 = [r3(t, 512) for t in mixBh]
        MIXB = [[Buf(f"mixB{h}_{m}") for m in range(4)] for h in range(8)]
        ident32 = A.f32(128)
        workB_base = A.top
        SCALE_B = 1.0 / math.sqrt(96.0)
        H1S = [Buf(f"h1s{i}") for i in range(16)]
        PHC = {}
        W1B = [Buf(f"W1_{g}") for g in range(8)]
        W2B = [Buf(f"W2_{g}") for g in range(8)]
        CQS = [Buf(f"cqs{i}") for i in range(4)]
        CKVS = [Buf(f"ckvs{i}") for i in range(8)]
        KRGS = [Buf(f"krgs{i}") for i in range(8)]
        SQKS = [Buf(f"sqks{i}") for i in range(8)]
        TRQS = [Buf(f"trqs{i}") for i in range(4)]

        def pass_B(hh):
            A.top = workB_base
            W1_ap = A.b16(8 * 4096)
            PHC["W1"] = W1_ap
            PHC["W2"] = A.b16(32 * 1024)
            A.top = workB_base
            Wc = A.b16(8 * 1152)
            Wc3 = r3(Wc, 1152)
            WCB = Buf("Wc")
            Wuq = A.b16(6 * 512)
            Wuq3 = r3(Wuq, 512)
            Wuq4 = Wuq.rearrange("p (k h c) -> p k h c", k=6, h=4, c=128)
            WUQ = Buf("Wuq")
            Wuk = A.b16(2 * 512)
            Wuk3 = r3(Wuk, 512)
            Wuv = A.b16(2 * 256)
            Wuv3 = r3(Wuv, 256)
            WUK = Buf("Wukv")
            xnT = [A.b16(8 * 512) for _ in range(2)]
            xnT3 = [r3(t, 512) for t in xnT]
            XNT = [Buf(f"xnT{i}") for i in range(2)]
            NB = NormBufs()
            cq = A.b16(6 * 512)
            cq3 = r3(cq, 512)
            CQ = Buf("cq")
            ckv = A.b16(2 * 512)
            ckv3 = r3(ckv, 512)
            CKV = Buf("ckv")
            rstd_c = A.f32(512)
            RSC = Buf("rstd_c")
            ssacc = A.f32(512)
            SSA = Buf("ssacc")
            qg = A.f32(512)
            QG = Buf("qg")
            krg = A.f32(512)
            KRG = Buf("krg")
            rtmp = A.f32(1024)
            rt3 = r3(rtmp, 512)
            RT = Buf("rtmp")
            cosT = [A.f32(512)] * 2
            sinT = [A.f32(512)] * 2
            TRIG = [Buf("trig")] * 2
            NBK = 48
            sin_tm = A.f32(NBK * 16)
            cos_tm = A.f32(NBK * 16)
            TM = Buf("trig_tm")
            assert A.top - workB_base >= 2 * 8 * 4096, "W1 must fit inside the dead-early region"
            PHC["nw2"] = max(0, min(8, (A.top - workB_base - 2 * 8 * 4096) // 8192)) if hh == 1 else 0
            QbT = [A.b16(4 * 512) for _ in range(2)]
            QbT3 = [r3(t, 512) for t in QbT]
            QBB = [[Buf(f"QbT{i}_{h}") for h in range(4)] for i in range(2)]
            if hh == 1:
                Wo = A.b16(8 * 1024)
                Wo3 = r3(Wo, 1024)
                WOB = Buf("Wo")
                h1t = [A.f32(1024) for _ in range(2)]
                H1T = [Buf(f"h1t{i}") for i in range(2)]
            KbT = A.b16(4 * S)
            KbT3 = r3(KbT, S)
            KBB = [Buf(f"KbT{c}") for c in range(8)]
            Vb = A.b16(32 * 4 * 65)
            Vb4 = Vb.rearrange("p (k h d) -> p k h d", h=4, d=65)
            VBB = [Buf(f"Vb{k}") for k in range(32)]
            AB = AttnBufs()

            if hh == 0:
                for g in range(2):
                    R.dma("pool", Wc3[:, :, g * 512:(g + 1) * 512],
                          w_in[:, 1536 + g * 512:1536 + (g + 1) * 512].rearrange("(k p) c -> p k c", p=128), writes=[WCB])
                R.op("pool", lambda e: e.memset(Wc3[:, :, 1024:1152], 0.0), writes=[WCB])
                R.dma("pool", Wc3[:, :, 1024:1040], w_in[:, 2560:2576].rearrange("(k p) c -> p k c", p=128), writes=[WCB])
                R.dma("pool", Wc3[:, :, 1056:1072], w_in[:, 2576:2592].rearrange("(k p) c -> p k c", p=128), writes=[WCB])
            skv = w_ukv[:, hh * 512:(hh + 1) * 512].rearrange("(k p) (h t d) -> p k h t d", p=128, t=2, d=64)
            Wuk4 = Wuk.rearrange("p (k h d) -> p k h d", k=2, h=4, d=128)
            Wuv4 = Wuv.rearrange("p (k h d) -> p k h d", k=2, h=4, d=64)
            R.op("pool", lambda e: e.memset(Wuk, 0.0), writes=[WUK])
            for k2 in range(2):
                R.dma("pool", Wuk4[:, k2, :, 64:128], skv[:, k2, :, 0, :], writes=[WUK])
                R.dma("pool", Wuv4[:, k2, :, :], skv[:, k2, :, 1, :], writes=[WUK])
            def late_weights():
                R.op("pool", lambda e: e.memset(Wuq, 0.0), writes=[WUQ])
                squ = w_uq[:, hh * 384:(hh + 1) * 384].rearrange("(k p) (h c) -> p k h c", p=128, c=96)
                for h4 in range(4):
                    R.dma("pool", Wuq4[:, :, h4, 0:16], squ[:, :, h4, 64:80], writes=[WUQ])
                    yield
                    R.dma("pool", Wuq4[:, :, h4, 32:48], squ[:, :, h4, 80:96], writes=[WUQ])
                    yield
                    R.dma("pool", Wuq4[:, :, h4, 64:128], squ[:, :, h4, 0:64], writes=[WUQ])
                    yield
                if hh == 1:
                    for g in range(2):
                        R.dma("pool", Wo3[:, :, g * 512:(g + 1) * 512],
                              w_o[:, g * 512:(g + 1) * 512].rearrange("(k p) c -> p k c", p=128), writes=[WOB])
                        yield
            R.op("pool", lambda e: e.memset(Vb4[:, :, :, 64:65], 1.0), writes=VBB)
            R.op("dve", lambda e: e.memset(rtmp, 0.0), writes=[RT])
            R.op("dve", lambda e: e.memset(cosT[0], 0.0), writes=[TRIG[0]])
            R.op("dve", lambda e: e.memset(sinT[0], 0.0), writes=[TRIG[0]])
            R.dma("sp", ident32, ident_d, writes=[CONST])

            PREP = [CQ, QG, KRG] + QBB[0] + QBB[1]
            n_el = NBK * 16
            tmp_pool = [cq[:, 0:1536].bitcast(F32), cq[:, 1536:3072].bitcast(F32),
                        QbT[0][:, 0:1536].bitcast(F32), QbT[1][:, 0:1536].bitcast(F32)]
            posi = tmp_pool[0].bitcast(I32)
            ang, nf, rr = tmp_pool[1], tmp_pool[2], tmp_pool[3]
            tt_ = tmp_pool[0]
            pti = qg[:, 0:NBK].bitcast(I32)
            ptf = krg[:, 0:NBK]
            if hh == 0:
                R.dma("sp", pti, postm_d, writes=PREP)
                R.op("dve", lambda e: e.tensor_copy(out=ptf, in_=pti), reads=PREP, writes=PREP)
                R.op("dve", lambda e: e.tensor_tensor(out=r3(ang, 16), in0=ptf.unsqueeze(2).to_broadcast([128, NBK, 16]),
                                                      in1=invf_row.unsqueeze(1).to_broadcast([128, NBK, 16]), op=ALU.mult),
                     reads=PREP + [CONST], writes=PREP)

            def reduce_and_sin(shift, dst):
                W = PREP
                R.op("dve", lambda e: e.tensor_scalar(out=posi, in0=ang, scalar1=1.0 / (2 * PI),
                                                      scalar2=0.5 + shift / (2 * PI), op0=ALU.mult, op1=ALU.add),
                     reads=W, writes=W)
                R.op("dve", lambda e: e.tensor_copy(out=nf, in_=posi), reads=W, writes=W)
                R.op("dve", lambda e: e.scalar_tensor_tensor(out=rr, in0=nf, scalar=-C1, in1=ang,
                                                             op0=ALU.mult, op1=ALU.add), reads=W, writes=W)
                R.op("dve", lambda e: e.scalar_tensor_tensor(out=rr, in0=nf, scalar=-C2, in1=rr,
                                                             op0=ALU.mult, op1=ALU.add), reads=W, writes=W)
                if shift != 0.0:
                    R.op("dve", lambda e: e.tensor_scalar(out=rr, in0=rr, scalar1=shift, scalar2=None,
                                                          op0=ALU.add), reads=W, writes=W)
                R.op("dve", lambda e: e.tensor_single_scalar(out=tt_, in_=rr, scalar=PI, op=ALU.is_gt),
                     reads=W, writes=W)
                R.op("dve", lambda e: e.scalar_tensor_tensor(out=rr, in0=tt_, scalar=-2 * PI, in1=rr,
                                                             op0=ALU.mult, op1=ALU.add), reads=W, writes=W)
                R.op("dve", lambda e: e.tensor_single_scalar(out=tt_, in_=rr, scalar=-PI, op=ALU.is_lt),
                     reads=W, writes=W)
                R.op("dve", lambda e: e.scalar_tensor_tensor(out=rr, in0=tt_, scalar=2 * PI, in1=rr,
                                                             op0=ALU.mult, op1=ALU.add), reads=W, writes=W)
                R.op("dve", lambda e: e.tensor_scalar(out=rr, in0=rr, scalar1=-PI, scalar2=PI,
                                                      op0=ALU.max, op1=ALU.min), reads=W, writes=W)
                R.op("act", lambda e: e.activation(out=dst, in_=rr, func=AF.Sin), reads=W, writes=[TM])
            if hh == 0:
                reduce_and_sin(0.0, sin_tm)
                reduce_and_sin(PI / 2, cos_tm)
            sin3 = r3(sin_tm, 16)
            cos3 = r3(cos_tm, 16)

            def rope_tables(blk0, ti):
                for tbl3, bank, dstT in ((cos3, BT, cosT[ti]), (sin3, BX, sinT[ti])):
                    yield from acq7("s2")
                    R.mm([(lambda e, j=j, tbl3=tbl3, bank=bank: e.transpose(ps[bank][0:16, j * 128:(j + 1) * 128],
                                                                            tbl3[:, blk0 + j, :], ident32))
                          for j in range(4)], reads=[TM, CONST], writes=[PSB[bank]])
                    yield
                    R.op("dve", lambda e, bank=bank, dstT=dstT: e.tensor_copy(out=dstT[0:16, :], in_=ps[bank][0:16, :]),
                         reads=[PSB[bank]], writes=[TRIG[ti]])
                    sgn = -1.0 if tbl3 is sin3 else 1.0
                    R.op("dve", lambda e, bank=bank, dstT=dstT, sgn=sgn: e.tensor_scalar(
                        out=dstT[32:48, :], in0=ps[bank][0:16, :], scalar1=sgn, scalar2=None, op0=ALU.mult),
                        reads=[PSB[bank]], writes=[TRIG[ti]])
                    rel7()
                    yield

            def rope(t, TB, ti):
                a, b, ab = slice(0, 16), slice(32, 48), slice(0, 48)
                cT, sT, TG = cosT[ti], sinT[ti], TRIG[ti]
                R.op("dve", lambda e: e.tensor_tensor(out=rt3[ab, 0, :], in0=t[ab, :], in1=cT[ab, :], op=ALU.mult),
                     reads=[TB, TG], writes=[RT])
                R.op("dve", lambda e: e.tensor_tensor(out=rt3[a, 1, :], in0=t[b, :], in1=sT[b, :], op=ALU.mult),
                     reads=[TB, TG], writes=[RT])
                R.op("dve", lambda e: e.tensor_tensor(out=rt3[b, 1, :], in0=t[a, :], in1=sT[a, :], op=ALU.mult),
                     reads=[TB, TG], writes=[RT])
                R.op("dve", lambda e: e.tensor_tensor(out=t[ab, :], in0=rt3[ab, 0, :], in1=rt3[ab, 1, :], op=ALU.add),
                     reads=[RT], writes=[TB])

            def c_norm(ntile, col0, gofs, st3, STB, x3, XN):
                for t in range(ntile):
                    bank = BP[t % 2]
                    yield from proj_fm(bank, 0, 128, Wc3, col0 + t * 128, 128, x3, 512, [WCB, XN])
                    sq, SQ, _, _ = NB.next()
                    R.op("act", lambda e, sq=sq, bank=bank: e.activation(out=sq, in_=ps[bank][:, :], func=AF.Square),
                         reads=[PSB[bank]], writes=[SQ])
                    R.op("dve", lambda e, t=t, bank=bank: e.tensor_scalar(out=st3[:, t, :], in0=ps[bank][:, :],
                                                                        scalar1=gcols[:, gofs + t:gofs + t + 1],
                                                                        scalar2=None, op0=ALU.mult),
                         reads=[PSB[bank], CONST], writes=[STB])
                    yield
                    yield from acq7("s2")
                    R.mm([lambda e, sq=sq: e.matmul(ps[BX][:, :], lhsT=ones_bf, rhs=sq, start=True, stop=True)],
                         reads=[SQ, CONST], writes=[PSB[BX]])
                    yield
                    if t == 0:
                        R.op("dve", lambda e: e.tensor_copy(out=ssacc, in_=ps[BX][:, :]), reads=[PSB[BX]], writes=[SSA])
                    else:
                        R.op("dve", lambda e: e.tensor_tensor(out=ssacc, in0=ps[BX][:, :], in1=ssacc, op=ALU.add),
                             reads=[PSB[BX], SSA], writes=[SSA])
                    rel7()
                    yield
                R.op("act", lambda e: e.activation(out=NB.lnv, in_=ssacc, func=AF.Ln, scale=1.0 / (ntile * 128.0), bias=EPS),
                     reads=[SSA], writes=[NB.LNV])
                R.op("act", lambda e: e.activation(out=rstd_c, in_=NB.lnv, func=AF.Exp, scale=-0.5),
                     reads=[NB.LNV], writes=[RSC])
                yield
                for t in range(ntile):
                    R.op("pool" if t % 2 else "dve",
                         lambda e, t=t: e.tensor_tensor(out=st3[:, t, :], in0=st3[:, t, :], in1=rstd_c, op=ALU.mult),
                         reads=[RSC, STB], writes=[STB])
                    if t % 2 == 1:
                        yield

            def s1_kv(c, xb):
                if hh == 0:
                    R.dma("sp", xnT[xb], xnTs[c], reads=[XSD[c]], writes=[XNT[xb]])
                yield

            def s1_q(M, xb):
                if hh == 0:
                    R.dma("sp", xnT[xb], xnTs[8 + M], reads=[XSD[8 + M]], writes=[XNT[xb]])
                else:
                    R.dma("sp", cosT[1][0:48, :], trqs[M, 0], reads=[TRQS[M]], writes=[TRIG[1]])
                    R.dma("sp", sinT[1][0:48, :], trqs[M, 1], reads=[TRQS[M]], writes=[TRIG[1]])
                    R.dma("sp", cq, cqs[M], reads=[CQS[M]], writes=[CQ])
                yield

            def s2_kv(c, xb):
                x3, XN = xnT3[xb], XNT[xb]
                cols = slice(c * 512, (c + 1) * 512)
                if hh == 0:
                    yield from c_norm(2, 768, G_CKV, ckv3, CKV, x3, XN)
                    R.dma("sp", ckvs[c], ckv, reads=[CKV], writes=[CKVS[c]])
                    yield from rope_tables(4 * c, 0)
                    bank = BP[0]
                    yield from proj_fm(bank, 0, 128, Wc3, 1024, 128, x3, 512, [WCB, XN])
                    for j in range(2):
                        R.op("act", lambda e, j=j, bank=bank: e.activation(out=NB.sq[j][0:64, :], in_=ps[bank][0:64, :], func=AF.Square),
                             reads=[PSB[bank]], writes=[NB.SQ[j]])
                    R.dma("sp", sqks[c], NB.sq[0][0:64, :], reads=[NB.SQ[0]], writes=[SQKS[c]])
                    R.op("dve", lambda e, bank=bank: e.tensor_scalar(out=krg[0:64, :], in0=ps[bank][0:64, :],
                                                                   scalar1=gcols[0:64, G_KB:G_KB + 1], scalar2=None, op0=ALU.mult),
                         reads=[PSB[bank], CONST], writes=[KRG])
                    yield
                    rope(krg, KRG, 0)
                    R.dma("sp", krgs[c], krg[0:64, :], reads=[KRG], writes=[KRGS[c]])
                    yield
                else:
                    R.dma("sp", ckv, ckvs[c], reads=[CKVS[c]], writes=[CKV])
                    R.dma("sp", krg[0:64, :], krgs[c], reads=[KRGS[c]], writes=[KRG])
                    for j in range(2):
                        R.dma("sp", NB.sq[j][0:64, :], sqks[c], reads=[SQKS[c]], writes=[NB.SQ[j]])
                    yield

                def final(hl, bank, rstd, RS):
                    R.op("dve", lambda e: e.scalar_tensor_tensor(
                        out=KbT3[64:128, hl, cols], in0=ps[bank][64:128, :], scalar=gcols[64:128, G_KB:G_KB + 1],
                        in1=rstd[64:128, :], op0=ALU.mult, op1=ALU.mult),
                        reads=[PSB[bank], RS, CONST], writes=[KBB[c]])
                    R.op("pool", lambda e: e.tensor_tensor(out=KbT3[0:64, hl, cols], in0=krg[0:64, :],
                                                           in1=rstd[0:64, :], op=ALU.mult),
                         reads=[KRG, RS], writes=[KBB[c]])
                yield from normed_proj_gen(
                    NB, 4, lambda hl, bank: proj_fm(bank, 0, 128, Wuk3, hl * 128, 128, ckv3, 512, [WUK, CKV]),
                    lambda hl: (64, 128), lambda hl: ((0, 128), ones_bf, 96.0), final)
                for j in range(4):
                    bank = BP[j % 2]
                    kb = c * 4 + j
                    R.mm([(lambda e, k=k, j=j, bank=bank: e.matmul(ps[bank][:, 0:256], lhsT=ckv3[:, k, j * 128:(j + 1) * 128],
                                                                  rhs=Wuv3[:, k, :], start=(k == 0), stop=(k == 1)))
                          for k in range(2)], reads=[WUK, CKV], writes=[PSB[bank]])
                    R.op("dve", lambda e, kb=kb, bank=bank: e.tensor_copy(out=Vb4[:, kb, :, 0:64],
                                                                       in_=ps[bank][:, 0:256].rearrange("p (h d) -> p h d", d=64)),
                         reads=[PSB[bank]], writes=[VBB[kb]])
                    yield

            def s2_q(M, xb):
                x3, XN = xnT3[xb], XNT[xb]
                qb = M % 2
                if hh == 0:
                    yield from rope_tables(32 + 4 * M, 1)
                    R.dma("sp", trqs[M, 0], cosT[1][0:48, :], reads=[TRIG[1]], writes=[TRQS[M]])
                    R.dma("sp", trqs[M, 1], sinT[1][0:48, :], reads=[TRIG[1]], writes=[TRQS[M]])
                    yield from c_norm(6, 0, G_CQ, cq3, CQ, x3, XN)
                    R.dma("sp", cqs[M], cq, reads=[CQ], writes=[CQS[M]])
                else:
                    yield

                def final(hl, bank, rstd, RS):
                    R.op("act", lambda e: e.activation(out=qg, in_=ps[bank][:, :], func=AF.Copy,
                                                       scale=gcols[:, G_QB:G_QB + 1]),
                         reads=[PSB[bank], CONST], writes=[QG])
                    rope(qg, QG, 1)
                    R.op("dve", lambda e: e.tensor_tensor(out=QbT3[qb][:, hl, :], in0=qg, in1=rstd, op=ALU.mult),
                         reads=[QG, RS], writes=[QBB[qb][hl]])
                yield from normed_proj_gen(
                    NB, 4, lambda hl, bank: proj_fm(bank, 0, 128, Wuq3, hl * 128, 128, cq3, 512, [WUQ, CQ]),
                    lambda hl: (0, 128), lambda hl: ((0, 128), ones_bf, 96.0), final)

            def att_B(M):
                qb = M % 2
                heads = []
                for hl in range(4):
                    hb = 4 * hh + hl
                    ft, bp = hb // 2, (hb % 2) * 64
                    units = []
                    for kb in range(0, 8 * M + 8):
                        Dd = 8 * M - kb
                        i0 = max(0, -((1 + Dd) // 2))
                        masks = []
                        for i in range(i0, 4):
                            dl = Dd + 2 * i
                            if dl in (-1, 0):
                                mt = mm1 if dl == -1 else m0

                                def mfn(e, pslot, mt=mt, i=i):
                                    pv = pslot[:, i * 128:(i + 1) * 128]
                                    return e.tensor_tensor(out=pv, in0=pv, in1=mt, op=ALU.mult)
                                masks.append(("pool", mfn, [CONST]))
                        units.append((KbT3[:, hl, kb * 128:(kb + 1) * 128], KBB[kb // 4],
                                      Vb4[:, kb, hl, :], VBB[kb], i0, 4, masks, 0))
                    dst_ = (mixB3[bp:bp + 64, ft, M * 512:(M + 1) * 512] if hh == 0
                            else mixBh3[M % 2][bp:bp + 64, ft - 2, :])
                    heads.append(dict(units=units, qT=QbT3[qb][:, hl, :], QB=QBB[qb][hl],
                                      dst=dst_, DST=MIXB[hb][M]))
                yield from attention(AB, heads, SCALE_B)

            def wo_B(M):
                if True:
                    mix_reads = [MIXA[h][M] for h in range(8)] + [MIXB[h][M] for h in range(8)]
                    for i in range(4):
                        j = i % 2
                        row0 = (M * 4 + i) * 128
                        R.dma("sp", h1t[j], xo[row0:row0 + 128, :], writes=[H1T[j]])
                        tc_ = slice(M * 512 + i * 128, M * 512 + (i + 1) * 128)
                        for half in range(2):
                            bank = BX
                            yield from acq7("wo")
                            R.mm([(lambda e, f=f, bank=bank, half=half, tc_=tc_, i=i, M=M: e.matmul(
                                ps[bank][:, :], lhsT=(mixA3[:, f, tc_] if f < 4 else mixB3[:, f - 4, tc_] if f < 6
                                                      else mixBh3[M % 2][:, f - 6, i * 128:(i + 1) * 128]),
                                rhs=Wo3[:, f, half * 512:(half + 1) * 512], start=(f == 0), stop=(f == 7)))
                                for f in range(8)], reads=mix_reads + [WOB], writes=[PSB[bank]])
                            yield
                            R.op("dve", lambda e, j=j, bank=bank, half=half: e.tensor_tensor(
                                out=h1t[j][:, half * 512:(half + 1) * 512], in0=ps[bank][:, :],
                                in1=h1t[j][:, half * 512:(half + 1) * 512], op=ALU.add),
                                reads=[PSB[bank], H1T[j]], writes=[H1T[j]])
                            rel7()
                            yield
                        R.dma("sp", h1s[row0:row0 + 128, :], h1t[j], reads=[H1T[j]], writes=[H1S[M * 4 + i]])
                        yield

            steps = []
            n = 0
            for c in range(8):
                steps.append(dict(s1=(lambda c=c, n=n: s1_kv(c, n % 2)), s2=(lambda c=c, n=n: s2_kv(c, n % 2))))
                n += 1
                if c % 2 == 1:
                    M = c // 2
                    steps.append(dict(s1=(lambda M=M, n=n: s1_q(M, n % 2)), s2=(lambda M=M, n=n: s2_q(M, n % 2)),
                                      att=(lambda M=M: att_B(M)), post=((lambda M=M: wo_B(M)) if hh == 1 else None)))
                    n += 1
            def prefetch_w1():
                for f_ in ENG:
                    if f_ != "pool" and R.cnt[f_] > 0:
                        R._wait("pool", (("e", f_), R.cnt[f_]))
                for i_, v_ in enumerate(R.dval):
                    if v_ > 0:
                        R._wait("pool", (("d", i_), v_))
                W13_ = r3(PHC["W1"], 4096)
                for g in range(8):
                    R.dma("pool", W13_[:, :, g * 512:(g + 1) * 512],
                          w_ff1[:, g * 512:(g + 1) * 512].rearrange("(k p) c -> p k c", p=128), writes=[W1B[g]])
                W23_ = r3(PHC["W2"], 1024)
                for g in range(PHC["nw2"]):
                    R.dma("pool", W23_[:, g * 4:(g + 1) * 4, :],
                          w_ff2[g * 512:(g + 1) * 512, :].rearrange("(f p) c -> p f c", p=128), writes=[W2B[g]])
            pipeline(steps, before_tail=(prefetch_w1 if hh == 1 else None), side=[late_weights()])
            if hh == 0:
                dump_ap("KbT", KbT, KBB, [128, 4 * S], BF16)
                dump_ap("Vb", Vb, VBB, [128, 32 * 4 * 65], BF16)
                dump_ap("QbT3", QbT[1], QBB[1], [128, 2048], BF16)
            if hh == 1:
                dump_ap("mixB", mixB, [b for hb in MIXB for b in hb], [128, 2 * 2048], BF16)
            R.barrier()

        def phase_C():
            A.top = work_base - 2 * 4 * 2048
            h1b = [A.f32(1024) for _ in range(4)]
            H1B = [Buf(f"h1b{i}") for i in range(4)]
            outt = [A.f32(1024) for _ in range(2)]
            OUTT = [Buf(f"outt{i}") for i in range(2)]
            assert A.top <= workB_base
            A.top = workB_base
            W1 = A.b16(8 * 4096)
            W13 = r3(W1, 4096)
            W2 = A.b16(32 * 1024)
            W23 = r3(W2, 1024)
            hid = A.b16(32 * 512)
            hid3 = r3(hid, 512)
            HID = Buf("hid")
            XP = XPipe(own_bufs=False)
            hnT = A.b16(8 * 512)
            hnT3 = r3(hnT, 512)
            HNT = Buf("hnT")
            rl = [A.f32(512) for _ in range(2)]
            RL = [Buf(f"rl{i}") for i in range(2)]
            for g in range(PHC.get("nw2", 0), 8):
                R.dma("pool", W23[:, g * 4:(g + 1) * 4, :],
                      w_ff2[g * 512:(g + 1) * 512, :].rearrange("(f p) c -> p f c", p=128), writes=[W2B[g]])
            FB = (0, 1, 2, 3)
            for b_ in range(4):
                PSB[b_] = Buf(f"psc{b_}", excl=True)
            no = 0
            for cc in range(4):
                srcs = []
                for j in range(4):
                    row0 = (cc * 4 + j) * 128
                    R.dma("sp", h1b[j], h1s[row0:row0 + 128, :], reads=[H1S[cc * 4 + j]], writes=[H1B[j]])
                    srcs.append((h1b[j], H1B[j]))
                drain(XP.chunk(srcs, G_FFN, hnT3, HNT))
                for f in range(32):
                    bank = FB[f % 4]
                    drain(proj_fm(bank, 0, 128, W13, f * 128, 128, hnT3, 512, [W1B[f // 4], HNT], chunk=8))
                    jr = f % 2
                    R.op("act", lambda e, jr=jr, bank=bank: e.activation(out=rl[jr], in_=ps[bank][:, :], func=AF.Relu),
                         reads=[PSB[bank]], writes=[RL[jr]])
                    R.op("pool" if f % 2 else "dve",
                         lambda e, jr=jr, f=f: e.tensor_tensor(out=hid3[:, f, :], in0=rl[jr], in1=rl[jr], op=ALU.mult),
                         reads=[RL[jr]], writes=[HID])
                for j in range(4):
                    row0 = (cc * 4 + j) * 128
                    jo = no % 2
                    no += 1
                    for half in range(2):
                        bank = BP[half]
                        R.mm([(lambda e, f=f, j=j, bank=bank, half=half: e.matmul(
                            ps[bank][:, :], lhsT=hid3[:, f, j * 128:(j + 1) * 128],
                            rhs=W23[:, f, half * 512:(half + 1) * 512], start=(f == 0), stop=(f == 31)))
                            for f in range(32)], reads=[HID] + W2B, writes=[PSB[bank]])
                        R.op("dve", lambda e, jo=jo, j=j, bank=bank, half=half: e.tensor_tensor(
                            out=outt[jo][:, half * 512:(half + 1) * 512], in0=ps[bank][:, :],
                            in1=h1b[j][:, half * 512:(half + 1) * 512], op=ALU.add),
                            reads=[PSB[bank], H1B[j]], writes=[OUTT[jo]])
                    R.dma("sp", out_d[row0:row0 + 128, :], outt[jo], reads=[OUTT[jo]])

        stop_after = os.environ.get("MK_STOP", "")
        pass_A()
        if stop_after != "A":
            pass_B(0)
            if stop_after != "B1":
                pass_B(1)
                if stop_after != "B2":
                    phase_C()

        R.finish()
        with nc.Block() as block:
            @block.tensor
            def _(e):
                for f in R.ops["pe"]:
                    f(e)

            @block.scalar
            def _(e):
                for f in R.ops["act"]:
                    f(e)

            @block.vector
            def _(e):
                for f in R.ops["dve"]:
                    f(e)

            @block.gpsimd
            def _(e):
                for f in R.ops["pool"]:
                    f(e)

            @block.sync
            def _(e):
                for f in R.ops["sp"]:
                    f(e)
    print("arena peak bytes", A.peak, "instr counts", {e: len(R.ops[e]) for e in ENG}, flush=True)
    return nc, dump_out


def t5_bucket(dist):
    dist = np.asarray(dist, dtype=np.int64)
    max_exact = 16
    safe = np.maximum(dist, 1).astype(np.float32)
    large = max_exact + (np.log(safe / max_exact) / math.log(2048 / max_exact) * (32 - max_exact)).astype(np.int64)
    large = np.minimum(large, 31)
    return np.where(dist < max_exact, dist, large).astype(np.int32)


def static_consts(p):
    x = np.arange(LU)
    o = x - 255 + 128 * p
    valid = (o >= 0) & (o <= 2048)
    oc = np.clip(o, 0, 2048)
    mult = ((oc <= 128).astype(np.float32) + ((oc % 4 == 0) & (oc <= 512)).astype(np.float32)
            + ((oc % 16 == 0) & (oc <= 2048)).astype(np.float32))
    bucket = t5_bucket(oc)
    oh = np.zeros((32, LU), np.float32)
    oh[bucket, x] = mult * valid
    k = np.arange(128)[:, None]
    q = np.arange(128)[None, :]
    tri = (q >= k).astype(np.float32)
    if p == 0:
        mm1, m0 = np.zeros((128, 128), np.float32), tri
    else:
        mm1, m0 = tri, np.ones((128, 128), np.float32)
    return oh, mm1, m0


def pad_b(g):
    o = np.zeros(128, np.float32)
    o[0:16] = g[64:80]
    o[32:48] = g[80:96]
    o[64:128] = g[0:64]
    return o


def make_in_maps(inp):
    f32 = np.float32
    x = np.ascontiguousarray(inp["x"], dtype=f32)
    pos = np.ascontiguousarray(inp["positions"]).astype(np.int32)
    gcols = np.zeros((128, 32), f32)
    gcols[:, 0:8] = np.asarray(inp["norm_mix_g"], f32).reshape(8, 128).T
    gcols[:, 8:16] = np.asarray(inp["norm_ffn_g"], f32).reshape(8, 128).T
    gcols[:, 16:22] = np.asarray(inp["cq_norm_g"], f32).reshape(6, 128).T
    gcols[:, 22:24] = np.asarray(inp["ckv_norm_g"], f32).reshape(2, 128).T
    gcols[:, 24] = np.tile(np.asarray(inp["qnorm_a_g"], f32), 2)
    gcols[:, 25] = np.tile(np.asarray(inp["knorm_a_g"], f32), 2)
    gcols[:, 26] = pad_b(np.asarray(inp["qnorm_b_g"], f32))
    gcols[:, 27] = pad_b(np.asarray(inp["knorm_b_g"], f32))
    inv_freq = (1.0 / (10000.0 ** (np.arange(0, 32, 2, dtype=f32) / f32(32)))).astype(f32)
    gcols[0:16, 28] = inv_freq
    gcols[32:48, 28] = inv_freq
    ident = np.eye(128, dtype=f32)
    onesblk = np.zeros((128, 128), f32)
    onesblk[0:64, 0:64] = 1
    onesblk[64:128, 64:128] = 1
    shared = dict(
        w_in=np.ascontiguousarray(inp["w_in"], f32), w_uq=np.ascontiguousarray(inp["w_uq"], f32),
        w_ukv=np.ascontiguousarray(inp["w_ukv"], f32), w_o=np.ascontiguousarray(inp["w_o"], f32),
        w_ff1=np.ascontiguousarray(inp["w_ff1"], f32), w_ff2=np.ascontiguousarray(inp["w_ff2"], f32),
        gcols=gcols, rel_bias=np.ascontiguousarray(inp["rel_bias"], f32), ident=ident, onesblk=onesblk)
    maps = []
    for c in range(8):
        b, p = c // 2, c % 2
        own_blocks = [2 * j + p for j in range(16)]
        rows = np.concatenate([np.arange(q * 128, (q + 1) * 128) for q in own_blocks])
        oh, mm1, m0 = static_consts(p)
        m = dict(shared)
        pos_tm = np.concatenate([pos[b].reshape(32, 128).T, pos[b][rows].reshape(16, 128).T], axis=1)
        m.update(xf=x[b], xo=np.ascontiguousarray(x[b][rows]), posf=pos[b][None, :],
                 poso=np.ascontiguousarray(pos[b][rows])[None, :], onehot=oh, mask_m1=mm1, mask_0=m0,
                 pos_tm=np.ascontiguousarray(pos_tm.astype(np.int32)),
                 invf_row=np.ascontiguousarray(np.tile(inv_freq[None, :], (128, 1))))
        maps.append(m)
    return maps


_CACHE = {}


def kernel(**inputs):
    if "nc" not in _CACHE:
        _CACHE["nc"] = build_program()[0]
    nc = _CACHE["nc"]
    maps = make_in_maps(inputs)
    res = run_bass_kernel_spmd(nc, maps, core_ids=list(range(8)))
    out = np.zeros((4, S, D), np.float32)
    for c in range(8):
        b, p = c // 2, c % 2
        o = np.asarray(res.results[c]["out"]).reshape(16, 128, D)
        for j in range(16):
            q = 2 * j + p
            out[b, q * 128:(q + 1) * 128, :] = o[j]
    return out
```

```python
import os
import math
import numpy as np
import concourse.bass as bass
import concourse.mybir as mybir
from concourse.bass_utils import run_bass_kernel_spmd

F32 = mybir.dt.float32
BF16 = mybir.dt.bfloat16
I32 = mybir.dt.int32
AF = mybir.ActivationFunctionType
ALU = mybir.AluOpType

S = 4096
D = 1024
NOWN = 2048
EPS = 1e-6
LU = 2432
NTT = 2304
NTTP = 2432
ENG = ("pe", "act", "dve", "pool", "sp")
N_DSEM = 76
N_DSEM_SW = 32
ARENA_BYTES = 212800

PI = math.pi
C1 = 6.28125
C2 = 2.0 * math.pi - 6.28125


class Buf:
    __slots__ = ("name", "w", "r", "excl")

    def __init__(self, name, excl=False):
        self.name = name
        self.w = None
        self.r = {}
        self.excl = excl


class Rec:
    def __init__(self, esem, dsem):
        self.esem = esem
        self.dsem = dsem
        self.ops = {e: [] for e in ENG}
        self.cnt = {e: 0 for e in ENG}
        self.seen = {e: {} for e in ENG}
        self.dval = [0] * len(dsem)
        self.drr = {}

    def _sem(self, key):
        return self.esem[key[1]] if key[0] == "e" else self.dsem[key[1]]

    def _wait(self, eng, ev):
        if ev is None:
            return
        key, val = ev
        if key == ("e", "pe") and eng == "pe":
            return
        if self.seen[eng].get(key, 0) >= val:
            return
        self.seen[eng][key] = val
        sem = self._sem(key)
        self.ops[eng].append(lambda e, sem=sem, val=val: e.wait_ge(sem, val))

    @staticmethod
    def _split(reads, writes):
        ex = [b for b in reads if b.excl]
        if ex:
            reads = [b for b in reads if not b.excl]
            writes = list(writes) + [b for b in ex if b not in writes]
        return reads, writes

    @staticmethod
    def _deps(reads, writes):
        deps = []
        for b in reads:
            if b.w is not None:
                deps.append(b.w)
        for b in writes:
            if b.w is not None:
                deps.append(b.w)
            deps.extend(b.r.values())
        return deps

    @staticmethod
    def _commit(ev, reads, writes):
        for b in reads:
            b.r[ev[0]] = ev
        for b in writes:
            b.w = ev
            b.r = {}

    def op(self, eng, fn, reads=(), writes=()):
        reads, writes = self._split(reads, writes)
        for ev in self._deps(reads, writes):
            self._wait(eng, ev)
        self.cnt[eng] += 1
        ev = (("e", eng), self.cnt[eng])
        sem = self.esem[eng]
        self.ops[eng].append(lambda e, fn=fn, sem=sem: fn(e).then_inc(sem, 1))
        self._commit(ev, reads, writes)
        return ev

    def mm(self, fns, reads=(), writes=()):
        reads, writes = self._split(reads, writes)
        for ev in self._deps(reads, writes):
            self._wait("pe", ev)
        self.cnt["pe"] += 1
        ev = (("e", "pe"), self.cnt["pe"])
        sem = self.esem["pe"]
        for f in fns[:-1]:
            self.ops["pe"].append(lambda e, f=f: f(e))
        self.ops["pe"].append(lambda e, f=fns[-1], sem=sem: f(e).then_inc(sem, 1))
        self._commit(ev, reads, writes)
        return ev

    def dma(self, eng, out, in_, reads=(), writes=()):
        lo, hi = (0, N_DSEM_SW) if eng == "pool" else (N_DSEM_SW, len(self.dsem))
        i = self.drr.get(eng, lo)
        self.drr[eng] = lo + (i + 1 - lo) % (hi - lo)
        if self.dval[i] > 0:
            self._wait(eng, (("d", i), self.dval[i]))
        for ev in self._deps(reads, writes):
            self._wait(eng, ev)
        self.dval[i] += 16
        ev = (("d", i), self.dval[i])
        sem = self.dsem[i]
        self.ops[eng].append(
            lambda e, out=out, in_=in_, sem=sem: e.dma_start(out=out, in_=in_).then_inc(sem, 16))
        self._commit(ev, reads, writes)
        return ev

    def barrier(self):
        for e in ENG:
            for f in ENG:
                if f != e and self.cnt[f] > 0:
                    self._wait(e, (("e", f), self.cnt[f]))
            for i, v in enumerate(self.dval):
                if v > 0:
                    self._wait(e, (("d", i), v))

    def finish(self):
        for i, v in enumerate(self.dval):
            if v > 0:
                self._wait("sp", (("d", i), v))
        for f in ENG:
            if f != "sp" and self.cnt[f] > 0:
                self._wait("sp", (("e", f), self.cnt[f]))


class Arena:
    def __init__(self, ap, cap):
        self.ap = ap
        self.cap = cap
        self.top = 0
        self.peak = 0

    def alloc(self, nbytes):
        off = (self.top + 63) // 64 * 64
        self.top = off + nbytes
        self.peak = max(self.peak, self.top)
        assert self.top <= self.cap, f"arena overflow {self.top} > {self.cap}"
        return off

    def b16(self, n, parts=128):
        off = self.alloc(2 * n)
        return self.ap[0:parts, off // 2: off // 2 + n]

    def f32(self, n, parts=128, dt=F32):
        off = self.alloc(4 * n)
        return self.ap[0:parts, off // 2: off // 2 + 2 * n].bitcast(dt)


def r3(ap, b):
    return ap.rearrange("p (a b) -> p a b", b=b)


def build_program(dump=None):
    dump = dump or ()
    nc = bass.Bass("TRN2", target_bir_lowering=False)

    def din(name, shape, dt=F32):
        return nc.dram_tensor(name, list(shape), dt, kind="ExternalInput").ap()

    xf = din("xf", [S, D])
    xo = din("xo", [NOWN, D])
    posf = din("posf", [1, S], I32)
    poso = din("poso", [1, NOWN], I32)
    w_in = din("w_in", [D, 2592])
    w_uq = din("w_uq", [768, 768])
    w_ukv = din("w_ukv", [256, 1024])
    w_o = din("w_o", [D, D])
    w_ff1 = din("w_ff1", [D, 4096])
    w_ff2 = din("w_ff2", [4096, D])
    gcols_d = din("gcols", [128, 32])
    relb_d = din("rel_bias", [32, 8])
    ident_d = din("ident", [128, 128])
    onesblk_d = din("onesblk", [128, 128])
    oh_d = din("onehot", [32, LU])
    mm1_d = din("mask_m1", [128, 128])
    m0_d = din("mask_0", [128, 128])
    postm_d = din("pos_tm", [128, 48], I32)
    invf_d = din("invf_row", [128, 16])
    out_d = nc.dram_tensor("out", [NOWN, D], F32, kind="ExternalOutput").ap()
    h1s = nc.dram_tensor("h1s", [NOWN, D], F32, kind="Internal").ap()
    uscr = nc.dram_tensor("uscr", [8, 128, LU], BF16, kind="Internal").ap()
    xnTs = nc.dram_tensor("xnTs", [12, 128, 8 * 512], BF16, kind="Internal").ap()
    cqs = nc.dram_tensor("cqs", [4, 128, 6 * 512], BF16, kind="Internal").ap()
    ckvs = nc.dram_tensor("ckvs", [8, 128, 2 * 512], BF16, kind="Internal").ap()
    krgs = nc.dram_tensor("krgs", [8, 64, 512], F32, kind="Internal").ap()
    sqks = nc.dram_tensor("sqks", [8, 64, 512], BF16, kind="Internal").ap()
    trqs = nc.dram_tensor("trqs", [4, 2, 48, 512], F32, kind="Internal").ap()
    dump_out = {}

    import contextlib
    with contextlib.ExitStack() as es:
        arena_t = es.enter_context(nc.sbuf_tensor("arena", [128, ARENA_BYTES // 2], BF16))
        pairT = [es.enter_context(nc.psum_tensor(f"pp{i}", [128, 1024], F32)) for i in range(2)]
        psb = [es.enter_context(nc.psum_tensor(f"ps{i}", [128, 512], F32)) for i in range(4, 8)]
        esem = {e: es.enter_context(nc.semaphore(f"sem_{e}")) for e in ENG}
        dsem = [es.enter_context(nc.semaphore(f"dsem{i}")) for i in range(N_DSEM)]
        R = Rec(esem, dsem)
        A = Arena(arena_t, ARENA_BYTES)
        pairs = [p[:] for p in pairT]
        ps = [pairs[0][:, 0:512], pairs[0][:, 512:1024], pairs[1][:, 0:512], pairs[1][:, 512:1024]] + [p[:] for p in psb]
        SPB = [Buf("sp0", excl=True), Buf("sp1", excl=True)]
        PSB = [SPB[0], SPB[0], SPB[1], SPB[1]] + [Buf(f"ps{i}", excl=True) for i in range(4, 8)]

        def dump_ap(name, ap, reads, shape, dt=F32):
            if name not in dump:
                return
            t = nc.dram_tensor("dbg_" + name, list(shape), dt, kind="ExternalOutput").ap()
            dump_out[name] = t
            R.dma("sp", t, ap, reads=reads)

        ident = A.b16(128)
        ones_bf = A.b16(128)
        onesblk = A.b16(128)
        ones32 = A.f32(128)
        gcols = A.f32(32)
        mm1 = A.b16(128)
        m0 = A.b16(128)
        small = A.f32(16)
        invf_row = A.f32(16)
        CONST = Buf("const")
        G_MIX, G_FFN, G_CQ, G_CKV, G_QA, G_KA, G_QB, G_KB, INVF = 0, 8, 16, 22, 24, 25, 26, 27, 28

        R.dma("pool", ident, ident_d, writes=[CONST])
        R.dma("pool", onesblk, onesblk_d, writes=[CONST])
        R.dma("pool", mm1, mm1_d, writes=[CONST])
        R.dma("pool", m0, m0_d, writes=[CONST])
        R.dma("sp", gcols, gcols_d, writes=[CONST])
        R.dma("sp", invf_row, invf_d, writes=[CONST])
        R.op("dve", lambda e: e.memset(ones_bf, 1.0), writes=[CONST])
        R.op("dve", lambda e: e.memset(ones32, 0.0), writes=[CONST])
        R.op("dve", lambda e: e.memset(ones32[64:65, :], 1.0), writes=[CONST])

        XSD = [Buf(f"xnTs{i}") for i in range(12)]
        mixA = A.b16(4 * 2048)
        mixA3 = r3(mixA, 2048)
        MIXA = [[Buf(f"mixA{h}_{m}") for m in range(4)] for h in range(8)]
        work_base = A.top

        BO = 4
        BP = (5, 6)
        BT = 7
        BX = 7

        ATT_PER_ROUND = int(os.environ.get("MK_APR", "1"))

        LOCK7 = {"o": None}

        def acq7(name):
            while LOCK7["o"] not in (None, name):
                yield
            LOCK7["o"] = name

        def rel7():
            LOCK7["o"] = None

        def drain(g):
            for _ in g:
                pass

        def chain(a, b):
            if a is not None:
                yield from a
            yield from b

        def pipeline(steps, before_tail=None, side=None):
            from collections import deque
            queue = deque()
            posts = []

            def pop_head():
                _, pf = queue.popleft()
                for g in posts:
                    for _ in range(100000):
                        try:
                            next(g)
                        except StopIteration:
                            break
                    else:
                        raise RuntimeError("post generator cannot make progress")
                del posts[:]
                if pf is not None:
                    posts.append(pf())

            def advance(n):
                for _ in range(n):
                    while queue:
                        try:
                            next(queue[0][0])
                            break
                        except StopIteration:
                            pop_head()
                    if not queue:
                        break
                for g in list(posts):
                    try:
                        next(g)
                    except StopIteration:
                        posts.remove(g)

            drain(steps[0]["s1"]())
            side = list(side or [])
            for i, st in enumerate(steps):
                if st.get("att") is not None:
                    while len(queue) >= 2:
                        try:
                            next(queue[0][0])
                        except StopIteration:
                            pop_head()
                        for g in list(posts):
                            try:
                                next(g)
                            except StopIteration:
                                posts.remove(g)
                    for g in side:
                        drain(g)
                    side = []
                active = [st["s2"]()]
                if i + 1 < len(steps):
                    active.append(steps[i + 1]["s1"]())
                while active:
                    advance(ATT_PER_ROUND)
                    for g in list(active):
                        try:
                            next(g)
                        except StopIteration:
                            active.remove(g)
                    for g in list(side):
                        try:
                            next(g)
                        except StopIteration:
                            side.remove(g)
                if st.get("att") is not None and not os.environ.get("MK_NOATT"):
                    queue.append((st["att"](), st.get("post")))
            if before_tail is not None:
                before_tail()
            while queue or posts:
                advance(1)

        class XPipe:
            def __init__(self, own_bufs=True):
                if own_bufs:
                    self.xbuf = [A.f32(1024) for _ in range(2)]
                    self.XB = [Buf(f"xbuf{i}") for i in range(2)]
                self.xs = [A.b16(1024) for _ in range(2)]
                self.XS = [Buf(f"xs{i}") for i in range(2)]
                self.n = 0
                self.SM = [Buf(f"small{i}") for i in range(2)]

            def front(self, src_rows, sb=None):
                j = self.n % 2
                self.n += 1
                xsj, XSj, SMj = self.xs[j], self.XS[j], self.SM[j]
                if sb is None:
                    xb, XBj = self.xbuf[j], self.XB[j]
                    R.dma("sp", xb, src_rows, writes=[XBj])
                else:
                    xb, XBj = sb
                ss = small[:, 4 * j + 0: 4 * j + 1]
                lnv = small[:, 4 * j + 1: 4 * j + 2]
                rstd = small[:, 4 * j + 2: 4 * j + 3]
                R.op("act", lambda e: e.activation(out=xsj, in_=xb, func=AF.Square, accum_out=ss),
                     reads=[XBj], writes=[XSj, SMj])
                R.op("act", lambda e: e.activation(out=lnv, in_=ss, func=AF.Ln, scale=1.0 / 1024.0, bias=EPS),
                     reads=[SMj], writes=[SMj])
                R.op("act", lambda e: e.activation(out=rstd, in_=lnv, func=AF.Exp, scale=-0.5),
                     reads=[SMj], writes=[SMj])
                R.op("pool", lambda e: e.tensor_scalar(out=xsj, in0=xb, scalar1=rstd, scalar2=1.0, op0=ALU.mult,
                                                       op1=ALU.mult),
                     reads=[XBj, SMj], writes=[XSj])
                return xsj, XSj

            def back_pe(self, xsj, XSj):
                pst = ps[BT].bitcast(BF16)
                R.mm([(lambda e, k=k: e.transpose(pst[:, k * 128:(k + 1) * 128], xsj[:, k * 128:(k + 1) * 128], ident))
                      for k in range(8)], reads=[XSj, CONST], writes=[PSB[BT]])

            def back_evac(self, gofs, dst3, DST):
                pst = ps[BT].bitcast(BF16)
                g3 = gcols[:, gofs:gofs + 8].unsqueeze(2).to_broadcast([128, 8, 128])
                R.op("dve", lambda e: e.tensor_tensor(out=dst3, in0=r3(pst, 128), in1=g3, op=ALU.mult),
                     reads=[PSB[BT], CONST], writes=[DST])

            def chunk(self, srcs, gofs, xnT3, XNT):
                def fr(j):
                    if isinstance(srcs[j], tuple):
                        return self.front(None, sb=srcs[j])
                    return self.front(srcs[j])
                cur = fr(0)
                yield
                for j in range(4):
                    nxt = fr(j + 1) if j + 1 < 4 else None
                    yield
                    yield from acq7("s1")
                    self.back_pe(*cur)
                    yield
                    self.back_evac(gofs, xnT3[:, :, j * 128:(j + 1) * 128], XNT)
                    rel7()
                    yield
                    cur = nxt

        PROJ_CHUNK = int(os.environ.get("MK_PCH", "4"))

        def proj_fm(bank, m_lo, m_hi, W3, c0, ncol, xnT3, n, reads, chunk=None):
            nk = W3.shape[1]
            ch = chunk or PROJ_CHUNK
            o = ps[bank][m_lo:m_hi, 0:n]
            for k0 in range(0, nk, ch):
                R.mm([(lambda e, k=k: e.matmul(o, lhsT=W3[:, k, c0:c0 + ncol], rhs=xnT3[:, k, 0:n],
                                                start=(k == 0), stop=(k == nk - 1))) for k in range(k0, min(nk, k0 + ch))],
                     reads=reads, writes=[PSB[bank]])
                if k0 + ch < nk:
                    yield

        class NormBufs:
            def __init__(self):
                self.sq = [A.b16(512) for _ in range(2)]
                self.SQ = [Buf(f"sq{i}") for i in range(2)]
                self.lnv = A.f32(512)
                self.LNV = Buf("lnv")
                self.rstd = [A.f32(512) for _ in range(2)]
                self.RSTD = [Buf(f"rstd{i}") for i in range(2)]
                self.n = 0

            def next(self):
                j = self.n % 2
                self.n += 1
                return self.sq[j], self.SQ[j], self.rstd[j], self.RSTD[j]

        def ns_square(NB, slot, bank, sq_rows):
            sq, SQ, rstd, RS = slot
            lo, hi = sq_rows
            R.op("act", lambda e: e.activation(out=sq[lo:hi, :], in_=ps[bank][lo:hi, :], func=AF.Square),
                 reads=[PSB[bank]], writes=[SQ])

        def ns_stats_mm(NB, slot, st_rows, ones_lhsT, denom):
            sq, SQ, rstd, RS = slot
            slo, shi = st_rows
            st = ps[BX][slo:shi, :]
            R.mm([lambda e: e.matmul(st, lhsT=ones_lhsT, rhs=sq[slo:shi, :], start=True, stop=True)],
                 reads=[SQ, CONST], writes=[PSB[BX]])

        def ns_stats_act(NB, slot, st_rows, ones_lhsT, denom):
            sq, SQ, rstd, RS = slot
            slo, shi = st_rows
            st = ps[BX][slo:shi, :]
            R.op("act", lambda e: e.activation(out=NB.lnv[slo:shi, :], in_=st, func=AF.Ln, scale=1.0 / denom, bias=EPS),
                 reads=[PSB[BX]], writes=[NB.LNV])
            R.op("act", lambda e: e.activation(out=rstd[slo:shi, :], in_=NB.lnv[slo:shi, :], func=AF.Exp, scale=-0.5),
                 reads=[NB.LNV], writes=[RS])

        def normed_proj_gen(NB, n_items, proj_fn, sq_rows_fn, stats_args_fn, final_fn):
            slots = [None] * n_items
            banks = [BP[i % 2] for i in range(n_items)]

            def head(i):
                yield from proj_fn(i, banks[i])
                slots[i] = NB.next()
                ns_square(NB, slots[i], banks[i], sq_rows_fn(i))
            for i in range(min(2, n_items)):
                yield from head(i)
                yield
            for p0 in range(0, n_items, 2):
                pair = [i for i in (p0, p0 + 1) if i < n_items]
                for i in pair:
                    yield from acq7("s2")
                    ns_stats_mm(NB, slots[i], *stats_args_fn(i))
                    yield
                    ns_stats_act(NB, slots[i], *stats_args_fn(i))
                    rel7()
                    yield
                for i in pair:
                    final_fn(i, banks[i], slots[i][2], slots[i][3])
                    yield
                    if i + 2 < n_items:
                        yield from head(i + 2)
                        yield

        class AttnBufs:
            def __init__(self):
                self.pb = [A.b16(1024) for _ in range(3)]
                self.PB = [Buf(f"pb{i}") for i in range(3)]
                self.osb = [A.f32(512) for _ in range(2)]
                self.OSB = [Buf(f"osb{i}") for i in range(2)]
                for o_, OB_ in zip(self.osb, self.OSB):
                    R.op("pool", lambda e, o_=o_: e.memset(o_, 0.0), writes=[OB_])
                self.u = 0
                self.hn = 0

        def attention(AB, heads, scale):
            groups = []
            for h in heads:
                us = h["units"]
                k = 0
                while k < len(us):
                    if (not os.environ.get("MK_NOPAIR") and k + 1 < len(us) and us[k][4:6] == us[k + 1][4:6]
                            and (h.get("gmask") is None or us[k + 1][7] - us[k][7] == 128)):
                        groups.append((h, [k, k + 1]))
                        k += 2
                    else:
                        groups.append((h, [k]))
                        k += 1
            nG = len(groups)

            def issue_S(g):
                h, idxs = groups[g]
                pr = g % 2
                for slot, k in enumerate(idxs):
                    kT, KB, v, VB, i0, i1, masks, meta = h["units"][k]
                    o = pairs[pr][:, slot * 512 + i0 * 128:slot * 512 + i1 * 128]
                    q = h["qT"][:, i0 * 128:i1 * 128]
                    R.mm([lambda e, o=o, kT=kT, q=q: e.matmul(o, lhsT=kT, rhs=q, start=True, stop=True)],
                         reads=[KB, h["QB"]], writes=[SPB[pr]])

            def issue_PV(g):
                h, idxs = groups[g]
                i0, i1 = h["units"][idxs[0]][4:6]
                pb, PBj = pbs[g]
                nun = len(h["units"])
                for slot, k in enumerate(idxs):
                    v, VB = h["units"][k][2:4]
                    oo = ps[BO][0:65, i0 * 128:i1 * 128]
                    pv = pb[:, slot * 512 + i0 * 128:slot * 512 + i1 * 128]
                    R.mm([lambda e, oo=oo, v=v, pv=pv, k=k, nun=nun: e.matmul(oo, lhsT=v, rhs=pv, start=(k == 0),
                                                                              stop=(k == nun - 1))],
                         reads=[VB, PBj], writes=[PSB[BO]])
                return idxs[-1] == nun - 1

            def fin_a(h):
                oj = AB.hn % 2
                AB.hn += 1
                osb, OSBj = AB.osb[oj], AB.OSB[oj]
                R.op("act", lambda e, osb=osb: e.activation(out=osb[0:65, :], in_=ps[BO][0:65, :], func=AF.Copy),
                     reads=[PSB[BO]], writes=[OSBj])
                R.op("act", lambda e, osb=osb: e.activation(out=osb[64:65, :], in_=osb[64:65, :], func=AF.Ln),
                     reads=[OSBj], writes=[OSBj])
                R.op("act", lambda e, osb=osb: e.activation(out=osb[64:65, :], in_=osb[64:65, :], func=AF.Exp, scale=-1.0),
                     reads=[OSBj], writes=[OSBj])
                return (h, osb, OSBj)

            def fin_b(h, osb, OSBj):
                bc = ps[BX][0:64, :]
                R.mm([lambda e, osb=osb: e.matmul(ps[BX][:, :], lhsT=ones32, rhs=osb, start=True, stop=True)],
                     reads=[OSBj, CONST], writes=[PSB[BX]])
                dst = h["dst"]
                R.op("dve", lambda e, osb=osb, bc=bc, dst=dst: e.tensor_tensor(out=dst, in0=osb[0:64, :], in1=bc, op=ALU.mult),
                     reads=[OSBj, PSB[BX]], writes=[h["DST"]])

            pbs = {}
            pend_a = []
            pend_b = []
            issue_S(0)
            for g in range(nG + 1):
                if g + 1 < nG:
                    issue_S(g + 1)
                if pend_b:
                    while LOCK7["o"] not in (None, "att"):
                        yield
                    for item in pend_b:
                        fin_b(*item)
                    pend_b = []
                if pend_a:
                    pend_b = [fin_a(h_) for h_ in pend_a]
                    pend_a = []
                if g >= 1:
                    if issue_PV(g - 1):
                        pend_a.append(groups[g - 1][0])
                    del pbs[g - 1]
                if g < nG:
                    h, idxs = groups[g]
                    pr = g % 2
                    ns = len(idxs)
                    i0, i1 = h["units"][idxs[0]][4:6]
                    pj = AB.u % 3
                    AB.u += 1
                    pb, PBj = AB.pb[pj], AB.PB[pj]
                    pbs[g] = (pb, PBj)
                    src = r3(pairs[pr], 512)[:, 0:ns, i0 * 128:i1 * 128]
                    pv3 = r3(pb, 512)[:, 0:ns, i0 * 128:i1 * 128]
                    R.op("act", lambda e, src=src, pv3=pv3: e.activation(out=pv3, in_=src, func=AF.Exp, scale=scale),
                         reads=[SPB[pr]], writes=[PBj])
                    if h.get("gmask") is not None:
                        meng, mfn, mreads = h["gmask"]([h["units"][k][7] for k in idxs], i0, i1, pb)
                        R.op(meng, mfn, reads=list(mreads) + [PBj], writes=[PBj])
                    for slot, k in enumerate(idxs):
                        for (meng, mfn, mreads) in h["units"][k][6]:
                            pslot = pb[:, slot * 512:(slot + 1) * 512]
                            R.op(meng, (lambda e, mfn=mfn, pslot=pslot: mfn(e, pslot)), reads=list(mreads) + [PBj],
                                 writes=[PBj])
                yield
            for _ in range(2):
                if pend_b:
                    while LOCK7["o"] not in (None, "att"):
                        yield
                    for item in pend_b:
                        fin_b(*item)
                    pend_b = []
                if pend_a:
                    pend_b = [fin_a(h_) for h_ in pend_a]
                    pend_a = []
                yield

        def pass_A():
            A.top = work_base
            WA = A.b16(8 * 1536)
            WA3 = r3(WA, 1536)
            WAB = Buf("WA")
            kat_off = (A.top + 63) // 64 * 64
            KaT = A.b16(4 * S)
            KaT3 = r3(KaT, S)
            KAB = [Buf(f"KaT{c}") for c in range(8)]
            Va = A.b16(32 * 8 * 65)
            Va4 = Va.rearrange("p (k h d) -> p k h d", h=8, d=65)
            VAB = [Buf(f"Va{k}") for k in range(32)]
            TT = A.b16(8 * NTTP)
            TT3 = r3(TT, NTTP)
            TTB = [Buf(f"TT{h}") for h in range(8)]
            XP = XPipe()
            xnT = [A.b16(8 * 512) for _ in range(2)]
            xnT3 = [r3(t, 512) for t in xnT]
            XNT = [Buf(f"xnT{i}") for i in range(2)]
            QaT = [A.b16(8 * 512) for _ in range(2)]
            QaT3 = [r3(t, 512) for t in QaT]
            QAB = [[Buf(f"QaT{i}_{f}") for f in range(4)] for i in range(2)]

            NB = NormBufs()
            AB = AttnBufs()


            ALLMIX = [b for hb in MIXA for b in hb]
            mixa_off = work_base - 2 * 4 * 2048
            stg = Arena(arena_t, ARENA_BYTES)
            stg.top = mixa_off
            relb = stg.f32(8, parts=32)
            eb = stg.b16(8 * 128, parts=32)
            oh = stg.b16(LU, parts=32)
            ust = stg.b16(LU)
            assert stg.top <= work_base
            R.dma("sp", relb, relb_d, writes=ALLMIX)
            R.dma("pool", oh, oh_d, writes=ALLMIX)
            for g in (1, 2, 0):
                R.dma("pool", WA3[:, :, g * 512:(g + 1) * 512],
                      w_in[:, g * 512:(g + 1) * 512].rearrange("(k p) c -> p k c", p=128), writes=[WAB])
            R.op("pool", lambda e: e.memset(Va4[:, :, :, 64:65], 1.0), writes=VAB)
            for i_ in range(2):
                R.op("pool", lambda e, i_=i_: e.memset(QaT[i_], 0.0), writes=QAB[i_])
            R.op("act", lambda e: e.activation(out=relb, in_=relb, func=AF.Exp), reads=ALLMIX, writes=ALLMIX)
            R.op("dve", lambda e: e.tensor_copy(out=r3(eb, 128), in_=relb.unsqueeze(2).to_broadcast([32, 8, 128])),
                 reads=ALLMIX, writes=ALLMIX)
            USCR = [Buf(f"uscr{h}") for h in range(8)]

            def tt_gen():
                nb = 0
                for h in range(8):
                    for n in range(5):
                        w = min(512, LU - n * 512)
                        bank = nb % 4
                        nb += 1
                        R.mm([lambda e, h=h, n=n, w=w, bank=bank: e.matmul(ps[bank][:, 0:w], lhsT=eb[:, h * 128:(h + 1) * 128],
                                                                           rhs=oh[:, n * 512:n * 512 + w], start=True, stop=True)],
                             reads=ALLMIX, writes=[PSB[bank]])
                        yield
                        if n % 2 == 0:
                            R.op("dve", lambda e, n=n, w=w, bank=bank: e.tensor_copy(out=ust[:, n * 512:n * 512 + w],
                                                                                  in_=ps[bank][:, 0:w]),
                                 reads=[PSB[bank]], writes=ALLMIX)
                        else:
                            R.op("act", lambda e, n=n, w=w, bank=bank: e.activation(out=ust[:, n * 512:n * 512 + w],
                                                                                 in_=ps[bank][:, 0:w], func=AF.Copy),
                                 reads=[PSB[bank]], writes=ALLMIX)
                        yield
                    R.dma("sp", uscr[h], ust, reads=ALLMIX, writes=[USCR[h]])
                    src = bass.AP(uscr.tensor, h * 128 * LU + 127, [[LU - 1, 128], [1, NTT]])
                    R.dma("sp", TT3[:, h, 0:NTT], src, reads=[USCR[h]], writes=[TTB[h]])
                    yield

            def s1_kv(c, xb):
                yield from XP.chunk([xf[(c * 4 + j) * 128:(c * 4 + j + 1) * 128, :] for j in range(4)], G_MIX,
                                    xnT3[xb], XNT[xb])
                R.dma("sp", xnTs[c], xnT[xb], reads=[XNT[xb]], writes=[XSD[c]])

            def s1_q(M, xb):
                yield from XP.chunk([xo[(M * 4 + i) * 128:(M * 4 + i + 1) * 128, :] for i in range(4)], G_MIX,
                                    xnT3[xb], XNT[xb])
                R.dma("sp", xnTs[8 + M], xnT[xb], reads=[XNT[xb]], writes=[XSD[8 + M]])

            def s2_kv(c, xb):
                x3, XN = xnT3[xb], XNT[xb]

                def final(ft, bank, rstd, RS):
                    R.op("dve", lambda e: e.scalar_tensor_tensor(out=KaT3[:, ft, c * 512:(c + 1) * 512], in0=ps[bank][:, :],
                                                                 scalar=gcols[:, G_KA:G_KA + 1], in1=rstd,
                                                                 op0=ALU.mult, op1=ALU.mult),
                         reads=[PSB[bank], RS, CONST], writes=[KAB[c]])
                yield from normed_proj_gen(
                    NB, 4, lambda ft, bank: proj_fm(bank, 0, 128, WA3, 512 + ft * 128, 128, x3, 512, [WAB, XN]),
                    lambda ft: (0, 128), lambda ft: ((0, 128), onesblk, 64.0), final)
                for j in range(4):
                    bank = BP[j % 2]
                    kb = c * 4 + j
                    for k0 in range(0, 8, PROJ_CHUNK):
                        R.mm([(lambda e, k=k, j=j, bank=bank: e.matmul(ps[bank][:, :], lhsT=x3[:, k, j * 128:(j + 1) * 128],
                                                                      rhs=WA3[:, k, 1024:1536], start=(k == 0), stop=(k == 7)))
                              for k in range(k0, k0 + PROJ_CHUNK)], reads=[WAB, XN], writes=[PSB[bank]])
                        yield
                    R.op("dve", lambda e, kb=kb, bank=bank: e.tensor_copy(out=Va4[:, kb, :, 0:64],
                                                                       in_=ps[bank][:, :].rearrange("p (h d) -> p h d", d=64)),
                         reads=[PSB[bank]], writes=[VAB[kb]])
                    yield

            def s2_q(M, xb):
                x3, XN = xnT3[xb], XNT[xb]
                qb = M % 2

                def final(ft, bank, rstd, RS):
                    for hp in range(2):
                        rw = slice(hp * 64, hp * 64 + 64)
                        R.op("dve", lambda e, rw=rw, hp=hp: e.scalar_tensor_tensor(
                            out=QaT3[qb][rw, 2 * ft + hp, :], in0=ps[bank][rw, :], scalar=gcols[rw, G_QA:G_QA + 1],
                            in1=rstd[rw, :], op0=ALU.mult, op1=ALU.mult),
                            reads=[PSB[bank], RS, CONST], writes=[QAB[qb][ft]])
                yield from normed_proj_gen(
                    NB, 4, lambda ft, bank: proj_fm(bank, 0, 128, WA3, ft * 128, 128, x3, 512, [WAB, XN]),
                    lambda ft: (0, 128), lambda ft: ((0, 128), onesblk, 64.0), final)

            def att_A(M):
                qb = M % 2
                heads = []
                for h in range(8):
                    ft, bp = h // 2, (h % 2) * 64
                    units = []
                    kbs = list(range(8 * M + 7, max(0, 8 * M - 16) - 1, -1))
                    first = 8 * M + 1
                    kbs.remove(first)
                    kbs = [first] + kbs
                    for kb in kbs:
                        Dd = 8 * M - kb
                        i0 = max(0, -((1 + Dd) // 2))
                        i1 = min(4, (16 - Dd) // 2 + 1)
                        if i1 <= i0:
                            continue
                        col0 = (Dd + 2 * i0 + 1) * 128
                        ni = i1 - i0
                        units.append((KaT3[:, ft, kb * 128:(kb + 1) * 128], KAB[kb // 4],
                                      Va4[:, kb, h, :], VAB[kb], i0, i1, [], col0))

                    def gmask(metas, i0, i1, pb, h=h):
                        ns, ni = len(metas), i1 - i0
                        t0 = TT3[:, h, metas[0]:metas[0] + 128]
                        ttv = bass.AP(t0.tensor, t0.offset, [list(t0.ap[0]), [128, ns], [256, ni], [1, 128]])
                        p0 = pb[:, i0 * 128:i0 * 128 + 128]
                        pv4 = bass.AP(p0.tensor, p0.offset, [list(p0.ap[0]), [512, ns], [128, ni], [1, 128]])
                        return ("dve", (lambda e: e.tensor_tensor(out=pv4, in0=pv4, in1=ttv, op=ALU.mult)), [TTB[h]])
                    heads.append(dict(units=units, qT=QaT3[qb][:, h, :], QB=QAB[qb][ft], gmask=gmask,
                                      dst=mixA3[bp:bp + 64, ft, M * 512:(M + 1) * 512], DST=MIXA[h][M]))
                return attention(AB, heads, 0.125)

            steps = []
            n = 0
            for c in range(8):
                steps.append(dict(s1=(lambda c=c, n=n: s1_kv(c, n % 2)), s2=(lambda c=c, n=n: s2_kv(c, n % 2))))
                n += 1
                if c % 2 == 1:
                    M = c // 2
                    steps.append(dict(s1=(lambda M=M, n=n: s1_q(M, n % 2)), s2=(lambda M=M, n=n: s2_q(M, n % 2)),
                                      att=(lambda M=M: att_A(M))))
                    n += 1
            pipeline(steps, side=[tt_gen()])
            dump_ap("KaT", KaT, KAB, [128, 4 * S], BF16)
            dump_ap("Va", Va, VAB, [128, 32 * 8 * 65], BF16)
            dump_ap("TT", TT, TTB, [128, 8 * NTTP], BF16)
            dump_ap("mixA", mixA, [b for hb in MIXA for b in hb], [128, 4 * 2048], BF16)
            R.barrier()

        A.top = work_base
        mixB = A.b16(2 * 2048)
        mixB3 = r3(mixB, 2048)
        mixBh = [A.b16(2 * 512) for _ in range(2)]
        mixBh3 = [r3(t, 512) for t in mixBh]
        MIXB = [[Buf(f"mixB{h}_{m}") for m in range(4)] for h in range(8)]
        ident32 = A.f32(128)
        workB_base = A.top
        SCALE_B = 1.0 / math.sqrt(96.0)
        H1S = [Buf(f"h1s{i}") for i in range(16)]
        PHC = {}
        W1B = [Buf(f"W1_{g}") for g in range(8)]
        W2B = [Buf(f"W2_{g}") for g in range(8)]
        CQS = [Buf(f"cqs{i}") for i in range(4)]
        CKVS = [Buf(f"ckvs{i}") for i in range(8)]
        KRGS = [Buf(f"krgs{i}") for i in range(8)]
        SQKS = [Buf(f"sqks{i}") for i in range(8)]
        TRQS = [Buf(f"trqs{i}") for i in range(4)]

        def pass_B(hh):
            A.top = workB_base
            W1_ap = A.b16(8 * 4096)
            PHC["W1"] = W1_ap
            PHC["W2"] = A.b16(32 * 1024)
            A.top = workB_base
            Wc = A.b16(8 * 1152)
            Wc3 = r3(Wc, 1152)
            WCB = Buf("Wc")
            Wuq = A.b16(6 * 512)
            Wuq3 = r3(Wuq, 512)
            Wuq4 = Wuq.rearrange("p (k h c) -> p k h c", k=6, h=4, c=128)
            WUQ = Buf("Wuq")
            Wuk = A.b16(2 * 512)
            Wuk3 = r3(Wuk, 512)
            Wuv = A.b16(2 * 256)
            Wuv3 = r3(Wuv, 256)
            WUK = Buf("Wukv")
            xnT = [A.b16(8 * 512) for _ in range(2)]
            xnT3 = [r3(t, 512) for t in xnT]
            XNT = [Buf(f"xnT{i}") for i in range(2)]
            NB = NormBufs()
            cq = A.b16(6 * 512)
            cq3 = r3(cq, 512)
            CQ = Buf("cq")
            ckv = A.b16(2 * 512)
            ckv3 = r3(ckv, 512)
            CKV = Buf("ckv")
            rstd_c = A.f32(512)
            RSC = Buf("rstd_c")
            ssacc = A.f32(512)
            SSA = Buf("ssacc")
            qg = A.f32(512)
            QG = Buf("qg")
            krg = A.f32(512)
            KRG = Buf("krg")
            rtmp = A.f32(1024)
            rt3 = r3(rtmp, 512)
            RT = Buf("rtmp")
            cosT = [A.f32(512)] * 2
            sinT = [A.f32(512)] * 2
            TRIG = [Buf("trig")] * 2
            NBK = 48
            sin_tm = A.f32(NBK * 16)
            cos_tm = A.f32(NBK * 16)
            TM = Buf("trig_tm")
            assert A.top - workB_base >= 2 * 8 * 4096, "W1 must fit inside the dead-early region"
            PHC["nw2"] = max(0, min(8, (A.top - workB_base - 2 * 8 * 4096) // 8192)) if hh == 1 else 0
            QbT = [A.b16(4 * 512) for _ in range(2)]
            QbT3 = [r3(t, 512) for t in QbT]
            QBB = [[Buf(f"QbT{i}_{h}") for h in range(4)] for i in range(2)]
            if hh == 1:
                Wo = A.b16(8 * 1024)
                Wo3 = r3(Wo, 1024)
                WOB = Buf("Wo")
                h1t = [A.f32(1024) for _ in range(2)]
                H1T = [Buf(f"h1t{i}") for i in range(2)]
            KbT = A.b16(4 * S)
            KbT3 = r3(KbT, S)
            KBB = [Buf(f"KbT{c}") for c in range(8)]
            Vb = A.b16(32 * 4 * 65)
            Vb4 = Vb.rearrange("p (k h d) -> p k h d", h=4, d=65)
            VBB = [Buf(f"Vb{k}") for k in range(32)]
            AB = AttnBufs()

            if hh == 0:
                for g in range(2):
                    R.dma("pool", Wc3[:, :, g * 512:(g + 1) * 512],
                          w_in[:, 1536 + g * 512:1536 + (g + 1) * 512].rearrange("(k p) c -> p k c", p=128), writes=[WCB])
                R.op("pool", lambda e: e.memset(Wc3[:, :, 1024:1152], 0.0), writes=[WCB])
                R.dma("pool", Wc3[:, :, 1024:1040], w_in[:, 2560:2576].rearrange("(k p) c -> p k c", p=128), writes=[WCB])
                R.dma("pool", Wc3[:, :, 1056:1072], w_in[:, 2576:2592].rearrange("(k p) c -> p k c", p=128), writes=[WCB])
            skv = w_ukv[:, hh * 512:(hh + 1) * 512].rearrange("(k p) (h t d) -> p k h t d", p=128, t=2, d=64)
            Wuk4 = Wuk.rearrange("p (k h d) -> p k h d", k=2, h=4, d=128)
            Wuv4 = Wuv.rearrange("p (k h d) -> p k h d", k=2, h=4, d=64)
            R.op("pool", lambda e: e.memset(Wuk, 0.0), writes=[WUK])
            for k2 in range(2):
                R.dma("pool", Wuk4[:, k2, :, 64:128], skv[:, k2, :, 0, :], writes=[WUK])
                R.dma("pool", Wuv4[:, k2, :, :], skv[:, k2, :, 1, :], writes=[WUK])
            def late_weights():
                R.op("pool", lambda e: e.memset(Wuq, 0.0), writes=[WUQ])
                squ = w_uq[:, hh * 384:(hh + 1) * 384].rearrange("(k p) (h c) -> p k h c", p=128, c=96)
                for h4 in range(4):
                    R.dma("pool", Wuq4[:, :, h4, 0:16], squ[:, :, h4, 64:80], writes=[WUQ])
                    yield
                    R.dma("pool", Wuq4[:, :, h4, 32:48], squ[:, :, h4, 80:96], writes=[WUQ])
                    yield
                    R.dma("pool", Wuq4[:, :, h4, 64:128], squ[:, :, h4, 0:64], writes=[WUQ])
                    yield
                if hh == 1:
                    for g in range(2):
                        R.dma("pool", Wo3[:, :, g * 512:(g + 1) * 512],
                              w_o[:, g * 512:(g + 1) * 512].rearrange("(k p) c -> p k c", p=128), writes=[WOB])
                        yield
            R.op("pool", lambda e: e.memset(Vb4[:, :, :, 64:65], 1.0), writes=VBB)
            R.op("dve", lambda e: e.memset(rtmp, 0.0), writes=[RT])
            R.op("dve", lambda e: e.memset(cosT[0], 0.0), writes=[TRIG[0]])
            R.op("dve", lambda e: e.memset(sinT[0], 0.0), writes=[TRIG[0]])
            R.dma("sp", ident32, ident_d, writes=[CONST])

            PREP = [CQ, QG, KRG] + QBB[0] + QBB[1]
            n_el = NBK * 16
            tmp_pool = [cq[:, 0:1536].bitcast(F32), cq[:, 1536:3072].bitcast(F32),
                        QbT[0][:, 0:1536].bitcast(F32), QbT[1][:, 0:1536].bitcast(F32)]
            posi = tmp_pool[0].bitcast(I32)
            ang, nf, rr = tmp_pool[1], tmp_pool[2], tmp_pool[3]
            tt_ = tmp_pool[0]
            pti = qg[:, 0:NBK].bitcast(I32)
            ptf = krg[:, 0:NBK]
            if hh == 0:
                R.dma("sp", pti, postm_d, writes=PREP)
                R.op("dve", lambda e: e.tensor_copy(out=ptf, in_=pti), reads=PREP, writes=PREP)
                R.op("dve", lambda e: e.tensor_tensor(out=r3(ang, 16), in0=ptf.unsqueeze(2).to_broadcast([128, NBK, 16]),
                                                      in1=invf_row.unsqueeze(1).to_broadcast([128, NBK, 16]), op=ALU.mult),
                     reads=PREP + [CONST], writes=PREP)

            def reduce_and_sin(shift, dst):
                W = PREP
                R.op("dve", lambda e: e.tensor_scalar(out=posi, in0=ang, scalar1=1.0 / (2 * PI),
                                                      scalar2=0.5 + shift / (2 * PI), op0=ALU.mult, op1=ALU.add),
                     reads=W, writes=W)
                R.op("dve", lambda e: e.tensor_copy(out=nf, in_=posi), reads=W, writes=W)
                R.op("dve", lambda e: e.scalar_tensor_tensor(out=rr, in0=nf, scalar=-C1, in1=ang,
                                                             op0=ALU.mult, op1=ALU.add), reads=W, writes=W)
                R.op("dve", lambda e: e.scalar_tensor_tensor(out=rr, in0=nf, scalar=-C2, in1=rr,
                                                             op0=ALU.mult, op1=ALU.add), reads=W, writes=W)
                if shift != 0.0:
                    R.op("dve", lambda e: e.tensor_scalar(out=rr, in0=rr, scalar1=shift, scalar2=None,
                                                          op0=ALU.add), reads=W, writes=W)
                R.op("dve", lambda e: e.tensor_single_scalar(out=tt_, in_=rr, scalar=PI, op=ALU.is_gt),
                     reads=W, writes=W)
                R.op("dve", lambda e: e.scalar_tensor_tensor(out=rr, in0=tt_, scalar=-2 * PI, in1=rr,
                                                             op0=ALU.mult, op1=ALU.add), reads=W, writes=W)
                R.op("dve", lambda e: e.tensor_single_scalar(out=tt_, in_=rr, scalar=-PI, op=ALU.is_lt),
                     reads=W, writes=W)
                R.op("dve", lambda e: e.scalar_tensor_tensor(out=rr, in0=tt_, scalar=2 * PI, in1=rr,
                                                             op0=ALU.mult, op1=ALU.add), reads=W, writes=W)
                R.op("dve", lambda e: e.tensor_scalar(out=rr, in0=rr, scalar1=-PI, scalar2=PI,
                                                      op0=ALU.max, op1=ALU.min), reads=W, writes=W)
                R.op("act", lambda e: e.activation(out=dst, in_=rr, func=AF.Sin), reads=W, writes=[TM])
            if hh == 0:
                reduce_and_sin(0.0, sin_tm)
                reduce_and_sin(PI / 2, cos_tm)
            sin3 = r3(sin_tm, 16)
            cos3 = r3(cos_tm, 16)

            def rope_tables(blk0, ti):
                for tbl3, bank, dstT in ((cos3, BT, cosT[ti]), (sin3, BX, sinT[ti])):
                    yield from acq7("s2")
                    R.mm([(lambda e, j=j, tbl3=tbl3, bank=bank: e.transpose(ps[bank][0:16, j * 128:(j + 1) * 128],
                                                                            tbl3[:, blk0 + j, :], ident32))
                          for j in range(4)], reads=[TM, CONST], writes=[PSB[bank]])
                    yield
                    R.op("dve", lambda e, bank=bank, dstT=dstT: e.tensor_copy(out=dstT[0:16, :], in_=ps[bank][0:16, :]),
                         reads=[PSB[bank]], writes=[TRIG[ti]])
                    sgn = -1.0 if tbl3 is sin3 else 1.0
                    R.op("dve", lambda e, bank=bank, dstT=dstT, sgn=sgn: e.tensor_scalar(
                        out=dstT[32:48, :], in0=ps[bank][0:16, :], scalar1=sgn, scalar2=None, op0=ALU.mult),
                        reads=[PSB[bank]], writes=[TRIG[ti]])
                    rel7()
                    yield

            def rope(t, TB, ti):
                a, b, ab = slice(0, 16), slice(32, 48), slice(0, 48)
                cT, sT, TG = cosT[ti], sinT[ti], TRIG[ti]
                R.op("dve", lambda e: e.tensor_tensor(out=rt3[ab, 0, :], in0=t[ab, :], in1=cT[ab, :], op=ALU.mult),
                     reads=[TB, TG], writes=[RT])
                R.op("dve", lambda e: e.tensor_tensor(out=rt3[a, 1, :], in0=t[b, :], in1=sT[b, :], op=ALU.mult),
                     reads=[TB, TG], writes=[RT])
                R.op("dve", lambda e: e.tensor_tensor(out=rt3[b, 1, :], in0=t[a, :], in1=sT[a, :], op=ALU.mult),
                     reads=[TB, TG], writes=[RT])
                R.op("dve", lambda e: e.tensor_tensor(out=t[ab, :], in0=rt3[ab, 0, :], in1=rt3[ab, 1, :], op=ALU.add),
                     reads=[RT], writes=[TB])

            def c_norm(ntile, col0, gofs, st3, STB, x3, XN):
                for t in range(ntile):
                    bank = BP[t % 2]
                    yield from proj_fm(bank, 0, 128, Wc3, col0 + t * 128, 128, x3, 512, [WCB, XN])
                    sq, SQ, _, _ = NB.next()
                    R.op("act", lambda e, sq=sq, bank=bank: e.activation(out=sq, in_=ps[bank][:, :], func=AF.Square),
                         reads=[PSB[bank]], writes=[SQ])
                    R.op("dve", lambda e, t=t, bank=bank: e.tensor_scalar(out=st3[:, t, :], in0=ps[bank][:, :],
                                                                        scalar1=gcols[:, gofs + t:gofs + t + 1],
                                                                        scalar2=None, op0=ALU.mult),
                         reads=[PSB[bank], CONST], writes=[STB])
                    yield
                    yield from acq7("s2")
                    R.mm([lambda e, sq=sq: e.matmul(ps[BX][:, :], lhsT=ones_bf, rhs=sq, start=True, stop=True)],
                         reads=[SQ, CONST], writes=[PSB[BX]])
                    yield
                    if t == 0:
                        R.op("dve", lambda e: e.tensor_copy(out=ssacc, in_=ps[BX][:, :]), reads=[PSB[BX]], writes=[SSA])
                    else:
                        R.op("dve", lambda e: e.tensor_tensor(out=ssacc, in0=ps[BX][:, :], in1=ssacc, op=ALU.add),
                             reads=[PSB[BX], SSA], writes=[SSA])
                    rel7()
                    yield
                R.op("act", lambda e: e.activation(out=NB.lnv, in_=ssacc, func=AF.Ln, scale=1.0 / (ntile * 128.0), bias=EPS),
                     reads=[SSA], writes=[NB.LNV])
                R.op("act", lambda e: e.activation(out=rstd_c, in_=NB.lnv, func=AF.Exp, scale=-0.5),
                     reads=[NB.LNV], writes=[RSC])
                yield
                for t in range(ntile):
                    R.op("pool" if t % 2 else "dve",
                         lambda e, t=t: e.tensor_tensor(out=st3[:, t, :], in0=st3[:, t, :], in1=rstd_c, op=ALU.mult),
                         reads=[RSC, STB], writes=[STB])
                    if t % 2 == 1:
                        yield

            def s1_kv(c, xb):
                if hh == 0:
                    R.dma("sp", xnT[xb], xnTs[c], reads=[XSD[c]], writes=[XNT[xb]])
                yield

            def s1_q(M, xb):
                if hh == 0:
                    R.dma("sp", xnT[xb], xnTs[8 + M], reads=[XSD[8 + M]], writes=[XNT[xb]])
                else:
                    R.dma("sp", cosT[1][0:48, :], trqs[M, 0], reads=[TRQS[M]], writes=[TRIG[1]])
                    R.dma("sp", sinT[1][0:48, :], trqs[M, 1], reads=[TRQS[M]], writes=[TRIG[1]])
                    R.dma("sp", cq, cqs[M], reads=[CQS[M]], writes=[CQ])
                yield

            def s2_kv(c, xb):
                x3, XN = xnT3[xb], XNT[xb]
                cols = slice(c * 512, (c + 1) * 512)
                if hh == 0:
                    yield from c_norm(2, 768, G_CKV, ckv3, CKV, x3, XN)
                    R.dma("sp", ckvs[c], ckv, reads=[CKV], writes=[CKVS[c]])
                    yield from rope_tables(4 * c, 0)
                    bank = BP[0]
                    yield from proj_fm(bank, 0, 128, Wc3, 1024, 128, x3, 512, [WCB, XN])
                    for j in range(2):
                        R.op("act", lambda e, j=j, bank=bank: e.activation(out=NB.sq[j][0:64, :], in_=ps[bank][0:64, :], func=AF.Square),
                             reads=[PSB[bank]], writes=[NB.SQ[j]])
                    R.dma("sp", sqks[c], NB.sq[0][0:64, :], reads=[NB.SQ[0]], writes=[SQKS[c]])
                    R.op("dve", lambda e, bank=bank: e.tensor_scalar(out=krg[0:64, :], in0=ps[bank][0:64, :],
                                                                   scalar1=gcols[0:64, G_KB:G_KB + 1], scalar2=None, op0=ALU.mult),
                         reads=[PSB[bank], CONST], writes=[KRG])
                    yield
                    rope(krg, KRG, 0)
                    R.dma("sp", krgs[c], krg[0:64, :], reads=[KRG], writes=[KRGS[c]])
                    yield
                else:
                    R.dma("sp", ckv, ckvs[c], reads=[CKVS[c]], writes=[CKV])
                    R.dma("sp", krg[0:64, :], krgs[c], reads=[KRGS[c]], writes=[KRG])
                    for j in range(2):
                        R.dma("sp", NB.sq[j][0:64, :], sqks[c], reads=[SQKS[c]], writes=[NB.SQ[j]])
                    yield

                def final(hl, bank, rstd, RS):
                    R.op("dve", lambda e: e.scalar_tensor_tensor(
                        out=KbT3[64:128, hl, cols], in0=ps[bank][64:128, :], scalar=gcols[64:128, G_KB:G_KB + 1],
                        in1=rstd[64:128, :], op0=ALU.mult, op1=ALU.mult),
                        reads=[PSB[bank], RS, CONST], writes=[KBB[c]])
                    R.op("pool", lambda e: e.tensor_tensor(out=KbT3[0:64, hl, cols], in0=krg[0:64, :],
                                                           in1=rstd[0:64, :], op=ALU.mult),
                         reads=[KRG, RS], writes=[KBB[c]])
                yield from normed_proj_gen(
                    NB, 4, lambda hl, bank: proj_fm(bank, 0, 128, Wuk3, hl * 128, 128, ckv3, 512, [WUK, CKV]),
                    lambda hl: (64, 128), lambda hl: ((0, 128), ones_bf, 96.0), final)
                for j in range(4):
                    bank = BP[j % 2]
                    kb = c * 4 + j
                    R.mm([(lambda e, k=k, j=j, bank=bank: e.matmul(ps[bank][:, 0:256], lhsT=ckv3[:, k, j * 128:(j + 1) * 128],
                                                                  rhs=Wuv3[:, k, :], start=(k == 0), stop=(k == 1)))
                          for k in range(2)], reads=[WUK, CKV], writes=[PSB[bank]])
                    R.op("dve", lambda e, kb=kb, bank=bank: e.tensor_copy(out=Vb4[:, kb, :, 0:64],
                                                                       in_=ps[bank][:, 0:256].rearrange("p (h d) -> p h d", d=64)),
                         reads=[PSB[bank]], writes=[VBB[kb]])
                    yield

            def s2_q(M, xb):
                x3, XN = xnT3[xb], XNT[xb]
                qb = M % 2
                if hh == 0:
                    yield from rope_tables(32 + 4 * M, 1)
                    R.dma("sp", trqs[M, 0], cosT[1][0:48, :], reads=[TRIG[1]], writes=[TRQS[M]])
                    R.dma("sp", trqs[M, 1], sinT[1][0:48, :], reads=[TRIG[1]], writes=[TRQS[M]])
                    yield from c_norm(6, 0, G_CQ, cq3, CQ, x3, XN)
                    R.dma("sp", cqs[M], cq, reads=[CQ], writes=[CQS[M]])
                else:
                    yield

                def final(hl, bank, rstd, RS):
                    R.op("act", lambda e: e.activation(out=qg, in_=ps[bank][:, :], func=AF.Copy,
                                                       scale=gcols[:, G_QB:G_QB + 1]),
                         reads=[PSB[bank], CONST], writes=[QG])
                    rope(qg, QG, 1)
                    R.op("dve", lambda e: e.tensor_tensor(out=QbT3[qb][:, hl, :], in0=qg, in1=rstd, op=ALU.mult),
                         reads=[QG, RS], writes=[QBB[qb][hl]])
                yield from normed_proj_gen(
                    NB, 4, lambda hl, bank: proj_fm(bank, 0, 128, Wuq3, hl * 128, 128, cq3, 512, [WUQ, CQ]),
                    lambda hl: (0, 128), lambda hl: ((0, 128), ones_bf, 96.0), final)

            def att_B(M):
                qb = M % 2
                heads = []
                for hl in range(4):
                    hb = 4 * hh + hl
                    ft, bp = hb // 2, (hb % 2) * 64
                    units = []
                    for kb in range(0, 8 * M + 8):
                        Dd = 8 * M - kb
                        i0 = max(0, -((1 + Dd) // 2))
                        masks = []
                        for i in range(i0, 4):
                            dl = Dd + 2 * i
                            if dl in (-1, 0):
                                mt = mm1 if dl == -1 else m0

                                def mfn(e, pslot, mt=mt, i=i):
                                    pv = pslot[:, i * 128:(i + 1) * 128]
                                    return e.tensor_tensor(out=pv, in0=pv, in1=mt, op=ALU.mult)
                                masks.append(("pool", mfn, [CONST]))
                        units.append((KbT3[:, hl, kb * 128:(kb + 1) * 128], KBB[kb // 4],
                                      Vb4[:, kb, hl, :], VBB[kb], i0, 4, masks, 0))
                    dst_ = (mixB3[bp:bp + 64, ft, M * 512:(M + 1) * 512] if hh == 0
                            else mixBh3[M % 2][bp:bp + 64, ft - 2, :])
                    heads.append(dict(units=units, qT=QbT3[qb][:, hl, :], QB=QBB[qb][hl],
                                      dst=dst_, DST=MIXB[hb][M]))
                yield from attention(AB, heads, SCALE_B)

            def wo_B(M):
                if True:
                    mix_reads = [MIXA[h][M] for h in range(8)] + [MIXB[h][M] for h in range(8)]
                    for i in range(4):
                        j = i % 2
                        row0 = (M * 4 + i) * 128
                        R.dma("sp", h1t[j], xo[row0:row0 + 128, :], writes=[H1T[j]])
                        tc_ = slice(M * 512 + i * 128, M * 512 + (i + 1) * 128)
                        for half in range(2):
                            bank = BX
                            yield from acq7("wo")
                            R.mm([(lambda e, f=f, bank=bank, half=half, tc_=tc_, i=i, M=M: e.matmul(
                                ps[bank][:, :], lhsT=(mixA3[:, f, tc_] if f < 4 else mixB3[:, f - 4, tc_] if f < 6
                                                      else mixBh3[M % 2][:, f - 6, i * 128:(i + 1) * 128]),
                                rhs=Wo3[:, f, half * 512:(half + 1) * 512], start=(f == 0), stop=(f == 7)))
                                for f in range(8)], reads=mix_reads + [WOB], writes=[PSB[bank]])
                            yield
                            R.op("dve", lambda e, j=j, bank=bank, half=half: e.tensor_tensor(
                                out=h1t[j][:, half * 512:(half + 1) * 512], in0=ps[bank][:, :],
                                in1=h1t[j][:, half * 512:(half + 1) * 512], op=ALU.add),
                                reads=[PSB[bank], H1T[j]], writes=[H1T[j]])
                            rel7()
                            yield
                        R.dma("sp", h1s[row0:row0 + 128, :], h1t[j], reads=[H1T[j]], writes=[H1S[M * 4 + i]])
                        yield

            steps = []
            n = 0
            for c in range(8):
                steps.append(dict(s1=(lambda c=c, n=n: s1_kv(c, n % 2)), s2=(lambda c=c, n=n: s2_kv(c, n % 2))))
                n += 1
                if c % 2 == 1:
                    M = c // 2
                    steps.append(dict(s1=(lambda M=M, n=n: s1_q(M, n % 2)), s2=(lambda M=M, n=n: s2_q(M, n % 2)),
                                      att=(lambda M=M: att_B(M)), post=((lambda M=M: wo_B(M)) if hh == 1 else None)))
                    n += 1
            def prefetch_w1():
                for f_ in ENG:
                    if f_ != "pool" and R.cnt[f_] > 0:
                        R._wait("pool", (("e", f_), R.cnt[f_]))
                for i_, v_ in enumerate(R.dval):
                    if v_ > 0:
                        R._wait("pool", (("d", i_), v_))
                W13_ = r3(PHC["W1"], 4096)
                for g in range(8):
                    R.dma("pool", W13_[:, :, g * 512:(g + 1) * 512],
                          w_ff1[:, g * 512:(g + 1) * 512].rearrange("(k p) c -> p k c", p=128), writes=[W1B[g]])
                W23_ = r3(PHC["W2"], 1024)
                for g in range(PHC["nw2"]):
                    R.dma("pool", W23_[:, g * 4:(g + 1) * 4, :],
                          w_ff2[g * 512:(g + 1) * 512, :].rearrange("(f p) c -> p f c", p=128), writes=[W2B[g]])
            pipeline(steps, before_tail=(prefetch_w1 if hh == 1 else None), side=[late_weights()])
            if hh == 0:
                dump_ap("KbT", KbT, KBB, [128, 4 * S], BF16)
                dump_ap("Vb", Vb, VBB, [128, 32 * 4 * 65], BF16)
                dump_ap("QbT3", QbT[1], QBB[1], [128, 2048], BF16)
            if hh == 1:
                dump_ap("mixB", mixB, [b for hb in MIXB for b in hb], [128, 2 * 2048], BF16)
            R.barrier()

        def phase_C():
            A.top = work_base - 2 * 4 * 2048
            h1b = [A.f32(1024) for _ in range(4)]
            H1B = [Buf(f"h1b{i}") for i in range(4)]
            outt = [A.f32(1024) for _ in range(2)]
            OUTT = [Buf(f"outt{i}") for i in range(2)]
            assert A.top <= workB_base
            A.top = workB_base
            W1 = A.b16(8 * 4096)
            W13 = r3(W1, 4096)
            W2 = A.b16(32 * 1024)
            W23 = r3(W2, 1024)
            hid = A.b16(32 * 512)
            hid3 = r3(hid, 512)
            HID = Buf("hid")
            XP = XPipe(own_bufs=False)
            hnT = A.b16(8 * 512)
            hnT3 = r3(hnT, 512)
            HNT = Buf("hnT")
            rl = [A.f32(512) for _ in range(2)]
            RL = [Buf(f"rl{i}") for i in range(2)]
            for g in range(PHC.get("nw2", 0), 8):
                R.dma("pool", W23[:, g * 4:(g + 1) * 4, :],
                      w_ff2[g * 512:(g + 1) * 512, :].rearrange("(f p) c -> p f c", p=128), writes=[W2B[g]])
            FB = (0, 1, 2, 3)
            for b_ in range(4):
                PSB[b_] = Buf(f"psc{b_}", excl=True)
            no = 0
            for cc in range(4):
                srcs = []
                for j in range(4):
                    row0 = (cc * 4 + j) * 128
                    R.dma("sp", h1b[j], h1s[row0:row0 + 128, :], reads=[H1S[cc * 4 + j]], writes=[H1B[j]])
                    srcs.append((h1b[j], H1B[j]))
                drain(XP.chunk(srcs, G_FFN, hnT3, HNT))
                for f in range(32):
                    bank = FB[f % 4]
                    drain(proj_fm(bank, 0, 128, W13, f * 128, 128, hnT3, 512, [W1B[f // 4], HNT], chunk=8))
                    jr = f % 2
                    R.op("act", lambda e, jr=jr, bank=bank: e.activation(out=rl[jr], in_=ps[bank][:, :], func=AF.Relu),
                         reads=[PSB[bank]], writes=[RL[jr]])
                    R.op("pool" if f % 2 else "dve",
                         lambda e, jr=jr, f=f: e.tensor_tensor(out=hid3[:, f, :], in0=rl[jr], in1=rl[jr], op=ALU.mult),
                         reads=[RL[jr]], writes=[HID])
                for j in range(4):
                    row0 = (cc * 4 + j) * 128
                    jo = no % 2
                    no += 1
                    for half in range(2):
                        bank = BP[half]
                        for g8 in range(8):
                            R.mm([(lambda e, f=f, j=j, bank=bank, half=half: e.matmul(
                                ps[bank][:, :], lhsT=hid3[:, f, j * 128:(j + 1) * 128],
                                rhs=W23[:, f, half * 512:(half + 1) * 512], start=(f == 0), stop=(f == 31)))
                                for f in range(4 * g8, 4 * g8 + 4)], reads=[HID, W2B[g8]], writes=[PSB[bank]])
                        R.op("dve", lambda e, jo=jo, j=j, bank=bank, half=half: e.tensor_tensor(
                            out=outt[jo][:, half * 512:(half + 1) * 512], in0=ps[bank][:, :],
                            in1=h1b[j][:, half * 512:(half + 1) * 512], op=ALU.add),
                            reads=[PSB[bank], H1B[j]], writes=[OUTT[jo]])
                    R.dma("sp", out_d[row0:row0 + 128, :], outt[jo], reads=[OUTT[jo]])

        stop_after = os.environ.get("MK_STOP", "")
        pass_A()
        if stop_after != "A":
            pass_B(0)
            if stop_after != "B1":
                pass_B(1)
                if stop_after != "B2":
                    phase_C()

        R.finish()
        with nc.Block() as block:
            @block.tensor
            def _(e):
                for f in R.ops["pe"]:
                    f(e)

            @block.scalar
            def _(e):
                for f in R.ops["act"]:
                    f(e)

            @block.vector
            def _(e):
                for f in R.ops["dve"]:
                    f(e)

            @block.gpsimd
            def _(e):
                for f in R.ops["pool"]:
                    f(e)

            @block.sync
            def _(e):
                for f in R.ops["sp"]:
                    f(e)
    print("arena peak bytes", A.peak, "instr counts", {e: len(R.ops[e]) for e in ENG}, flush=True)
    return nc, dump_out


def t5_bucket(dist):
    dist = np.asarray(dist, dtype=np.int64)
    max_exact = 16
    safe = np.maximum(dist, 1).astype(np.float32)
    large = max_exact + (np.log(safe / max_exact) / math.log(2048 / max_exact) * (32 - max_exact)).astype(np.int64)
    large = np.minimum(large, 31)
    return np.where(dist < max_exact, dist, large).astype(np.int32)


def static_consts(p):
    x = np.arange(LU)
    o = x - 255 + 128 * p
    valid = (o >= 0) & (o <= 2048)
    oc = np.clip(o, 0, 2048)
    mult = ((oc <= 128).astype(np.float32) + ((oc % 4 == 0) & (oc <= 512)).astype(np.float32)
            + ((oc % 16 == 0) & (oc <= 2048)).astype(np.float32))
    bucket = t5_bucket(oc)
    oh = np.zeros((32, LU), np.float32)
    oh[bucket, x] = mult * valid
    k = np.arange(128)[:, None]
    q = np.arange(128)[None, :]
    tri = (q >= k).astype(np.float32)
    if p == 0:
        mm1, m0 = np.zeros((128, 128), np.float32), tri
    else:
        mm1, m0 = tri, np.ones((128, 128), np.float32)
    return oh, mm1, m0


def pad_b(g):
    o = np.zeros(128, np.float32)
    o[0:16] = g[64:80]
    o[32:48] = g[80:96]
    o[64:128] = g[0:64]
    return o


def make_in_maps(inp):
    f32 = np.float32
    x = np.ascontiguousarray(inp["x"], dtype=f32)
    pos = np.ascontiguousarray(inp["positions"]).astype(np.int32)
    gcols = np.zeros((128, 32), f32)
    gcols[:, 0:8] = np.asarray(inp["norm_mix_g"], f32).reshape(8, 128).T
    gcols[:, 8:16] = np.asarray(inp["norm_ffn_g"], f32).reshape(8, 128).T
    gcols[:, 16:22] = np.asarray(inp["cq_norm_g"], f32).reshape(6, 128).T
    gcols[:, 22:24] = np.asarray(inp["ckv_norm_g"], f32).reshape(2, 128).T
    gcols[:, 24] = np.tile(np.asarray(inp["qnorm_a_g"], f32), 2)
    gcols[:, 25] = np.tile(np.asarray(inp["knorm_a_g"], f32), 2)
    gcols[:, 26] = pad_b(np.asarray(inp["qnorm_b_g"], f32))
    gcols[:, 27] = pad_b(np.asarray(inp["knorm_b_g"], f32))
    inv_freq = (1.0 / (10000.0 ** (np.arange(0, 32, 2, dtype=f32) / f32(32)))).astype(f32)
    gcols[0:16, 28] = inv_freq
    gcols[32:48, 28] = inv_freq
    ident = np.eye(128, dtype=f32)
    onesblk = np.zeros((128, 128), f32)
    onesblk[0:64, 0:64] = 1
    onesblk[64:128, 64:128] = 1
    shared = dict(
        w_in=np.ascontiguousarray(inp["w_in"], f32), w_uq=np.ascontiguousarray(inp["w_uq"], f32),
        w_ukv=np.ascontiguousarray(inp["w_ukv"], f32), w_o=np.ascontiguousarray(inp["w_o"], f32),
        w_ff1=np.ascontiguousarray(inp["w_ff1"], f32), w_ff2=np.ascontiguousarray(inp["w_ff2"], f32),
        gcols=gcols, rel_bias=np.ascontiguousarray(inp["rel_bias"], f32), ident=ident, onesblk=onesblk)
    maps = []
    for c in range(8):
        b, p = c // 2, c % 2
        own_blocks = [2 * j + p for j in range(16)]
        rows = np.concatenate([np.arange(q * 128, (q + 1) * 128) for q in own_blocks])
        oh, mm1, m0 = static_consts(p)
        m = dict(shared)
        pos_tm = np.concatenate([pos[b].reshape(32, 128).T, pos[b][rows].reshape(16, 128).T], axis=1)
        m.update(xf=x[b], xo=np.ascontiguousarray(x[b][rows]), posf=pos[b][None, :],
                 poso=np.ascontiguousarray(pos[b][rows])[None, :], onehot=oh, mask_m1=mm1, mask_0=m0,
                 pos_tm=np.ascontiguousarray(pos_tm.astype(np.int32)),
                 invf_row=np.ascontiguousarray(np.tile(inv_freq[None, :], (128, 1))))
        maps.append(m)
    return maps


_CACHE = {}


def kernel(**inputs):
    if "nc" not in _CACHE:
        _CACHE["nc"] = build_program()[0]
    nc = _CACHE["nc"]
    maps = make_in_maps(inputs)
    res = run_bass_kernel_spmd(nc, maps, core_ids=list(range(8)))
    out = np.zeros((4, S, D), np.float32)
    for c in range(8):
        b, p = c // 2, c % 2
        o = np.asarray(res.results[c]["out"]).reshape(16, 128, D)
        for j in range(16):
            q = 2 * j + p
            out[b, q * 128:(q + 1) * 128, :] = o[j]
    return out
```

```python
import os
import math
import numpy as np
import concourse.bass as bass
import concourse.mybir as mybir
from concourse.bass_utils import run_bass_kernel_spmd

F32 = mybir.dt.float32
BF16 = mybir.dt.bfloat16
I32 = mybir.dt.int32
AF = mybir.ActivationFunctionType
ALU = mybir.AluOpType

S = 4096
D = 1024
NOWN = 2048
EPS = 1e-6
LU = 2432
NTT = 2304
NTTP = 2432
ENG = ("pe", "act", "dve", "pool", "sp")
N_DSEM = 76
N_DSEM_SW = 32
ARENA_BYTES = 212800

PI = math.pi
C1 = 6.28125
C2 = 2.0 * math.pi - 6.28125


class Buf:
    __slots__ = ("name", "w", "r", "excl")

    def __init__(self, name, excl=False):
        self.name = name
        self.w = None
        self.r = {}
        self.excl = excl


class Rec:
    def __init__(self, esem, dsem):
        self.esem = esem
        self.dsem = dsem
        self.ops = {e: [] for e in ENG}
        self.cnt = {e: 0 for e in ENG}
        self.seen = {e: {} for e in ENG}
        self.dval = [0] * len(dsem)
        self.drr = {}

    def _sem(self, key):
        return self.esem[key[1]] if key[0] == "e" else self.dsem[key[1]]

    def _wait(self, eng, ev):
        if ev is None:
            return
        key, val = ev
        if key == ("e", "pe") and eng == "pe":
            return
        if self.seen[eng].get(key, 0) >= val:
            return
        self.seen[eng][key] = val
        sem = self._sem(key)
        self.ops[eng].append(lambda e, sem=sem, val=val: e.wait_ge(sem, val))

    @staticmethod
    def _split(reads, writes):
        ex = [b for b in reads if b.excl]
        if ex:
            reads = [b for b in reads if not b.excl]
            writes = list(writes) + [b for b in ex if b not in writes]
        return reads, writes

    @staticmethod
    def _deps(reads, writes):
        deps = []
        for b in reads:
            if b.w is not None:
                deps.append(b.w)
        for b in writes:
            if b.w is not None:
                deps.append(b.w)
            deps.extend(b.r.values())
        return deps

    @staticmethod
    def _commit(ev, reads, writes):
        for b in reads:
            b.r[ev[0]] = ev
        for b in writes:
            b.w = ev
            b.r = {}

    def op(self, eng, fn, reads=(), writes=()):
        reads, writes = self._split(reads, writes)
        for ev in self._deps(reads, writes):
            self._wait(eng, ev)
        self.cnt[eng] += 1
        ev = (("e", eng), self.cnt[eng])
        sem = self.esem[eng]
        self.ops[eng].append(lambda e, fn=fn, sem=sem: fn(e).then_inc(sem, 1))
        self._commit(ev, reads, writes)
        return ev

    def mm(self, fns, reads=(), writes=()):
        reads, writes = self._split(reads, writes)
        for ev in self._deps(reads, writes):
            self._wait("pe", ev)
        self.cnt["pe"] += 1
        ev = (("e", "pe"), self.cnt["pe"])
        sem = self.esem["pe"]
        for f in fns[:-1]:
            self.ops["pe"].append(lambda e, f=f: f(e))
        self.ops["pe"].append(lambda e, f=fns[-1], sem=sem: f(e).then_inc(sem, 1))
        self._commit(ev, reads, writes)
        return ev

    def dma(self, eng, out, in_, reads=(), writes=()):
        lo, hi = (0, N_DSEM_SW) if eng == "pool" else (N_DSEM_SW, len(self.dsem))
        i = self.drr.get(eng, lo)
        self.drr[eng] = lo + (i + 1 - lo) % (hi - lo)
        if self.dval[i] > 0:
            self._wait(eng, (("d", i), self.dval[i]))
        for ev in self._deps(reads, writes):
            self._wait(eng, ev)
        self.dval[i] += 16
        ev = (("d", i), self.dval[i])
        sem = self.dsem[i]
        self.ops[eng].append(
            lambda e, out=out, in_=in_, sem=sem: e.dma_start(out=out, in_=in_).then_inc(sem, 16))
        self._commit(ev, reads, writes)
        return ev

    def barrier(self):
        for e in ENG:
            for f in ENG:
                if f != e and self.cnt[f] > 0:
                    self._wait(e, (("e", f), self.cnt[f]))
            for i, v in enumerate(self.dval):
                if v > 0:
                    self._wait(e, (("d", i), v))

    def finish(self):
        for i, v in enumerate(self.dval):
            if v > 0:
                self._wait("sp", (("d", i), v))
        for f in ENG:
            if f != "sp" and self.cnt[f] > 0:
                self._wait("sp", (("e", f), self.cnt[f]))


class Arena:
    def __init__(self, ap, cap):
        self.ap = ap
        self.cap = cap
        self.top = 0
        self.peak = 0

    def alloc(self, nbytes):
        off = (self.top + 63) // 64 * 64
        self.top = off + nbytes
        self.peak = max(self.peak, self.top)
        assert self.top <= self.cap, f"arena overflow {self.top} > {self.cap}"
        return off

    def b16(self, n, parts=128):
        off = self.alloc(2 * n)
        return self.ap[0:parts, off // 2: off // 2 + n]

    def f32(self, n, parts=128, dt=F32):
        off = self.alloc(4 * n)
        return self.ap[0:parts, off // 2: off // 2 + 2 * n].bitcast(dt)


def r3(ap, b):
    return ap.rearrange("p (a b) -> p a b", b=b)


def build_program(dump=None):
    dump = dump or ()
    nc = bass.Bass("TRN2", target_bir_lowering=False)

    def din(name, shape, dt=F32):
        return nc.dram_tensor(name, list(shape), dt, kind="ExternalInput").ap()

    xf = din("xf", [S, D])
    xo = din("xo", [NOWN, D])
    posf = din("posf", [1, S], I32)
    poso = din("poso", [1, NOWN], I32)
    w_in = din("w_in", [D, 2592])
    w_uq = din("w_uq", [768, 768])
    w_ukv = din("w_ukv", [256, 1024])
    w_o = din("w_o", [D, D])
    w_ff1 = din("w_ff1", [D, 4096])
    w_ff2 = din("w_ff2", [4096, D])
    gcols_d = din("gcols", [128, 32])
    relb_d = din("rel_bias", [32, 8])
    ident_d = din("ident", [128, 128])
    onesblk_d = din("onesblk", [128, 128])
    oh_d = din("onehot", [32, LU])
    mm1_d = din("mask_m1", [128, 128])
    m0_d = din("mask_0", [128, 128])
    postm_d = din("pos_tm", [128, 48], I32)
    invf_d = din("invf_row", [128, 16])
    out_d = nc.dram_tensor("out", [NOWN, D], F32, kind="ExternalOutput").ap()
    h1s = nc.dram_tensor("h1s", [NOWN, D], F32, kind="Internal").ap()
    uscr = nc.dram_tensor("uscr", [8, 128, LU], BF16, kind="Internal").ap()
    xnTs = nc.dram_tensor("xnTs", [12, 128, 8 * 512], BF16, kind="Internal").ap()
    cqs = nc.dram_tensor("cqs", [4, 128, 6 * 512], BF16, kind="Internal").ap()
    ckvs = nc.dram_tensor("ckvs", [8, 128, 2 * 512], BF16, kind="Internal").ap()
    krgs = nc.dram_tensor("krgs", [8, 64, 512], F32, kind="Internal").ap()
    sqks = nc.dram_tensor("sqks", [8, 64, 512], BF16, kind="Internal").ap()
    trqs = nc.dram_tensor("trqs", [4, 2, 48, 512], F32, kind="Internal").ap()
    dump_out = {}

    import contextlib
    with contextlib.ExitStack() as es:
        arena_t = es.enter_context(nc.sbuf_tensor("arena", [128, ARENA_BYTES // 2], BF16))
        pairT = [es.enter_context(nc.psum_tensor(f"pp{i}", [128, 1024], F32)) for i in range(2)]
        psb = [es.enter_context(nc.psum_tensor(f"ps{i}", [128, 512], F32)) for i in range(4, 8)]
        esem = {e: es.enter_context(nc.semaphore(f"sem_{e}")) for e in ENG}
        dsem = [es.enter_context(nc.semaphore(f"dsem{i}")) for i in range(N_DSEM)]
        R = Rec(esem, dsem)
        A = Arena(arena_t, ARENA_BYTES)
        pairs = [p[:] for p in pairT]
        ps = [pairs[0][:, 0:512], pairs[0][:, 512:1024], pairs[1][:, 0:512], pairs[1][:, 512:1024]] + [p[:] for p in psb]
        SPB = [Buf("sp0", excl=True), Buf("sp1", excl=True)]
        PSB = [SPB[0], SPB[0], SPB[1], SPB[1]] + [Buf(f"ps{i}", excl=True) for i in range(4, 8)]

        def dump_ap(name, ap, reads, shape, dt=F32):
            if name not in dump:
                return
            t = nc.dram_tensor("dbg_" + name, list(shape), dt, kind="ExternalOutput").ap()
            dump_out[name] = t
            R.dma("sp", t, ap, reads=reads)

        ident = A.b16(128)
        ones_bf = A.b16(128)
        onesblk = A.b16(128)
        ones32 = A.f32(128)
        gcols = A.f32(32)
        mm1 = A.b16(128)
        m0 = A.b16(128)
        small = A.f32(16)
        invf_row = A.f32(16)
        CONST = Buf("const")
        G_MIX, G_FFN, G_CQ, G_CKV, G_QA, G_KA, G_QB, G_KB, INVF = 0, 8, 16, 22, 24, 25, 26, 27, 28

        R.dma("pool", ident, ident_d, writes=[CONST])
        R.dma("pool", onesblk, onesblk_d, writes=[CONST])
        R.dma("pool", mm1, mm1_d, writes=[CONST])
        R.dma("pool", m0, m0_d, writes=[CONST])
        R.dma("sp", gcols, gcols_d, writes=[CONST])
        R.dma("sp", invf_row, invf_d, writes=[CONST])
        R.op("dve", lambda e: e.memset(ones_bf, 1.0), writes=[CONST])
        R.op("dve", lambda e: e.memset(ones32, 0.0), writes=[CONST])
        R.op("dve", lambda e: e.memset(ones32[64:65, :], 1.0), writes=[CONST])

        XSD = [Buf(f"xnTs{i}") for i in range(12)]
        mixA = A.b16(4 * 2048)
        mixA3 = r3(mixA, 2048)
        MIXA = [[Buf(f"mixA{h}_{m}") for m in range(4)] for h in range(8)]
        work_base = A.top

        BO = 4
        BP = (5, 6)
        BT = 7
        BX = 7

        ATT_PER_ROUND = int(os.environ.get("MK_APR", "1"))

        LOCK7 = {"o": None}

        def acq7(name):
            while LOCK7["o"] not in (None, name):
                yield
            LOCK7["o"] = name

        def rel7():
            LOCK7["o"] = None

        def drain(g):
            for _ in g:
                pass

        def chain(a, b):
            if a is not None:
                yield from a
            yield from b

        def pipeline(steps, before_tail=None, side=None):
            from collections import deque
            queue = deque()
            posts = []

            def pop_head():
                _, pf = queue.popleft()
                for g in posts:
                    for _ in range(100000):
                        try:
                            next(g)
                        except StopIteration:
                            break
                    else:
                        raise RuntimeError("post generator cannot make progress")
                del posts[:]
                if pf is not None:
                    posts.append(pf())

            def advance(n):
                for _ in range(n):
                    while queue:
                        try:
                            next(queue[0][0])
                            break
                        except StopIteration:
                            pop_head()
                    if not queue:
                        break
                for g in list(posts):
                    try:
                        next(g)
                    except StopIteration:
                        posts.remove(g)

            drain(steps[0]["s1"]())
            side = list(side or [])
            for i, st in enumerate(steps):
                if st.get("att") is not None:
                    while len(queue) >= 2:
                        try:
                            next(queue[0][0])
                        except StopIteration:
                            pop_head()
                        for g in list(posts):
                            try:
                                next(g)
                            except StopIteration:
                                posts.remove(g)
                    for g in side:
                        drain(g)
                    side = []
                active = [st["s2"]()]
                if i + 1 < len(steps):
                    active.append(steps[i + 1]["s1"]())
                while active:
                    advance(ATT_PER_ROUND)
                    for g in list(active):
                        try:
                            next(g)
                        except StopIteration:
                            active.remove(g)
                    for g in list(side):
                        try:
                            next(g)
                        except StopIteration:
                            side.remove(g)
                if st.get("att") is not None and not os.environ.get("MK_NOATT"):
                    queue.append((st["att"](), st.get("post")))
            if before_tail is not None:
                before_tail()
            while queue or posts:
                advance(1)

        class XPipe:
            def __init__(self, own_bufs=True):
                if own_bufs:
                    self.xbuf = [A.f32(1024) for _ in range(2)]
                    self.XB = [Buf(f"xbuf{i}") for i in range(2)]
                self.xs = [A.b16(1024) for _ in range(2)]
                self.XS = [Buf(f"xs{i}") for i in range(2)]
                self.n = 0
                self.SM = [Buf(f"small{i}") for i in range(2)]

            def front(self, src_rows, sb=None):
                j = self.n % 2
                self.n += 1
                xsj, XSj, SMj = self.xs[j], self.XS[j], self.SM[j]
                if sb is None:
                    xb, XBj = self.xbuf[j], self.XB[j]
                    R.dma("sp", xb, src_rows, writes=[XBj])
                else:
                    xb, XBj = sb
                ss = small[:, 4 * j + 0: 4 * j + 1]
                lnv = small[:, 4 * j + 1: 4 * j + 2]
                rstd = small[:, 4 * j + 2: 4 * j + 3]
                R.op("act", lambda e: e.activation(out=xsj, in_=xb, func=AF.Square, accum_out=ss),
                     reads=[XBj], writes=[XSj, SMj])
                R.op("act", lambda e: e.activation(out=lnv, in_=ss, func=AF.Ln, scale=1.0 / 1024.0, bias=EPS),
                     reads=[SMj], writes=[SMj])
                R.op("act", lambda e: e.activation(out=rstd, in_=lnv, func=AF.Exp, scale=-0.5),
                     reads=[SMj], writes=[SMj])
                R.op("pool", lambda e: e.tensor_scalar(out=xsj, in0=xb, scalar1=rstd, scalar2=1.0, op0=ALU.mult,
                                                       op1=ALU.mult),
                     reads=[XBj, SMj], writes=[XSj])
                return xsj, XSj

            def back_pe(self, xsj, XSj):
                pst = ps[BT].bitcast(BF16)
                R.mm([(lambda e, k=k: e.transpose(pst[:, k * 128:(k + 1) * 128], xsj[:, k * 128:(k + 1) * 128], ident))
                      for k in range(8)], reads=[XSj, CONST], writes=[PSB[BT]])

            def back_evac(self, gofs, dst3, DST):
                pst = ps[BT].bitcast(BF16)
                g3 = gcols[:, gofs:gofs + 8].unsqueeze(2).to_broadcast([128, 8, 128])
                R.op("dve", lambda e: e.tensor_tensor(out=dst3, in0=r3(pst, 128), in1=g3, op=ALU.mult),
                     reads=[PSB[BT], CONST], writes=[DST])

            def chunk(self, srcs, gofs, xnT3, XNT):
                def fr(j):
                    if isinstance(srcs[j], tuple):
                        return self.front(None, sb=srcs[j])
                    return self.front(srcs[j])
                cur = fr(0)
                yield
                for j in range(4):
                    nxt = fr(j + 1) if j + 1 < 4 else None
                    yield
                    yield from acq7("s1")
                    self.back_pe(*cur)
                    yield
                    self.back_evac(gofs, xnT3[:, :, j * 128:(j + 1) * 128], XNT)
                    rel7()
                    yield
                    cur = nxt

        PROJ_CHUNK = int(os.environ.get("MK_PCH", "4"))

        def proj_fm(bank, m_lo, m_hi, W3, c0, ncol, xnT3, n, reads, chunk=None):
            nk = W3.shape[1]
            ch = chunk or PROJ_CHUNK
            o = ps[bank][m_lo:m_hi, 0:n]
            for k0 in range(0, nk, ch):
                R.mm([(lambda e, k=k: e.matmul(o, lhsT=W3[:, k, c0:c0 + ncol], rhs=xnT3[:, k, 0:n],
                                                start=(k == 0), stop=(k == nk - 1))) for k in range(k0, min(nk, k0 + ch))],
                     reads=reads, writes=[PSB[bank]])
                if k0 + ch < nk:
                    yield

        class NormBufs:
            def __init__(self):
                self.sq = [A.b16(512) for _ in range(2)]
                self.SQ = [Buf(f"sq{i}") for i in range(2)]
                self.lnv = A.f32(512)
                self.LNV = Buf("lnv")
                self.rstd = [A.f32(512) for _ in range(2)]
                self.RSTD = [Buf(f"rstd{i}") for i in range(2)]
                self.n = 0

            def next(self):
                j = self.n % 2
                self.n += 1
                return self.sq[j], self.SQ[j], self.rstd[j], self.RSTD[j]

        def ns_square(NB, slot, bank, sq_rows):
            sq, SQ, rstd, RS = slot
            lo, hi = sq_rows
            R.op("act", lambda e: e.activation(out=sq[lo:hi, :], in_=ps[bank][lo:hi, :], func=AF.Square),
                 reads=[PSB[bank]], writes=[SQ])

        def ns_stats_mm(NB, slot, st_rows, ones_lhsT, denom):
            sq, SQ, rstd, RS = slot
            slo, shi = st_rows
            st = ps[BX][slo:shi, :]
            R.mm([lambda e: e.matmul(st, lhsT=ones_lhsT, rhs=sq[slo:shi, :], start=True, stop=True)],
                 reads=[SQ, CONST], writes=[PSB[BX]])

        def ns_stats_act(NB, slot, st_rows, ones_lhsT, denom):
            sq, SQ, rstd, RS = slot
            slo, shi = st_rows
            st = ps[BX][slo:shi, :]
            R.op("act", lambda e: e.activation(out=NB.lnv[slo:shi, :], in_=st, func=AF.Ln, scale=1.0 / denom, bias=EPS),
                 reads=[PSB[BX]], writes=[NB.LNV])
            R.op("act", lambda e: e.activation(out=rstd[slo:shi, :], in_=NB.lnv[slo:shi, :], func=AF.Exp, scale=-0.5),
                 reads=[NB.LNV], writes=[RS])

        def normed_proj_gen(NB, n_items, proj_fn, sq_rows_fn, stats_args_fn, final_fn):
            slots = [None] * n_items
            banks = [BP[i % 2] for i in range(n_items)]

            def head(i):
                yield from proj_fn(i, banks[i])
                slots[i] = NB.next()
                ns_square(NB, slots[i], banks[i], sq_rows_fn(i))
            for i in range(min(2, n_items)):
                yield from head(i)
                yield
            for p0 in range(0, n_items, 2):
                pair = [i for i in (p0, p0 + 1) if i < n_items]
                for i in pair:
                    yield from acq7("s2")
                    ns_stats_mm(NB, slots[i], *stats_args_fn(i))
                    yield
                    ns_stats_act(NB, slots[i], *stats_args_fn(i))
                    rel7()
                    yield
                for i in pair:
                    final_fn(i, banks[i], slots[i][2], slots[i][3])
                    yield
                    if i + 2 < n_items:
                        yield from head(i + 2)
                        yield

        class AttnBufs:
            def __init__(self):
                self.pb = [A.b16(1024) for _ in range(3)]
                self.PB = [Buf(f"pb{i}") for i in range(3)]
                self.osb = [A.f32(512) for _ in range(2)]
                self.OSB = [Buf(f"osb{i}") for i in range(2)]
                for o_, OB_ in zip(self.osb, self.OSB):
                    R.op("pool", lambda e, o_=o_: e.memset(o_, 0.0), writes=[OB_])
                self.u = 0
                self.hn = 0

        def attention(AB, heads, scale):
            groups = []
            for h in heads:
                us = h["units"]
                k = 0
                while k < len(us):
                    if (not os.environ.get("MK_NOPAIR") and k + 1 < len(us) and us[k][4:6] == us[k + 1][4:6]
                            and (h.get("gmask") is None or us[k + 1][7] - us[k][7] == 128)):
                        groups.append((h, [k, k + 1]))
                        k += 2
                    else:
                        groups.append((h, [k]))
                        k += 1
            nG = len(groups)

            def issue_S(g):
                h, idxs = groups[g]
                pr = g % 2
                for slot, k in enumerate(idxs):
                    kT, KB, v, VB, i0, i1, masks, meta = h["units"][k]
                    o = pairs[pr][:, slot * 512 + i0 * 128:slot * 512 + i1 * 128]
                    q = h["qT"][:, i0 * 128:i1 * 128]
                    R.mm([lambda e, o=o, kT=kT, q=q: e.matmul(o, lhsT=kT, rhs=q, start=True, stop=True)],
                         reads=[KB, h["QB"]], writes=[SPB[pr]])

            def issue_PV(g):
                h, idxs = groups[g]
                i0, i1 = h["units"][idxs[0]][4:6]
                pb, PBj = pbs[g]
                nun = len(h["units"])
                for slot, k in enumerate(idxs):
                    v, VB = h["units"][k][2:4]
                    oo = ps[BO][0:65, i0 * 128:i1 * 128]
                    pv = pb[:, slot * 512 + i0 * 128:slot * 512 + i1 * 128]
                    R.mm([lambda e, oo=oo, v=v, pv=pv, k=k, nun=nun: e.matmul(oo, lhsT=v, rhs=pv, start=(k == 0),
                                                                              stop=(k == nun - 1))],
                         reads=[VB, PBj], writes=[PSB[BO]])
                return idxs[-1] == nun - 1

            def fin_a(h):
                oj = AB.hn % 2
                AB.hn += 1
                osb, OSBj = AB.osb[oj], AB.OSB[oj]
                R.op("act", lambda e, osb=osb: e.activation(out=osb[0:65, :], in_=ps[BO][0:65, :], func=AF.Copy),
                     reads=[PSB[BO]], writes=[OSBj])
                R.op("act", lambda e, osb=osb: e.activation(out=osb[64:65, :], in_=osb[64:65, :], func=AF.Ln),
                     reads=[OSBj], writes=[OSBj])
                R.op("act", lambda e, osb=osb: e.activation(out=osb[64:65, :], in_=osb[64:65, :], func=AF.Exp, scale=-1.0),
                     reads=[OSBj], writes=[OSBj])
                return (h, osb, OSBj)

            def fin_b(h, osb, OSBj):
                bc = ps[BX][0:64, :]
                R.mm([lambda e, osb=osb: e.matmul(ps[BX][:, :], lhsT=ones32, rhs=osb, start=True, stop=True)],
                     reads=[OSBj, CONST], writes=[PSB[BX]])
                dst = h["dst"]
                R.op("dve", lambda e, osb=osb, bc=bc, dst=dst: e.tensor_tensor(out=dst, in0=osb[0:64, :], in1=bc, op=ALU.mult),
                     reads=[OSBj, PSB[BX]], writes=[h["DST"]])

            pbs = {}
            pend_a = []
            pend_b = []
            issue_S(0)
            for g in range(nG + 1):
                if g + 1 < nG:
                    issue_S(g + 1)
                if pend_b:
                    while LOCK7["o"] not in (None, "att"):
                        yield
                    for item in pend_b:
                        fin_b(*item)
                    pend_b = []
                if pend_a:
                    pend_b = [fin_a(h_) for h_ in pend_a]
                    pend_a = []
                if g >= 1:
                    if issue_PV(g - 1):
                        pend_a.append(groups[g - 1][0])
                    del pbs[g - 1]
                if g < nG:
                    h, idxs = groups[g]
                    pr = g % 2
                    ns = len(idxs)
                    i0, i1 = h["units"][idxs[0]][4:6]
                    pj = AB.u % 3
                    AB.u += 1
                    pb, PBj = AB.pb[pj], AB.PB[pj]
                    pbs[g] = (pb, PBj)
                    src = r3(pairs[pr], 512)[:, 0:ns, i0 * 128:i1 * 128]
                    pv3 = r3(pb, 512)[:, 0:ns, i0 * 128:i1 * 128]
                    R.op("act", lambda e, src=src, pv3=pv3: e.activation(out=pv3, in_=src, func=AF.Exp, scale=scale),
                         reads=[SPB[pr]], writes=[PBj])
                    if h.get("gmask") is not None:
                        meng, mfn, mreads = h["gmask"]([h["units"][k][7] for k in idxs], i0, i1, pb)
                        R.op(meng, mfn, reads=list(mreads) + [PBj], writes=[PBj])
                    for slot, k in enumerate(idxs):
                        for (meng, mfn, mreads) in h["units"][k][6]:
                            pslot = pb[:, slot * 512:(slot + 1) * 512]
                            R.op(meng, (lambda e, mfn=mfn, pslot=pslot: mfn(e, pslot)), reads=list(mreads) + [PBj],
                                 writes=[PBj])
                yield
            for _ in range(2):
                if pend_b:
                    while LOCK7["o"] not in (None, "att"):
                        yield
                    for item in pend_b:
                        fin_b(*item)
                    pend_b = []
                if pend_a:
                    pend_b = [fin_a(h_) for h_ in pend_a]
                    pend_a = []
                yield

        def pass_A():
            A.top = work_base
            WA = A.b16(8 * 1536)
            WA3 = r3(WA, 1536)
            WAB = Buf("WA")
            kat_off = (A.top + 63) // 64 * 64
            KaT = A.b16(4 * S)
            KaT3 = r3(KaT, S)
            KAB = [Buf(f"KaT{c}") for c in range(8)]
            Va = A.b16(32 * 8 * 65)
            Va4 = Va.rearrange("p (k h d) -> p k h d", h=8, d=65)
            VAB = [Buf(f"Va{k}") for k in range(32)]
            TT = A.b16(8 * NTTP)
            TT3 = r3(TT, NTTP)
            TTB = [Buf(f"TT{h}") for h in range(8)]
            XP = XPipe()
            xnT = [A.b16(8 * 512) for _ in range(2)]
            xnT3 = [r3(t, 512) for t in xnT]
            XNT = [Buf(f"xnT{i}") for i in range(2)]
            QaT = [A.b16(8 * 512) for _ in range(2)]
            QaT3 = [r3(t, 512) for t in QaT]
            QAB = [[Buf(f"QaT{i}_{f}") for f in range(4)] for i in range(2)]

            NB = NormBufs()
            AB = AttnBufs()


            ALLMIX = [b for hb in MIXA for b in hb]
            mixa_off = work_base - 2 * 4 * 2048
            stg = Arena(arena_t, ARENA_BYTES)
            stg.top = mixa_off
            relb = stg.f32(8, parts=32)
            eb = stg.b16(8 * 128, parts=32)
            oh = stg.b16(LU, parts=32)
            ust = stg.b16(LU)
            assert stg.top <= work_base
            R.dma("sp", relb, relb_d, writes=ALLMIX)
            R.dma("pool", oh, oh_d, writes=ALLMIX)
            for g in (1, 2, 0):
                R.dma("pool", WA3[:, :, g * 512:(g + 1) * 512],
                      w_in[:, g * 512:(g + 1) * 512].rearrange("(k p) c -> p k c", p=128), writes=[WAB])
            R.op("pool", lambda e: e.memset(Va4[:, :, :, 64:65], 1.0), writes=VAB)
            for i_ in range(2):
                R.op("pool", lambda e, i_=i_: e.memset(QaT[i_], 0.0), writes=QAB[i_])
            R.op("act", lambda e: e.activation(out=relb, in_=relb, func=AF.Exp), reads=ALLMIX, writes=ALLMIX)
            R.op("dve", lambda e: e.tensor_copy(out=r3(eb, 128), in_=relb.unsqueeze(2).to_broadcast([32, 8, 128])),
                 reads=ALLMIX, writes=ALLMIX)
            USCR = [Buf(f"uscr{h}") for h in range(8)]

            def tt_gen():
                nb = 0
                for h in range(8):
                    for n in range(5):
                        w = min(512, LU - n * 512)
                        bank = nb % 4
                        nb += 1
                        R.mm([lambda e, h=h, n=n, w=w, bank=bank: e.matmul(ps[bank][:, 0:w], lhsT=eb[:, h * 128:(h + 1) * 128],
                                                                           rhs=oh[:, n * 512:n * 512 + w], start=True, stop=True)],
                             reads=ALLMIX, writes=[PSB[bank]])
                        yield
                        if n % 2 == 0:
                            R.op("dve", lambda e, n=n, w=w, bank=bank: e.tensor_copy(out=ust[:, n * 512:n * 512 + w],
                                                                                  in_=ps[bank][:, 0:w]),
                                 reads=[PSB[bank]], writes=ALLMIX)
                        else:
                            R.op("act", lambda e, n=n, w=w, bank=bank: e.activation(out=ust[:, n * 512:n * 512 + w],
                                                                                 in_=ps[bank][:, 0:w], func=AF.Copy),
                                 reads=[PSB[bank]], writes=ALLMIX)
                        yield
                    R.dma("sp", uscr[h], ust, reads=ALLMIX, writes=[USCR[h]])
                    src = bass.AP(uscr.tensor, h * 128 * LU + 127, [[LU - 1, 128], [1, NTT]])
                    R.dma("sp", TT3[:, h, 0:NTT], src, reads=[USCR[h]], writes=[TTB[h]])
                    yield

            def s1_kv(c, xb):
                yield from XP.chunk([xf[(c * 4 + j) * 128:(c * 4 + j + 1) * 128, :] for j in range(4)], G_MIX,
                                    xnT3[xb], XNT[xb])
                R.dma("sp", xnTs[c], xnT[xb], reads=[XNT[xb]], writes=[XSD[c]])

            def s1_q(M, xb):
                yield from XP.chunk([xo[(M * 4 + i) * 128:(M * 4 + i + 1) * 128, :] for i in range(4)], G_MIX,
                                    xnT3[xb], XNT[xb])
                R.dma("sp", xnTs[8 + M], xnT[xb], reads=[XNT[xb]], writes=[XSD[8 + M]])

            def s2_kv(c, xb):
                x3, XN = xnT3[xb], XNT[xb]

                def final(ft, bank, rstd, RS):
                    R.op("dve", lambda e: e.scalar_tensor_tensor(out=KaT3[:, ft, c * 512:(c + 1) * 512], in0=ps[bank][:, :],
                                                                 scalar=gcols[:, G_KA:G_KA + 1], in1=rstd,
                                                                 op0=ALU.mult, op1=ALU.mult),
                         reads=[PSB[bank], RS, CONST], writes=[KAB[c]])
                yield from normed_proj_gen(
                    NB, 4, lambda ft, bank: proj_fm(bank, 0, 128, WA3, 512 + ft * 128, 128, x3, 512, [WAB, XN]),
                    lambda ft: (0, 128), lambda ft: ((0, 128), onesblk, 64.0), final)
                for j in range(4):
                    bank = BP[j % 2]
                    kb = c * 4 + j
                    for k0 in range(0, 8, PROJ_CHUNK):
                        R.mm([(lambda e, k=k, j=j, bank=bank: e.matmul(ps[bank][:, :], lhsT=x3[:, k, j * 128:(j + 1) * 128],
                                                                      rhs=WA3[:, k, 1024:1536], start=(k == 0), stop=(k == 7)))
                              for k in range(k0, k0 + PROJ_CHUNK)], reads=[WAB, XN], writes=[PSB[bank]])
                        yield
                    R.op("dve", lambda e, kb=kb, bank=bank: e.tensor_copy(out=Va4[:, kb, :, 0:64],
                                                                       in_=ps[bank][:, :].rearrange("p (h d) -> p h d", d=64)),
                         reads=[PSB[bank]], writes=[VAB[kb]])
                    yield

            def s2_q(M, xb):
                x3, XN = xnT3[xb], XNT[xb]
                qb = M % 2

                def final(ft, bank, rstd, RS):
                    for hp in range(2):
                        rw = slice(hp * 64, hp * 64 + 64)
                        R.op("dve", lambda e, rw=rw, hp=hp: e.scalar_tensor_tensor(
                            out=QaT3[qb][rw, 2 * ft + hp, :], in0=ps[bank][rw, :], scalar=gcols[rw, G_QA:G_QA + 1],
                            in1=rstd[rw, :], op0=ALU.mult, op1=ALU.mult),
                            reads=[PSB[bank], RS, CONST], writes=[QAB[qb][ft]])
                yield from normed_proj_gen(
                    NB, 4, lambda ft, bank: proj_fm(bank, 0, 128, WA3, ft * 128, 128, x3, 512, [WAB, XN]),
                    lambda ft: (0, 128), lambda ft: ((0, 128), onesblk, 64.0), final)

            def att_A(M):
                qb = M % 2
                heads = []
                for h in range(8):
                    ft, bp = h // 2, (h % 2) * 64
                    units = []
                    kbs = list(range(8 * M + 7, max(0, 8 * M - 16) - 1, -1))
                    first = 8 * M + 1
                    kbs.remove(first)
                    kbs = [first] + kbs
                    for kb in kbs:
                        Dd = 8 * M - kb
                        i0 = max(0, -((1 + Dd) // 2))
                        i1 = min(4, (16 - Dd) // 2 + 1)
                        if i1 <= i0:
                            continue
                        col0 = (Dd + 2 * i0 + 1) * 128
                        ni = i1 - i0
                        units.append((KaT3[:, ft, kb * 128:(kb + 1) * 128], KAB[kb // 4],
                                      Va4[:, kb, h, :], VAB[kb], i0, i1, [], col0))

                    def gmask(metas, i0, i1, pb, h=h):
                        ns, ni = len(metas), i1 - i0
                        t0 = TT3[:, h, metas[0]:metas[0] + 128]
                        ttv = bass.AP(t0.tensor, t0.offset, [list(t0.ap[0]), [128, ns], [256, ni], [1, 128]])
                        p0 = pb[:, i0 * 128:i0 * 128 + 128]
                        pv4 = bass.AP(p0.tensor, p0.offset, [list(p0.ap[0]), [512, ns], [128, ni], [1, 128]])
                        return ("dve", (lambda e: e.tensor_tensor(out=pv4, in0=pv4, in1=ttv, op=ALU.mult)), [TTB[h]])
                    heads.append(dict(units=units, qT=QaT3[qb][:, h, :], QB=QAB[qb][ft], gmask=gmask,
                                      dst=mixA3[bp:bp + 64, ft, M * 512:(M + 1) * 512], DST=MIXA[h][M]))
                return attention(AB, heads, 0.125)

            steps = []
            n = 0
            for c in range(8):
                steps.append(dict(s1=(lambda c=c, n=n: s1_kv(c, n % 2)), s2=(lambda c=c, n=n: s2_kv(c, n % 2))))
                n += 1
                if c % 2 == 1:
                    M = c // 2
                    steps.append(dict(s1=(lambda M=M, n=n: s1_q(M, n % 2)), s2=(lambda M=M, n=n: s2_q(M, n % 2)),
                                      att=(lambda M=M: att_A(M))))
                    n += 1
            pipeline(steps, side=[tt_gen()])
            dump_ap("KaT", KaT, KAB, [128, 4 * S], BF16)
            dump_ap("Va", Va, VAB, [128, 32 * 8 * 65], BF16)
            dump_ap("TT", TT, TTB, [128, 8 * NTTP], BF16)
            dump_ap("mixA", mixA, [b for hb in MIXA for b in hb], [128, 4 * 2048], BF16)
            R.barrier()

        A.top = work_base
        mixB = A.b16(2 * 2048)
        mixB3 = r3(mixB, 2048)
        mixBh = [A.b16(2 * 512) for _ in range(2)]
        mixBh3 = [r3(t, 512) for t in mixBh]
        MIXB = [[Buf(f"mixB{h}_{m}") for m in range(4)] for h in range(8)]
        ident32 = A.f32(128)
        workB_base = A.top
        SCALE_B = 1.0 / math.sqrt(96.0)
        H1S = [Buf(f"h1s{i}") for i in range(16)]
        PHC = {}
        W1B = [Buf(f"W1_{g}") for g in range(8)]
        W2B = [Buf(f"W2_{g}") for g in range(8)]
        CQS = [Buf(f"cqs{i}") for i in range(4)]
        CKVS = [Buf(f"ckvs{i}") for i in range(8)]
        KRGS = [Buf(f"krgs{i}") for i in range(8)]
        SQKS = [Buf(f"sqks{i}") for i in range(8)]
        TRQS = [Buf(f"trqs{i}") for i in range(4)]

        def pass_B(hh):
            A.top = workB_base
            W1_ap = A.b16(8 * 4096)
            PHC["W1"] = W1_ap
            PHC["W2"] = A.b16(32 * 1024)
            A.top = workB_base
            Wc = A.b16(8 * 1152)
            Wc3 = r3(Wc, 1152)
            WCB = Buf("Wc")
            Wuq = A.b16(6 * 512)
            Wuq3 = r3(Wuq, 512)
            Wuq4 = Wuq.rearrange("p (k h c) -> p k h c", k=6, h=4, c=128)
            WUQ = Buf("Wuq")
            Wuk = A.b16(2 * 512)
            Wuk3 = r3(Wuk, 512)
            Wuv = A.b16(2 * 256)
            Wuv3 = r3(Wuv, 256)
            WUK = Buf("Wukv")
            xnT = [A.b16(8 * 512) for _ in range(2)]
            xnT3 = [r3(t, 512) for t in xnT]
            XNT = [Buf(f"xnT{i}") for i in range(2)]
            NB = NormBufs()
            cq = A.b16(6 * 512)
            cq3 = r3(cq, 512)
            CQ = Buf("cq")
            ckv = A.b16(2 * 512)
            ckv3 = r3(ckv, 512)
            CKV = Buf("ckv")
            rstd_c = A.f32(512)
            RSC = Buf("rstd_c")
            ssacc = A.f32(512)
            SSA = Buf("ssacc")
            qg = A.f32(512)
            QG = Buf("qg")
            krg = A.f32(512)
            KRG = Buf("krg")
            rtmp = A.f32(1024)
            rt3 = r3(rtmp, 512)
            RT = Buf("rtmp")
            cosT = [A.f32(512)] * 2
            sinT = [A.f32(512)] * 2
            TRIG = [Buf("trig")] * 2
            NBK = 48
            sin_tm = A.f32(NBK * 16)
            cos_tm = A.f32(NBK * 16)
            TM = Buf("trig_tm")
            assert A.top - workB_base >= 2 * 8 * 4096, "W1 must fit inside the dead-early region"
            PHC["nw2"] = max(0, min(8, (A.top - workB_base - 2 * 8 * 4096) // 8192)) if hh == 1 else 0
            QbT = [A.b16(4 * 512) for _ in range(2)]
            QbT3 = [r3(t, 512) for t in QbT]
            QBB = [[Buf(f"QbT{i}_{h}") for h in range(4)] for i in range(2)]
            if hh == 1:
                Wo = A.b16(8 * 1024)
                Wo3 = r3(Wo, 1024)
                WOB = Buf("Wo")
                h1t = [A.f32(1024) for _ in range(2)]
                H1T = [Buf(f"h1t{i}") for i in range(2)]
            KbT = A.b16(4 * S)
            KbT3 = r3(KbT, S)
            KBB = [Buf(f"KbT{c}") for c in range(8)]
            Vb = A.b16(32 * 4 * 65)
            Vb4 = Vb.rearrange("p (k h d) -> p k h d", h=4, d=65)
            VBB = [Buf(f"Vb{k}") for k in range(32)]
            AB = AttnBufs()

            if hh == 0:
                for g in range(2):
                    R.dma("pool", Wc3[:, :, g * 512:(g + 1) * 512],
                          w_in[:, 1536 + g * 512:1536 + (g + 1) * 512].rearrange("(k p) c -> p k c", p=128), writes=[WCB])
                R.op("pool", lambda e: e.memset(Wc3[:, :, 1024:1152], 0.0), writes=[WCB])
                R.dma("pool", Wc3[:, :, 1024:1040], w_in[:, 2560:2576].rearrange("(k p) c -> p k c", p=128), writes=[WCB])
                R.dma("pool", Wc3[:, :, 1056:1072], w_in[:, 2576:2592].rearrange("(k p) c -> p k c", p=128), writes=[WCB])
            skv = w_ukv[:, hh * 512:(hh + 1) * 512].rearrange("(k p) (h t d) -> p k h t d", p=128, t=2, d=64)
            Wuk4 = Wuk.rearrange("p (k h d) -> p k h d", k=2, h=4, d=128)
            Wuv4 = Wuv.rearrange("p (k h d) -> p k h d", k=2, h=4, d=64)
            R.op("pool", lambda e: e.memset(Wuk, 0.0), writes=[WUK])
            for k2 in range(2):
                R.dma("pool", Wuk4[:, k2, :, 64:128], skv[:, k2, :, 0, :], writes=[WUK])
                R.dma("pool", Wuv4[:, k2, :, :], skv[:, k2, :, 1, :], writes=[WUK])
            def late_weights():
                R.op("pool", lambda e: e.memset(Wuq, 0.0), writes=[WUQ])
                squ = w_uq[:, hh * 384:(hh + 1) * 384].rearrange("(k p) (h c) -> p k h c", p=128, c=96)
                for h4 in range(4):
                    R.dma("pool", Wuq4[:, :, h4, 0:16], squ[:, :, h4, 64:80], writes=[WUQ])
                    yield
                    R.dma("pool", Wuq4[:, :, h4, 32:48], squ[:, :, h4, 80:96], writes=[WUQ])
                    yield
                    R.dma("pool", Wuq4[:, :, h4, 64:128], squ[:, :, h4, 0:64], writes=[WUQ])
                    yield
                if hh == 1:
                    for g in range(2):
                        R.dma("pool", Wo3[:, :, g * 512:(g + 1) * 512],
                              w_o[:, g * 512:(g + 1) * 512].rearrange("(k p) c -> p k c", p=128), writes=[WOB])
                        yield
            R.op("pool", lambda e: e.memset(Vb4[:, :, :, 64:65], 1.0), writes=VBB)
            R.op("dve", lambda e: e.memset(rtmp, 0.0), writes=[RT])
            R.op("dve", lambda e: e.memset(cosT[0], 0.0), writes=[TRIG[0]])
            R.op("dve", lambda e: e.memset(sinT[0], 0.0), writes=[TRIG[0]])
            R.dma("sp", ident32, ident_d, writes=[CONST])

            PREP = [CQ, QG, KRG] + QBB[0] + QBB[1]
            n_el = NBK * 16
            tmp_pool = [cq[:, 0:1536].bitcast(F32), cq[:, 1536:3072].bitcast(F32),
                        QbT[0][:, 0:1536].bitcast(F32), QbT[1][:, 0:1536].bitcast(F32)]
            posi = tmp_pool[0].bitcast(I32)
            ang, nf, rr = tmp_pool[1], tmp_pool[2], tmp_pool[3]
            tt_ = tmp_pool[0]
            pti = qg[:, 0:NBK].bitcast(I32)
            ptf = krg[:, 0:NBK]
            if hh == 0:
                R.dma("sp", pti, postm_d, writes=PREP)
                R.op("dve", lambda e: e.tensor_copy(out=ptf, in_=pti), reads=PREP, writes=PREP)
                R.op("dve", lambda e: e.tensor_tensor(out=r3(ang, 16), in0=ptf.unsqueeze(2).to_broadcast([128, NBK, 16]),
                                                      in1=invf_row.unsqueeze(1).to_broadcast([128, NBK, 16]), op=ALU.mult),
                     reads=PREP + [CONST], writes=PREP)

            def reduce_and_sin(shift, dst):
                W = PREP
                R.op("dve", lambda e: e.tensor_scalar(out=posi, in0=ang, scalar1=1.0 / (2 * PI),
                                                      scalar2=0.5 + shift / (2 * PI), op0=ALU.mult, op1=ALU.add),
                     reads=W, writes=W)
                R.op("dve", lambda e: e.tensor_copy(out=nf, in_=posi), reads=W, writes=W)
                R.op("dve", lambda e: e.scalar_tensor_tensor(out=rr, in0=nf, scalar=-C1, in1=ang,
                                                             op0=ALU.mult, op1=ALU.add), reads=W, writes=W)
                R.op("dve", lambda e: e.scalar_tensor_tensor(out=rr, in0=nf, scalar=-C2, in1=rr,
                                                             op0=ALU.mult, op1=ALU.add), reads=W, writes=W)
                if shift != 0.0:
                    R.op("dve", lambda e: e.tensor_scalar(out=rr, in0=rr, scalar1=shift, scalar2=None,
                                                          op0=ALU.add), reads=W, writes=W)
                R.op("dve", lambda e: e.tensor_single_scalar(out=tt_, in_=rr, scalar=PI, op=ALU.is_gt),
                     reads=W, writes=W)
                R.op("dve", lambda e: e.scalar_tensor_tensor(out=rr, in0=tt_, scalar=-2 * PI, in1=rr,
                                                             op0=ALU.mult, op1=ALU.add), reads=W, writes=W)
                R.op("dve", lambda e: e.tensor_single_scalar(out=tt_, in_=rr, scalar=-PI, op=ALU.is_lt),
                     reads=W, writes=W)
                R.op("dve", lambda e: e.scalar_tensor_tensor(out=rr, in0=tt_, scalar=2 * PI, in1=rr,
                                                             op0=ALU.mult, op1=ALU.add), reads=W, writes=W)
                R.op("dve", lambda e: e.tensor_scalar(out=rr, in0=rr, scalar1=-PI, scalar2=PI,
                                                      op0=ALU.max, op1=ALU.min), reads=W, writes=W)
                R.op("act", lambda e: e.activation(out=dst, in_=rr, func=AF.Sin), reads=W, writes=[TM])
            if hh == 0:
                reduce_and_sin(0.0, sin_tm)
                reduce_and_sin(PI / 2, cos_tm)
            sin3 = r3(sin_tm, 16)
            cos3 = r3(cos_tm, 16)

            def rope_tables(blk0, ti):
                for tbl3, bank, dstT in ((cos3, BT, cosT[ti]), (sin3, BX, sinT[ti])):
                    yield from acq7("s2")
                    R.mm([(lambda e, j=j, tbl3=tbl3, bank=bank: e.transpose(ps[bank][0:16, j * 128:(j + 1) * 128],
                                                                            tbl3[:, blk0 + j, :], ident32))
                          for j in range(4)], reads=[TM, CONST], writes=[PSB[bank]])
                    yield
                    R.op("dve", lambda e, bank=bank, dstT=dstT: e.tensor_copy(out=dstT[0:16, :], in_=ps[bank][0:16, :]),
                         reads=[PSB[bank]], writes=[TRIG[ti]])
                    sgn = -1.0 if tbl3 is sin3 else 1.0
                    R.op("dve", lambda e, bank=bank, dstT=dstT, sgn=sgn: e.tensor_scalar(
                        out=dstT[32:48, :], in0=ps[bank][0:16, :], scalar1=sgn, scalar2=None, op0=ALU.mult),
                        reads=[PSB[bank]], writes=[TRIG[ti]])
                    rel7()
                    yield

            def rope(t, TB, ti):
                a, b, ab = slice(0, 16), slice(32, 48), slice(0, 48)
                cT, sT, TG = cosT[ti], sinT[ti], TRIG[ti]
                R.op("dve", lambda e: e.tensor_tensor(out=rt3[ab, 0, :], in0=t[ab, :], in1=cT[ab, :], op=ALU.mult),
                     reads=[TB, TG], writes=[RT])
                R.op("dve", lambda e: e.tensor_tensor(out=rt3[a, 1, :], in0=t[b, :], in1=sT[b, :], op=ALU.mult),
                     reads=[TB, TG], writes=[RT])
                R.op("dve", lambda e: e.tensor_tensor(out=rt3[b, 1, :], in0=t[a, :], in1=sT[a, :], op=ALU.mult),
                     reads=[TB, TG], writes=[RT])
                R.op("dve", lambda e: e.tensor_tensor(out=t[ab, :], in0=rt3[ab, 0, :], in1=rt3[ab, 1, :], op=ALU.add),
                     reads=[RT], writes=[TB])

            def c_norm(ntile, col0, gofs, st3, STB, x3, XN):
                for t in range(ntile):
                    bank = BP[t % 2]
                    yield from proj_fm(bank, 0, 128, Wc3, col0 + t * 128, 128, x3, 512, [WCB, XN])
                    sq, SQ, _, _ = NB.next()
                    R.op("act", lambda e, sq=sq, bank=bank: e.activation(out=sq, in_=ps[bank][:, :], func=AF.Square),
                         reads=[PSB[bank]], writes=[SQ])
                    R.op("dve", lambda e, t=t, bank=bank: e.tensor_scalar(out=st3[:, t, :], in0=ps[bank][:, :],
                                                                        scalar1=gcols[:, gofs + t:gofs + t + 1],
                                                                        scalar2=None, op0=ALU.mult),
                         reads=[PSB[bank], CONST], writes=[STB])
                    yield
                    yield from acq7("s2")
                    R.mm([lambda e, sq=sq: e.matmul(ps[BX][:, :], lhsT=ones_bf, rhs=sq, start=True, stop=True)],
                         reads=[SQ, CONST], writes=[PSB[BX]])
                    yield
                    if t == 0:
                        R.op("dve", lambda e: e.tensor_copy(out=ssacc, in_=ps[BX][:, :]), reads=[PSB[BX]], writes=[SSA])
                    else:
                        R.op("dve", lambda e: e.tensor_tensor(out=ssacc, in0=ps[BX][:, :], in1=ssacc, op=ALU.add),
                             reads=[PSB[BX], SSA], writes=[SSA])
                    rel7()
                    yield
                R.op("act", lambda e: e.activation(out=NB.lnv, in_=ssacc, func=AF.Ln, scale=1.0 / (ntile * 128.0), bias=EPS),
                     reads=[SSA], writes=[NB.LNV])
                R.op("act", lambda e: e.activation(out=rstd_c, in_=NB.lnv, func=AF.Exp, scale=-0.5),
                     reads=[NB.LNV], writes=[RSC])
                yield
                for t in range(ntile):
                    R.op("pool" if t % 2 else "dve",
                         lambda e, t=t: e.tensor_tensor(out=st3[:, t, :], in0=st3[:, t, :], in1=rstd_c, op=ALU.mult),
                         reads=[RSC, STB], writes=[STB])
                    if t % 2 == 1:
                        yield

            def s1_kv(c, xb):
                if hh == 0:
                    R.dma("sp", xnT[xb], xnTs[c], reads=[XSD[c]], writes=[XNT[xb]])
                yield

            def s1_q(M, xb):
                if hh == 0:
                    R.dma("sp", xnT[xb], xnTs[8 + M], reads=[XSD[8 + M]], writes=[XNT[xb]])
                else:
                    R.dma("sp", cosT[1][0:48, :], trqs[M, 0], reads=[TRQS[M]], writes=[TRIG[1]])
                    R.dma("sp", sinT[1][0:48, :], trqs[M, 1], reads=[TRQS[M]], writes=[TRIG[1]])
                    R.dma("sp", cq, cqs[M], reads=[CQS[M]], writes=[CQ])
                yield

            def s2_kv(c, xb):
                x3, XN = xnT3[xb], XNT[xb]
                cols = slice(c * 512, (c + 1) * 512)
                if hh == 0:
                    yield from c_norm(2, 768, G_CKV, ckv3, CKV, x3, XN)
                    R.dma("sp", ckvs[c], ckv, reads=[CKV], writes=[CKVS[c]])
                    yield from rope_tables(4 * c, 0)
                    bank = BP[0]
                    yield from proj_fm(bank, 0, 128, Wc3, 1024, 128, x3, 512, [WCB, XN])
                    for j in range(2):
                        R.op("act", lambda e, j=j, bank=bank: e.activation(out=NB.sq[j][0:64, :], in_=ps[bank][0:64, :], func=AF.Square),
                             reads=[PSB[bank]], writes=[NB.SQ[j]])
                    R.dma("sp", sqks[c], NB.sq[0][0:64, :], reads=[NB.SQ[0]], writes=[SQKS[c]])
                    R.op("dve", lambda e, bank=bank: e.tensor_scalar(out=krg[0:64, :], in0=ps[bank][0:64, :],
                                                                   scalar1=gcols[0:64, G_KB:G_KB + 1], scalar2=None, op0=ALU.mult),
                         reads=[PSB[bank], CONST], writes=[KRG])
                    yield
                    rope(krg, KRG, 0)
                    R.dma("sp", krgs[c], krg[0:64, :], reads=[KRG], writes=[KRGS[c]])
                    yield
                else:
                    R.dma("sp", ckv, ckvs[c], reads=[CKVS[c]], writes=[CKV])
                    R.dma("sp", krg[0:64, :], krgs[c], reads=[KRGS[c]], writes=[KRG])
                    for j in range(2):
                        R.dma("sp", NB.sq[j][0:64, :], sqks[c], reads=[SQKS[c]], writes=[NB.SQ[j]])
                    yield

                def final(hl, bank, rstd, RS):
                    R.op("dve", lambda e: e.scalar_tensor_tensor(
                        out=KbT3[64:128, hl, cols], in0=ps[bank][64:128, :], scalar=gcols[64:128, G_KB:G_KB + 1],
                        in1=rstd[64:128, :], op0=ALU.mult, op1=ALU.mult),
                        reads=[PSB[bank], RS, CONST], writes=[KBB[c]])
                    R.op("dve" if hh == 1 else "pool",
                         lambda e: e.tensor_tensor(out=KbT3[0:64, hl, cols], in0=krg[0:64, :],
                                                   in1=rstd[0:64, :], op=ALU.mult),
                         reads=[KRG, RS], writes=[KBB[c]])
                yield from normed_proj_gen(
                    NB, 4, lambda hl, bank: proj_fm(bank, 0, 128, Wuk3, hl * 128, 128, ckv3, 512, [WUK, CKV]),
                    lambda hl: (64, 128), lambda hl: ((0, 128), ones_bf, 96.0), final)
                for j in range(4):
                    bank = BP[j % 2]
                    kb = c * 4 + j
                    R.mm([(lambda e, k=k, j=j, bank=bank: e.matmul(ps[bank][:, 0:256], lhsT=ckv3[:, k, j * 128:(j + 1) * 128],
                                                                  rhs=Wuv3[:, k, :], start=(k == 0), stop=(k == 1)))
                          for k in range(2)], reads=[WUK, CKV], writes=[PSB[bank]])
                    R.op("dve", lambda e, kb=kb, bank=bank: e.tensor_copy(out=Vb4[:, kb, :, 0:64],
                                                                       in_=ps[bank][:, 0:256].rearrange("p (h d) -> p h d", d=64)),
                         reads=[PSB[bank]], writes=[VBB[kb]])
                    yield

            def s2_q(M, xb):
                x3, XN = xnT3[xb], XNT[xb]
                qb = M % 2
                if hh == 0:
                    yield from rope_tables(32 + 4 * M, 1)
                    R.dma("sp", trqs[M, 0], cosT[1][0:48, :], reads=[TRIG[1]], writes=[TRQS[M]])
                    R.dma("sp", trqs[M, 1], sinT[1][0:48, :], reads=[TRIG[1]], writes=[TRQS[M]])
                    yield from c_norm(6, 0, G_CQ, cq3, CQ, x3, XN)
                    R.dma("sp", cqs[M], cq, reads=[CQ], writes=[CQS[M]])
                else:
                    yield

                def final(hl, bank, rstd, RS):
                    R.op("act", lambda e: e.activation(out=qg, in_=ps[bank][:, :], func=AF.Copy,
                                                       scale=gcols[:, G_QB:G_QB + 1]),
                         reads=[PSB[bank], CONST], writes=[QG])
                    rope(qg, QG, 1)
                    R.op("dve", lambda e: e.tensor_tensor(out=QbT3[qb][:, hl, :], in0=qg, in1=rstd, op=ALU.mult),
                         reads=[QG, RS], writes=[QBB[qb][hl]])
                yield from normed_proj_gen(
                    NB, 4, lambda hl, bank: proj_fm(bank, 0, 128, Wuq3, hl * 128, 128, cq3, 512, [WUQ, CQ]),
                    lambda hl: (0, 128), lambda hl: ((0, 128), ones_bf, 96.0), final)

            def att_B(M):
                qb = M % 2
                heads = []
                for hl in range(4):
                    hb = 4 * hh + hl
                    ft, bp = hb // 2, (hb % 2) * 64
                    units = []
                    for kb in range(0, 8 * M + 8):
                        Dd = 8 * M - kb
                        i0 = max(0, -((1 + Dd) // 2))
                        masks = []
                        for i in range(i0, 4):
                            dl = Dd + 2 * i
                            if dl in (-1, 0):
                                mt = mm1 if dl == -1 else m0

                                def mfn(e, pslot, mt=mt, i=i):
                                    pv = pslot[:, i * 128:(i + 1) * 128]
                                    return e.tensor_tensor(out=pv, in0=pv, in1=mt, op=ALU.mult)
                                masks.append(("pool", mfn, [CONST]))
                        units.append((KbT3[:, hl, kb * 128:(kb + 1) * 128], KBB[kb // 4],
                                      Vb4[:, kb, hl, :], VBB[kb], i0, 4, masks, 0))
                    dst_ = (mixB3[bp:bp + 64, ft, M * 512:(M + 1) * 512] if hh == 0
                            else mixBh3[M % 2][bp:bp + 64, ft - 2, :])
                    heads.append(dict(units=units, qT=QbT3[qb][:, hl, :], QB=QBB[qb][hl],
                                      dst=dst_, DST=MIXB[hb][M]))
                yield from attention(AB, heads, SCALE_B)

            def wo_B(M):
                if True:
                    mix_reads = [MIXA[h][M] for h in range(8)] + [MIXB[h][M] for h in range(8)]
                    for i in range(4):
                        j = i % 2
                        row0 = (M * 4 + i) * 128
                        R.dma("sp", h1t[j], xo[row0:row0 + 128, :], writes=[H1T[j]])
                        tc_ = slice(M * 512 + i * 128, M * 512 + (i + 1) * 128)
                        for half in range(2):
                            bank = BX
                            yield from acq7("wo")
                            R.mm([(lambda e, f=f, bank=bank, half=half, tc_=tc_, i=i, M=M: e.matmul(
                                ps[bank][:, :], lhsT=(mixA3[:, f, tc_] if f < 4 else mixB3[:, f - 4, tc_] if f < 6
                                                      else mixBh3[M % 2][:, f - 6, i * 128:(i + 1) * 128]),
                                rhs=Wo3[:, f, half * 512:(half + 1) * 512], start=(f == 0), stop=(f == 7)))
                                for f in range(8)], reads=mix_reads + [WOB], writes=[PSB[bank]])
                            yield
                            R.op("dve", lambda e, j=j, bank=bank, half=half: e.tensor_tensor(
                                out=h1t[j][:, half * 512:(half + 1) * 512], in0=ps[bank][:, :],
                                in1=h1t[j][:, half * 512:(half + 1) * 512], op=ALU.add),
                                reads=[PSB[bank], H1T[j]], writes=[H1T[j]])
                            rel7()
                            yield
                        R.dma("sp", h1s[row0:row0 + 128, :], h1t[j], reads=[H1T[j]], writes=[H1S[M * 4 + i]])
                        yield

            steps = []
            n = 0
            for c in range(8):
                steps.append(dict(s1=(lambda c=c, n=n: s1_kv(c, n % 2)), s2=(lambda c=c, n=n: s2_kv(c, n % 2))))
                n += 1
                if c % 2 == 1:
                    M = c // 2
                    steps.append(dict(s1=(lambda M=M, n=n: s1_q(M, n % 2)), s2=(lambda M=M, n=n: s2_q(M, n % 2)),
                                      att=(lambda M=M: att_B(M)), post=((lambda M=M: wo_B(M)) if hh == 1 else None)))
                    n += 1
            def prefetch_w1():
                for f_ in ENG:
                    if f_ != "pool" and R.cnt[f_] > 0:
                        R._wait("pool", (("e", f_), R.cnt[f_]))
                for i_, v_ in enumerate(R.dval):
                    if v_ > 0:
                        R._wait("pool", (("d", i_), v_))
                W13_ = r3(PHC["W1"], 4096)
                for g in range(8):
                    R.dma("pool", W13_[:, :, g * 512:(g + 1) * 512],
                          w_ff1[:, g * 512:(g + 1) * 512].rearrange("(k p) c -> p k c", p=128), writes=[W1B[g]])
                W23_ = r3(PHC["W2"], 1024)
                for g in range(PHC["nw2"]):
                    R.dma("pool", W23_[:, g * 4:(g + 1) * 4, :],
                          w_ff2[g * 512:(g + 1) * 512, :].rearrange("(f p) c -> p f c", p=128), writes=[W2B[g]])
            pipeline(steps, before_tail=(prefetch_w1 if hh == 1 else None), side=[late_weights()])
            if hh == 0:
                dump_ap("KbT", KbT, KBB, [128, 4 * S], BF16)
                dump_ap("Vb", Vb, VBB, [128, 32 * 4 * 65], BF16)
                dump_ap("QbT3", QbT[1], QBB[1], [128, 2048], BF16)
            if hh == 1:
                dump_ap("mixB", mixB, [b for hb in MIXB for b in hb], [128, 2 * 2048], BF16)
            R.barrier()

        def phase_C():
            A.top = work_base - 2 * 4 * 2048
            h1b = [A.f32(1024) for _ in range(4)]
            H1B = [Buf(f"h1b{i}") for i in range(4)]
            outt = [A.f32(1024) for _ in range(2)]
            OUTT = [Buf(f"outt{i}") for i in range(2)]
            assert A.top <= workB_base
            A.top = workB_base
            W1 = A.b16(8 * 4096)
            W13 = r3(W1, 4096)
            W2 = A.b16(32 * 1024)
            W23 = r3(W2, 1024)
            hid = A.b16(32 * 512)
            hid3 = r3(hid, 512)
            HID = Buf("hid")
            XP = XPipe(own_bufs=False)
            hnT = A.b16(8 * 512)
            hnT3 = r3(hnT, 512)
            HNT = Buf("hnT")
            rl = [A.f32(512) for _ in range(2)]
            RL = [Buf(f"rl{i}") for i in range(2)]
            for g in range(PHC.get("nw2", 0), 8):
                R.dma("pool", W23[:, g * 4:(g + 1) * 4, :],
                      w_ff2[g * 512:(g + 1) * 512, :].rearrange("(f p) c -> p f c", p=128), writes=[W2B[g]])
            FB = (0, 1, 2, 3)
            for b_ in range(4):
                PSB[b_] = Buf(f"psc{b_}", excl=True)
            no = 0
            for cc in range(4):
                srcs = []
                for j in range(4):
                    row0 = (cc * 4 + j) * 128
                    R.dma("sp", h1b[j], h1s[row0:row0 + 128, :], reads=[H1S[cc * 4 + j]], writes=[H1B[j]])
                    srcs.append((h1b[j], H1B[j]))
                drain(XP.chunk(srcs, G_FFN, hnT3, HNT))
                for f in range(32):
                    bank = FB[f % 4]
                    drain(proj_fm(bank, 0, 128, W13, f * 128, 128, hnT3, 512, [W1B[f // 4], HNT], chunk=8))
                    jr = f % 2
                    R.op("act", lambda e, jr=jr, bank=bank: e.activation(out=rl[jr], in_=ps[bank][:, :], func=AF.Relu),
                         reads=[PSB[bank]], writes=[RL[jr]])
                    R.op("pool" if f % 2 else "dve",
                         lambda e, jr=jr, f=f: e.tensor_tensor(out=hid3[:, f, :], in0=rl[jr], in1=rl[jr], op=ALU.mult),
                         reads=[RL[jr]], writes=[HID])
                for j in range(4):
                    row0 = (cc * 4 + j) * 128
                    jo = no % 2
                    no += 1
                    for half in range(2):
                        bank = BP[half]
                        R.mm([(lambda e, f=f, j=j, bank=bank, half=half: e.matmul(
                            ps[bank][:, :], lhsT=hid3[:, f, j * 128:(j + 1) * 128],
                            rhs=W23[:, f, half * 512:(half + 1) * 512], start=(f == 0), stop=(f == 31)))
                            for f in range(32)], reads=[HID] + W2B, writes=[PSB[bank]])
                        R.op("dve", lambda e, jo=jo, j=j, bank=bank, half=half: e.tensor_tensor(
                            out=outt[jo][:, half * 512:(half + 1) * 512], in0=ps[bank][:, :],
                            in1=h1b[j][:, half * 512:(half + 1) * 512], op=ALU.add),
                            reads=[PSB[bank], H1B[j]], writes=[OUTT[jo]])
                    R.dma("sp", out_d[row0:row0 + 128, :], outt[jo], reads=[OUTT[jo]])

        stop_after = os.environ.get("MK_STOP", "")
        pass_A()
        if stop_after != "A":
            pass_B(0)
            if stop_after != "B1":
                pass_B(1)
                if stop_after != "B2":
                    phase_C()

        R.finish()
        with nc.Block() as block:
            @block.tensor
            def _(e):
                for f in R.ops["pe"]:
                    f(e)

            @block.scalar
            def _(e):
                for f in R.ops["act"]:
                    f(e)

            @block.vector
            def _(e):
                for f in R.ops["dve"]:
                    f(e)

            @block.gpsimd
            def _(e):
                for f in R.ops["pool"]:
                    f(e)

            @block.sync
            def _(e):
                for f in R.ops["sp"]:
                    f(e)
    print("arena peak bytes", A.peak, "instr counts", {e: len(R.ops[e]) for e in ENG}, flush=True)
    return nc, dump_out


def t5_bucket(dist):
    dist = np.asarray(dist, dtype=np.int64)
    max_exact = 16
    safe = np.maximum(dist, 1).astype(np.float32)
    large = max_exact + (np.log(safe / max_exact) / math.log(2048 / max_exact) * (32 - max_exact)).astype(np.int64)
    large = np.minimum(large, 31)
    return np.where(dist < max_exact, dist, large).astype(np.int32)


def static_consts(p):
    x = np.arange(LU)
    o = x - 255 + 128 * p
    valid = (o >= 0) & (o <= 2048)
    oc = np.clip(o, 0, 2048)
    mult = ((oc <= 128).astype(np.float32) + ((oc % 4 == 0) & (oc <= 512)).astype(np.float32)
            + ((oc % 16 == 0) & (oc <= 2048)).astype(np.float32))
    bucket = t5_bucket(oc)
    oh = np.zeros((32, LU), np.float32)
    oh[bucket, x] = mult * valid
    k = np.arange(128)[:, None]
    q = np.arange(128)[None, :]
    tri = (q >= k).astype(np.float32)
    if p == 0:
        mm1, m0 = np.zeros((128, 128), np.float32), tri
    else:
        mm1, m0 = tri, np.ones((128, 128), np.float32)
    return oh, mm1, m0


def pad_b(g):
    o = np.zeros(128, np.float32)
    o[0:16] = g[64:80]
    o[32:48] = g[80:96]
    o[64:128] = g[0:64]
    return o


def make_in_maps(inp):
    f32 = np.float32
    x = np.ascontiguousarray(inp["x"], dtype=f32)
    pos = np.ascontiguousarray(inp["positions"]).astype(np.int32)
    gcols = np.zeros((128, 32), f32)
    gcols[:, 0:8] = np.asarray(inp["norm_mix_g"], f32).reshape(8, 128).T
    gcols[:, 8:16] = np.asarray(inp["norm_ffn_g"], f32).reshape(8, 128).T
    gcols[:, 16:22] = np.asarray(inp["cq_norm_g"], f32).reshape(6, 128).T
    gcols[:, 22:24] = np.asarray(inp["ckv_norm_g"], f32).reshape(2, 128).T
    gcols[:, 24] = np.tile(np.asarray(inp["qnorm_a_g"], f32), 2)
    gcols[:, 25] = np.tile(np.asarray(inp["knorm_a_g"], f32), 2)
    gcols[:, 26] = pad_b(np.asarray(inp["qnorm_b_g"], f32))
    gcols[:, 27] = pad_b(np.asarray(inp["knorm_b_g"], f32))
    inv_freq = (1.0 / (10000.0 ** (np.arange(0, 32, 2, dtype=f32) / f32(32)))).astype(f32)
    gcols[0:16, 28] = inv_freq
    gcols[32:48, 28] = inv_freq
    ident = np.eye(128, dtype=f32)
    onesblk = np.zeros((128, 128), f32)
    onesblk[0:64, 0:64] = 1
    onesblk[64:128, 64:128] = 1
    shared = dict(
        w_in=np.ascontiguousarray(inp["w_in"], f32), w_uq=np.ascontiguousarray(inp["w_uq"], f32),
        w_ukv=np.ascontiguousarray(inp["w_ukv"], f32), w_o=np.ascontiguousarray(inp["w_o"], f32),
        w_ff1=np.ascontiguousarray(inp["w_ff1"], f32), w_ff2=np.ascontiguousarray(inp["w_ff2"], f32),
        gcols=gcols, rel_bias=np.ascontiguousarray(inp["rel_bias"], f32), ident=ident, onesblk=onesblk)
    maps = []
    for c in range(8):
        b, p = c // 2, c % 2
        own_blocks = [2 * j + p for j in range(16)]
        rows = np.concatenate([np.arange(q * 128, (q + 1) * 128) for q in own_blocks])
        oh, mm1, m0 = static_consts(p)
        m = dict(shared)
        pos_tm = np.concatenate([pos[b].reshape(32, 128).T, pos[b][rows].reshape(16, 128).T], axis=1)
        m.update(xf=x[b], xo=np.ascontiguousarray(x[b][rows]), posf=pos[b][None, :],
                 poso=np.ascontiguousarray(pos[b][rows])[None, :], onehot=oh, mask_m1=mm1, mask_0=m0,
                 pos_tm=np.ascontiguousarray(pos_tm.astype(np.int32)),
                 invf_row=np.ascontiguousarray(np.tile(inv_freq[None, :], (128, 1))))
        maps.append(m)
    return maps


_CACHE = {}


def kernel(**inputs):
    if "nc" not in _CACHE:
        _CACHE["nc"] = build_program()[0]
    nc = _CACHE["nc"]
    maps = make_in_maps(inputs)
    res = run_bass_kernel_spmd(nc, maps, core_ids=list(range(8)))
    out = np.zeros((4, S, D), np.float32)
    for c in range(8):
        b, p = c // 2, c % 2
        o = np.asarray(res.results[c]["out"]).reshape(16, 128, D)
        for j in range(16):
            q = 2 * j + p
            out[b, q * 128:(q + 1) * 128, :] = o[j]
    return out
```
